# Optimizing a Trainium2 kernel written in Bass

```python
import math
import jax, jax.numpy as jnp
from jax import lax
import numpy as np

D_MODEL = 1024
BATCH = 1
SEQ = 16384
DEPTH = 2

GRID_W = 64
CTX_LEN = 256
N_EVEN = (DEPTH + 1) // 2
N_ODD = DEPTH // 2
S5_WIDTH = D_MODEL // 2
S5_H = 16
S5_GROUPS = S5_WIDTH // S5_H
S5_P = 64
CONV_WIDTH = D_MODEL - S5_WIDTH
CONV_K = 31
IN_EVEN = S5_WIDTH + 2 * CONV_WIDTH
POOL_WINDOWS = (2, 4, 8, 16)
POOL_GROUPS = len(POOL_WINDOWS)
POOL_CH = D_MODEL // POOL_GROUPS
D_FF = 4 * D_MODEL
N_MOD = 6
EPS = 1e-6
DT_MIN = 1e-3
DT_MAX = 1e-1
LAMBDA_RE_MAX = -1e-4

kernel_name = 'hybrid_s5_conformer_pool_dit'


def rms_norm(x, g):
    xf = x.astype(jnp.float32)
    y = xf * lax.rsqrt(jnp.mean(xf * xf, axis=-1, keepdims=True) + EPS)
    return (y * g.astype(jnp.float32)).astype(x.dtype)


def layer_norm(x, g, b):
    xf = x.astype(jnp.float32)
    xc = xf - jnp.mean(xf, axis=-1, keepdims=True)
    var = jnp.mean(xc * xc, axis=-1, keepdims=True)
    y = xc * lax.rsqrt(var + EPS) * g.astype(jnp.float32) + b.astype(jnp.float32)
    return y.astype(x.dtype)


def modulate(h, shift, scale):
    return h * (1 + scale[:, None]) + shift[:, None]


def ada_mod(cvec, w, b, dtype):
    m = jax.nn.silu(cvec.astype(jnp.float32)) @ w.astype(jnp.float32) + b.astype(jnp.float32)
    return jnp.split(m.astype(dtype), N_MOD, axis=-1)


def grid_pos_embed(rows, dtype):
    row = jnp.repeat(jnp.arange(rows, dtype=jnp.float32), GRID_W)
    col = jnp.tile(jnp.arange(GRID_W, dtype=jnp.float32), rows)
    quarter = D_MODEL // 4
    omega = 1.0 / (10000.0 ** (jnp.arange(quarter, dtype=jnp.float32) / quarter))
    def emb(pos):
        ang = pos[:, None] * omega[None, :]
        return jnp.concatenate([jnp.sin(ang), jnp.cos(ang)], axis=-1)
    return jnp.concatenate([emb(row), emb(col)], axis=-1).astype(dtype)


def s5_discretize(lam_re, lam_im, log_step, b_re, b_im, c_re, c_im):
    lam = lax.complex(jnp.minimum(lam_re.astype(jnp.float32), LAMBDA_RE_MAX), lam_im.astype(jnp.float32))
    dt = jnp.exp(log_step.astype(jnp.float32))[:, None]
    lam_bar = jnp.exp(lam * dt)
    b = lax.complex(b_re.astype(jnp.float32), b_im.astype(jnp.float32))
    b_bar = ((lam_bar - 1.0) / lam)[..., None] * b
    cm = lax.complex(c_re.astype(jnp.float32), c_im.astype(jnp.float32))
    return lam_bar, b_bar, cm


def _ssm_combine(left, right):
    a_l, b_l = left
    a_r, b_r = right
    return a_r * a_l, a_r * b_l + b_r


def s5_scan(ug, lam_bar, b_bar, cm, s0):
    bu = jnp.einsum('blgh,gph->blgp', ug.astype(jnp.complex64), b_bar)
    bu = bu.at[:, 0].add(lam_bar[None] * s0)
    a = jnp.broadcast_to(lam_bar, bu.shape)
    _, states = lax.associative_scan(_ssm_combine, (a, bu), axis=1)
    y = jnp.einsum('blgp,ghp->blgh', states, cm).real
    return y, states[:, -1]


def even_mixer(h, ep, s0_f, s0_b):
    (w_in, w_out, lam_re, lam_im, log_step, b_re, b_im, c_re, c_im,
     d_skip, w_glu, conv_w, conv_b, ln_g, ln_b) = ep
    bsz, length, _ = h.shape
    z = h @ w_in
    u = z[..., :S5_WIDTH]
    v = z[..., S5_WIDTH:S5_WIDTH + CONV_WIDTH]
    gt = z[..., S5_WIDTH + CONV_WIDTH:]
    uf = u.astype(jnp.float32)
    ug = uf.reshape(bsz, length, S5_GROUPS, S5_H)
    lb_f, bb_f, cm_f = s5_discretize(lam_re[0], lam_im[0], log_step[0], b_re[0], b_im[0], c_re[0], c_im[0])
    lb_b, bb_b, cm_b = s5_discretize(lam_re[1], lam_im[1], log_step[1], b_re[1], b_im[1], c_re[1], c_im[1])
    y_f, s_f = s5_scan(ug, lb_f, bb_f, cm_f, s0_f)
    y_b, s_b = s5_scan(jnp.flip(ug, axis=1), lb_b, bb_b, cm_b, s0_b)
    ya = (y_f + jnp.flip(y_b, axis=1)).reshape(bsz, length, S5_WIDTH) + d_skip.astype(jnp.float32) * uf
    ya = jax.nn.gelu(ya)
    ya = ya * jax.nn.sigmoid(ya @ w_glu.astype(jnp.float32))
    vb = v * jax.nn.sigmoid(gt)
    vb = lax.conv_general_dilated(vb, conv_w.astype(vb.dtype)[:, None, :], window_strides=(1,),
                                  padding=[(CONV_K // 2, CONV_K // 2)],
                                  dimension_numbers=('NWC', 'WIO', 'NWC'),
                                  feature_group_count=CONV_WIDTH) + conv_b
    vb = jax.nn.silu(layer_norm(vb, ln_g, ln_b))
    y = jnp.concatenate([ya.astype(h.dtype), vb], axis=-1) @ w_out
    return y, s_f, s_b


def centred_mean_minus_self(xg, win):
    bsz, length, ch = xg.shape
    left = win // 2
    right = win - 1 - left
    cs = jnp.concatenate([jnp.zeros((bsz, 1, ch), xg.dtype), jnp.cumsum(xg, axis=1)], axis=1)
    t = jnp.arange(length)
    lo = jnp.maximum(t - left, 0)
    hi = jnp.minimum(t + right, length - 1)
    s = jnp.take(cs, hi + 1, axis=1) - jnp.take(cs, lo, axis=1)
    cnt = (hi - lo + 1).astype(jnp.float32)[None, :, None]
    return s / cnt - xg


def pool_mixer(h, pool_w, pool_b, pool_scale):
    hf = h.astype(jnp.float32)
    outs = []
    for gi, win in enumerate(POOL_WINDOWS):
        hg = hf[..., gi * POOL_CH:(gi + 1) * POOL_CH]
        pg = centred_mean_minus_self(hg, win)
        outs.append(pg @ pool_w[gi].astype(jnp.float32) + pool_b[gi].astype(jnp.float32))
    y = jnp.concatenate(outs, axis=-1) * pool_scale.astype(jnp.float32)
    return y.astype(h.dtype)


def sq_relu_mlp(h, w1, w2):
    return jnp.square(jax.nn.relu(h @ w1)) @ w2


def setup_inputs(seed: int = 0) -> dict:
    key = jax.random.key(seed)
    ks = jax.random.split(key, 32)
    f32 = jnp.float32
    def nrm(k, shape, s):
        return jax.random.normal(k, shape, f32) * s
    n_idx = jnp.arange(S5_P, dtype=f32)
    return {
        'x': nrm(ks[0], (BATCH, SEQ, D_MODEL), 1.0),
        'c': nrm(ks[1], (BATCH, D_MODEL), 1.0),
        'ctx': nrm(ks[2], (BATCH, CTX_LEN, D_MODEL), 1.0),
        'c_ctx': nrm(ks[3], (D_MODEL,), 1.0),
        'w_ada': nrm(ks[4], (DEPTH, D_MODEL, N_MOD * D_MODEL), 0.5 * D_MODEL ** -0.5),
        'b_ada': nrm(ks[5], (DEPTH, N_MOD * D_MODEL), 0.02),
        'norm_mix_g': 1.0 + nrm(ks[6], (DEPTH, D_MODEL), 0.02),
        'norm_mlp_g': 1.0 + nrm(ks[7], (DEPTH, D_MODEL), 0.02),
        'w_in': nrm(ks[8], (N_EVEN, D_MODEL, IN_EVEN), D_MODEL ** -0.5),
        'w_out': nrm(ks[9], (N_EVEN, D_MODEL, D_MODEL), D_MODEL ** -0.5),
        's5_lam_re': -0.5 + nrm(ks[10], (N_EVEN, 2, S5_GROUPS, S5_P), 0.01),
        's5_lam_im': math.pi * n_idx + nrm(ks[11], (N_EVEN, 2, S5_GROUPS, S5_P), 0.01),
        's5_log_step': jax.random.uniform(ks[12], (N_EVEN, 2, S5_GROUPS), f32,
                                          minval=math.log(DT_MIN), maxval=math.log(DT_MAX)),
        's5_b_re': nrm(ks[13], (N_EVEN, 2, S5_GROUPS, S5_P, S5_H), (2 * S5_H) ** -0.5),
        's5_b_im': nrm(ks[14], (N_EVEN, 2, S5_GROUPS, S5_P, S5_H), (2 * S5_H) ** -0.5),
        's5_c_re': nrm(ks[15], (N_EVEN, 2, S5_GROUPS, S5_H, S5_P), S5_P ** -0.5),
        's5_c_im': nrm(ks[16], (N_EVEN, 2, S5_GROUPS, S5_H, S5_P), S5_P ** -0.5),
        's5_d': nrm(ks[17], (N_EVEN, S5_WIDTH), 0.5),
        's5_w_glu': nrm(ks[18], (N_EVEN, S5_WIDTH, S5_WIDTH), S5_WIDTH ** -0.5),
        'conv_w': nrm(ks[19], (N_EVEN, CONV_K, CONV_WIDTH), CONV_K ** -0.5),
        'conv_b': nrm(ks[20], (N_EVEN, CONV_WIDTH), 0.02),
        'conv_ln_g': 1.0 + nrm(ks[21], (N_EVEN, CONV_WIDTH), 0.02),
        'conv_ln_b': nrm(ks[22], (N_EVEN, CONV_WIDTH), 0.02),
        'pool_w': nrm(ks[23], (N_ODD, POOL_GROUPS, POOL_CH, POOL_CH), POOL_CH ** -0.5),
        'pool_b': nrm(ks[24], (N_ODD, POOL_GROUPS, POOL_CH), 0.02),
        'pool_scale': 1.0 + nrm(ks[25], (N_ODD, D_MODEL), 0.1),
        'mlp_w1': nrm(ks[26], (DEPTH, D_MODEL, D_FF), D_MODEL ** -0.5),
        'mlp_w2': nrm(ks[27], (DEPTH, D_FF, D_MODEL), D_FF ** -0.5),
        'final_g': 1.0 + nrm(ks[28], (D_MODEL,), 0.02),
    }


def reference(x, c, ctx, c_ctx, w_ada, b_ada, norm_mix_g, norm_mlp_g, w_in, w_out,
              s5_lam_re, s5_lam_im, s5_log_step, s5_b_re, s5_b_im, s5_c_re, s5_c_im,
              s5_d, s5_w_glu, conv_w, conv_b, conv_ln_g, conv_ln_b,
              pool_w, pool_b, pool_scale, mlp_w1, mlp_w2, final_g):
    bsz, length, _ = x.shape
    ROWS = length // GRID_W
    h = x + grid_pos_embed(ROWS, x.dtype)[None]
    hc = ctx
    for i in range(DEPTH):
        j = i // 2
        ctx_next = i < DEPTH - 1
        sh1, sc1, g1, sh2, sc2, g2 = ada_mod(c, w_ada[i], b_ada[i], h.dtype)
        n_x = modulate(rms_norm(h, norm_mix_g[i]), sh1, sc1)
        if i % 2 == 0 or ctx_next:
            csh1, csc1, cg1, csh2, csc2, cg2 = ada_mod(c_ctx[None], w_ada[i], b_ada[i], hc.dtype)
            n_c = modulate(rms_norm(hc, norm_mix_g[i]), csh1, csc1)
        if i % 2 == 0:
            ep = (w_in[j], w_out[j], s5_lam_re[j], s5_lam_im[j], s5_log_step[j],
                  s5_b_re[j], s5_b_im[j], s5_c_re[j], s5_c_im[j], s5_d[j], s5_w_glu[j],
                  conv_w[j], conv_b[j], conv_ln_g[j], conv_ln_b[j])
            zeros = jnp.zeros((hc.shape[0], S5_GROUPS, S5_P), jnp.complex64)
            y_c, s_f, s_b = even_mixer(n_c, ep, zeros, zeros)
            y_x, _, _ = even_mixer(n_x, ep, s_f, s_b)
        else:
            y_x = pool_mixer(n_x, pool_w[j], pool_b[j], pool_scale[j])
            if ctx_next:
                y_c = pool_mixer(n_c, pool_w[j], pool_b[j], pool_scale[j])
        h = h + g1[:, None] * y_x
        h = h + g2[:, None] * sq_relu_mlp(modulate(rms_norm(h, norm_mlp_g[i]), sh2, sc2), mlp_w1[i], mlp_w2[i])
        if ctx_next:
            hc = hc + cg1[:, None] * y_c
            hc = hc + cg2[:, None] * sq_relu_mlp(modulate(rms_norm(hc, norm_mlp_g[i]), csh2, csc2), mlp_w1[i], mlp_w2[i])
    return rms_norm(h, final_g)
```

```python
import math
import numpy as np
import concourse.bass as bass
import concourse.mybir as mybir
from concourse.bass_utils import run_bass_kernel_spmd

F32 = mybir.dt.float32
BF16 = mybir.dt.bfloat16
I32 = mybir.dt.int32
AF = mybir.ActivationFunctionType
ALU = mybir.AluOpType
AX = mybir.AxisListType

NC = 8
D = 1024
L = 16384
TPC = L // NC
CW = 512
NCHUNK = TPC // CW
CTX = 256
EPS = 1e-6
PI = math.pi
TWO_PI = 2 * math.pi
C1 = 6.28125
C2 = 2 * math.pi - 6.28125


class _Op:
    __slots__ = ("eng", "fn", "deps", "signal", "idx", "dma", "dsem", "dcount", "n")


class Sched:
    N_DMA_SEMS = 48

    def __init__(self, nc):
        self.nc = nc
        self.ops = []
        self.last_w = {}
        self.readers = {}
        self.E = {"pe": nc.tensor, "dve": nc.vector, "act": nc.scalar,
                  "pool": nc.gpsimd, "sp": nc.sync}

    def add(self, eng, fn, reads=(), writes=(), dma=False):
        op = _Op()
        op.eng = eng
        op.fn = fn
        op.dma = dma
        op.signal = False
        op.idx = None
        op.n = len(self.ops)
        deps = {}
        for r in reads:
            w = self.last_w.get(r)
            if w is not None:
                deps[w.n] = w
        for r in writes:
            w = self.last_w.get(r)
            if w is not None:
                deps[w.n] = w
            rd = self.readers.get(r)
            if rd:
                for lst in rd.values():
                    for o in lst:
                        deps[o.n] = o
        for r in reads:
            rd = self.readers.setdefault(r, {})
            if dma:
                rd.setdefault("dma", []).append(op)
            else:
                rd[eng] = [op]
        for r in writes:
            self.last_w[r] = op
            self.readers[r] = {}
        dl = []
        for d in deps.values():
            if d is op:
                continue
            if (not d.dma) and (not dma) and d.eng == eng and eng == "pe":
                continue
            d.signal = True
            dl.append(d)
        op.deps = dl
        self.ops.append(op)
        return op

    def emit(self):
        nc = self.nc
        esem = {e: nc.alloc_semaphore("sem_" + e) for e in self.E}
        ecount = {e: 0 for e in self.E}
        dsems = [nc.alloc_semaphore("dsem%d" % i) for i in range(self.N_DMA_SEMS)]
        dcum = [0] * self.N_DMA_SEMS
        dnext = 0
        waited = {e: {} for e in self.E}

        def do_wait(eng, key, sem, val):
            w = waited[eng]
            if w.get(key, 0) >= val:
                return
            w[key] = val
            self.E[eng].wait_ge(sem, val)

        for op in self.ops:
            eng = op.eng
            for d in op.deps:
                if d.dma:
                    do_wait(eng, ("d", d.dsem), dsems[d.dsem], d.dcount)
                else:
                    do_wait(eng, ("e", d.eng), esem[d.eng], d.idx)
            if op.dma:
                s = dnext
                dnext = (dnext + 1) % self.N_DMA_SEMS
                if dcum[s] > 0:
                    do_wait(eng, ("d", s), dsems[s], dcum[s])
                ins = op.fn()
                ins.then_inc(dsems[s], 16)
                dcum[s] += 16
                op.dsem = s
                op.dcount = dcum[s]
            else:
                ins = op.fn()
                if op.signal:
                    ecount[eng] += 1
                    op.idx = ecount[eng]
                    ins.then_inc(esem[eng], 1)
        for s, c in zip(dsems, dcum):
            if c:
                nc.sync.wait_ge(s, c)


class B:
    def __init__(self):
        self.nc = bass.Bass("TRN2", target_bir_lowering=False)
        self.S = Sched(self.nc)
        self.npsum = 0
        self.ins = {}
        self.outs = {}

    def din(self, name, shape, dt=F32):
        ap = self.nc.dram_tensor(name, list(shape), dt, kind="ExternalInput").ap()
        self.ins[name] = ap
        return ap

    def dout(self, name, shape, dt=F32):
        ap = self.nc.dram_tensor(name, list(shape), dt, kind="ExternalOutput").ap()
        self.outs[name] = ap
        return ap

    def sb(self, name, shape, dt=F32):
        return self.nc.alloc_sbuf_tensor(name, list(shape), dt)

    def psum(self, name, shape=(128, 512), dt=F32):
        return self.nc.alloc_psum_tensor(name, list(shape), dt)

    def dma(self, out, in_, r=(), w=(), **kw):
        nc = self.nc
        self.S.add("sp", lambda: nc.sync.dma_start(out=out, in_=in_, **kw), r, w, dma=True)

    def act(self, out, in_, func, r, w, bias=None, scale=None):
        nc = self.nc
        kw = {}
        if bias is not None:
            kw["bias"] = bias
        if scale is not None:
            kw["scale"] = scale
        self.S.add("act", lambda: nc.scalar.activation(out=out, in_=in_, func=func, **kw), r, w)

    def tt(self, eng, out, in0, in1, op, r, w):
        e = self.S.E[eng]
        self.S.add(eng, lambda: e.tensor_tensor(out=out, in0=in0, in1=in1, op=op), r, w)

    def ts(self, eng, out, in0, s1, op0, r, w, s2=None, op1=None):
        e = self.S.E[eng]
        if op1 is None:
            self.S.add(eng, lambda: e.tensor_scalar(out=out, in0=in0, scalar1=s1, scalar2=None, op0=op0), r, w)
        else:
            self.S.add(eng, lambda: e.tensor_scalar(out=out, in0=in0, scalar1=s1, scalar2=s2, op0=op0, op1=op1), r, w)

    def stt(self, eng, out, in0, scalar, in1, op0, op1, r, w):
        e = self.S.E[eng]
        self.S.add(eng, lambda: e.scalar_tensor_tensor(out=out, in0=in0, scalar=scalar, in1=in1, op0=op0, op1=op1), r, w)

    def copy(self, eng, out, in_, r, w):
        if eng == "act":
            nc = self.nc
            self.S.add("act", lambda: nc.scalar.copy(out=out, in_=in_), r, w)
        else:
            e = self.S.E[eng]
            self.S.add(eng, lambda: e.tensor_copy(out=out, in_=in_), r, w)

    def memset(self, eng, ap, val, w):
        e = self.S.E[eng]
        self.S.add(eng, lambda: e.memset(ap, val), (), w)

    def recip(self, out, in_, r, w):
        nc = self.nc
        self.S.add("dve", lambda: nc.vector.reciprocal(out=out, in_=in_), r, w)

    def mm(self, out, lhsT, rhs, start, stop, r, w):
        nc = self.nc
        self.S.add("pe", lambda: nc.tensor.matmul(out, lhsT=lhsT, rhs=rhs, start=start, stop=stop), r, w)

    def transpose(self, out, in_, ident, r, w):
        nc = self.nc
        self.S.add("pe", lambda: nc.tensor.transpose(out=out, in_=in_, identity=ident), r, w)

    def scan(self, out, d0, d1, init, r, w):
        nc = self.nc
        self.S.add("dve", lambda: nc.vector.tensor_tensor_scan(out=out, data0=d0, data1=d1, initial=init,
                                                             op0=ALU.mult, op1=ALU.add), r, w)

    def iota(self, out, pattern, base, cm, w):
        nc = self.nc
        self.S.add("pool", lambda: nc.gpsimd.iota(out, pattern=pattern, base=base, channel_multiplier=cm), (), w)

    def finish(self):
        self.S.emit()
        return self.nc

    def sin_of(self, out, ang, shape, tmp, r, w):
        ki, kf, ra, rb = tmp["ki"], tmp["kf"], tmp["ra"], tmp["rb"]
        tk = tmp["key"]
        self.ts("dve", ki, ang, 1.0 / TWO_PI, ALU.mult, r, [tk + "ki"])
        self.copy("dve", kf, ki, [tk + "ki"], [tk + "kf"])
        self.stt("dve", ra, kf, -C1, ang, ALU.mult, ALU.add, r + [tk + "kf"], [tk + "ra"])
        self.stt("dve", rb, kf, -C2, ra, ALU.mult, ALU.add, [tk + "kf", tk + "ra"], [tk + "rb"])
        self.ts("dve", kf, rb, PI, ALU.is_gt, [tk + "rb"], [tk + "kf"], s2=-TWO_PI, op1=ALU.mult)
        self.tt("dve", ra, rb, kf, ALU.add, [tk + "rb", tk + "kf"], [tk + "ra"])
        self.ts("dve", kf, ra, -PI, ALU.is_lt, [tk + "ra"], [tk + "kf"], s2=TWO_PI, op1=ALU.mult)
        self.tt("dve", rb, ra, kf, ALU.add, [tk + "ra", tk + "kf"], [tk + "rb"])
        self.ts("dve", ra, rb, -PI, ALU.max, [tk + "rb"], [tk + "ra"], s2=PI, op1=ALU.min)
        self.act(out, ra, AF.Sin, [tk + "ra"], w)


def run(b, in_maps):
    b.finish()
    res = run_bass_kernel_spmd(b.nc, in_maps, core_ids=list(range(NC)))
    return res.results


def vec_pb(v, nblk):
    return np.ascontiguousarray(np.asarray(v, np.float32).reshape(nblk, 128).T)


def build_L0():
    b = B()
    w = b.din("w", [D, 1536])
    bb = b.din("b", [128, 12])
    cc = b.din("cc", [128, 8, 2])
    o = b.dout("o", [128, 12, 2])
    wsb = b.sb("wsb", [128, 8, 1536])
    bsb = b.sb("bsb", [128, 12])
    ccs = b.sb("ccs", [128, 8, 2])
    sc = b.sb("sc", [128, 8, 2])
    osb = b.sb("osb", [128, 12, 2])
    ps = b.psum("ps", [128, 12, 2])
    for kb in range(8):
        b.dma(wsb[:, kb, :], w[kb * 128:(kb + 1) * 128, :], w=["w%d" % kb])
    b.dma(bsb[:], bb, w=["bsb"])
    b.dma(ccs[:], cc, w=["ccs"])
    b.act(sc[:], ccs[:], AF.Silu, ["ccs"], ["sc"])
    for cb in range(12):
        for kb in range(8):
            b.mm(ps[:, cb, :], wsb[:, kb, cb * 128:(cb + 1) * 128], sc[:, kb, :], kb == 0, kb == 7,
                 ["w%d" % kb, "sc"], ["ps"])
    b.tt("dve", osb[:], ps[:], bsb[:].unsqueeze(2).to_broadcast([128, 12, 2]), ALU.add, ["ps", "bsb"], ["osb"])
    b.dma(o, osb[:], r=["osb"])
    return b


def host_L0(inp):
    b = build_L0()
    w_ada = inp["w_ada"]
    b_ada = inp["b_ada"]
    cc = np.stack([vec_pb(inp["c"][0], 8), vec_pb(inp["c_ctx"], 8)], axis=-1)
    maps = []
    for k in range(NC):
        i, q = k // 4, k % 4
        maps.append({"w": np.ascontiguousarray(w_ada[i][:, q * 1536:(q + 1) * 1536]),
                     "b": vec_pb(b_ada[i][q * 1536:(q + 1) * 1536], 12), "cc": cc})
    res = run(b, maps)
    mods = np.zeros((2, 6144, 2), np.float32)
    for k in range(NC):
        i, q = k // 4, k % 4
        o = res[k]["o"]
        mods[i, q * 1536:(q + 1) * 1536, :] = o.transpose(1, 0, 2).reshape(1536, 2)
    return mods


def mod_vec(mods, layer, which, j):
    return vec_pb(mods[layer, which * D:(which + 1) * D, j], 8)


def rms_rstd(b, h_ap_fn, nblk, n, onesD, sq, pst, rstd, tmp, rkeys, tag, inv_dim_in_ones=True):
    for kb in range(nblk):
        b.act(sq[:, kb, :n], h_ap_fn(kb), AF.Square, rkeys, [tag + "sq%d" % kb])
    for kb in range(nblk):
        b.mm(pst[:, :n], onesD[:], sq[:, kb, :n], kb == 0, kb == nblk - 1, [tag + "sq%d" % kb, "onesD"], [tag + "pst"])
    b.act(tmp[:, :n], pst[:, :n], AF.Sqrt, [tag + "pst", "epsb"], [tag + "tmp"], bias=b.epsb[:, 0:1])
    b.recip(rstd[:, :n], tmp[:, :n], [tag + "tmp"], [tag + "rstd"])


def make_consts(b, dim):
    onesD = b.sb("onesD", [128, 128], BF16)
    b.memset("pool", onesD[:], 1.0 / dim, ["onesD"])
    epsb = b.sb("epsb", [128, 1])
    b.memset("pool", epsb[:], EPS, ["epsb"])
    b.epsb = epsb
    return onesD


def load_weight_bf16(b, dst, src, nkb, ncols, stage, tag, eng_cycle=("act", "pool")):
    for kb in range(nkb):
        st = stage[kb % len(stage)]
        sk = tag + "st%d" % (kb % len(stage))
        b.dma(st[:, :ncols], src[kb * 128:(kb + 1) * 128, :], w=[sk])
        b.copy(eng_cycle[kb % len(eng_cycle)], dst[:, kb, :], st[:, :ncols], [sk], [tag + "w%d" % kb])


def build_LA():
    b = B()
    xT = b.din("xT", [D, TPC])
    ridx = b.din("ridx", [128, 32])
    cidx = b.din("cidx", [128, 64])
    cxT = b.din("cxT", [D, 32])
    pv = b.din("pv", [128, 5, 8])
    w_in = b.din("w_in", [D, 1536])
    hT = b.dout("hT", [D, TPC])
    uT = b.dout("uT", [512, TPC])
    vbT = b.dout("vbT", [512, TPC])
    ucT = b.dout("ucT", [512, 32])

    onesD = make_consts(b, D)
    h = b.sb("h", [128, 8, TPC])
    ri = b.sb("ri", [128, 32])
    ci = b.sb("ci", [128, 64])
    hc = b.sb("hc", [128, 8, 32])
    pvs = b.sb("pvs", [128, 5, 8])
    gm = b.sb("gm", [128, 2, 8])
    win = b.sb("win", [128, 8, 1536], BF16)
    stage = [b.sb("stg%d" % i, [128, 1536]) for i in range(2)]
    for kb in range(8):
        b.dma(h[:, kb, :], xT[kb * 128:(kb + 1) * 128, :], w=["h%d" % kb])
    b.dma(ri[:], ridx, w=["ri"])
    b.dma(ci[:], cidx, w=["ci"])
    b.dma(hc[:], cxT.rearrange("(kb p) n -> p kb n", p=128), w=["hc"])
    b.dma(pvs[:], pv, w=["pvs"])
    load_weight_bf16(b, win, w_in, 8, 1536, stage, "win")
    b.ts("dve", gm[:, 0, :], pvs[:, 2, :], 1.0, ALU.add, ["pvs"], ["gm0a"])
    b.tt("dve", gm[:, 0, :], gm[:, 0, :], pvs[:, 0, :], ALU.mult, ["gm0a", "pvs"], ["gm0"])
    b.ts("dve", gm[:, 1, :], pvs[:, 4, :], 1.0, ALU.add, ["pvs"], ["gm1a"])
    b.tt("dve", gm[:, 1, :], gm[:, 1, :], pvs[:, 0, :], ALU.mult, ["gm1a", "pvs"], ["gm1"])
    ki0 = b.sb("ki0", [128, 2], I32)
    kf0 = b.sb("kf0", [128, 2])
    om = b.sb("om", [128, 2])
    b.iota(ki0[:], [[128, 2]], 0, 1, ["ki0"])
    b.copy("dve", kf0[:], ki0[:], ["ki0"], ["kf0"])
    b.act(om[:], kf0[:], AF.Exp, ["kf0"], ["om"], scale=-math.log(10000.0) / 256.0)

    tki = b.sb("t_ki", [128, 64], I32); tkf = b.sb("t_kf", [128, 64]); tra = b.sb("t_ra", [128, 64]); trb = b.sb("t_rb", [128, 64])
    ang = b.sb("ang", [128, 64])
    rowtab = b.sb("rowtab", [128, 4, 32])
    coltab = b.sb("coltab", [128, 4, 64])
    for blk in range(4):
        j = blk % 2
        ph = PI / 2 if blk >= 2 else 0.0
        for (idx, ik, n_, tab, tk) in ((ri, "ri", 32, rowtab, "rowtab"), (ci, "ci", 64, coltab, "coltab")):
            tmp = {"ki": tki[:, :n_], "kf": tkf[:, :n_], "ra": tra[:, :n_], "rb": trb[:, :n_], "key": "t_"}
            b.ts("dve", ang[:, :n_], idx[:], om[:, j:j + 1], ALU.mult, [ik, "om"], ["ang"], s2=ph, op1=ALU.add)
            b.sin_of(tab[:, blk, :], ang[:, :n_], None, tmp, ["ang"], [tk])
    sq = b.sb("sq", [128, 8, CW], BF16)
    pst = b.psum("pst")
    rstd = b.sb("rstd", [128, CW])
    rt = b.sb("rt", [128, CW])
    hn = b.sb("hn", [128, CW])
    n = b.sb("n", [128, 8, CW], BF16)
    PS = [b.psum("ps%d" % i) for i in range(4)]
    uo = b.sb("uo", [128, 4, CW])
    vo = b.sb("vo", [128, 4, CW])
    sig = b.sb("sig", [128, CW])
    psn = 0

    for c in range(NCHUNK):
        sl = slice(c * CW, (c + 1) * CW)
        b.tt("pool", h[:, 0:4, sl].rearrange("p b (r c) -> p b r c", c=64), h[:, 0:4, sl].rearrange("p b (r c) -> p b r c", c=64),
             rowtab[:, :, 8 * c:8 * c + 8].unsqueeze(3).to_broadcast([128, 4, 8, 64]), ALU.add,
             ["h0", "h1", "h2", "h3", "rowtab"], ["h0", "h1", "h2", "h3"] + ["hf%d_%d" % (k, c) for k in range(4)])
        b.tt("dve", h[:, 4:8, sl].rearrange("p b (r c) -> p b r c", c=64), h[:, 4:8, sl].rearrange("p b (r c) -> p b r c", c=64),
             coltab[:].unsqueeze(2).to_broadcast([128, 4, 8, 64]), ALU.add,
             ["h4", "h5", "h6", "h7", "coltab"], ["h4", "h5", "h6", "h7"] + ["hf%d_%d" % (k, c) for k in range(4, 8)])
        b.dma(hT[:, sl].rearrange("(kb p) n -> p kb n", p=128), h[:, :, sl], r=["hf%d_%d" % (k, c) for k in range(8)])
        rms_rstd(b, lambda kb: h[:, kb, sl], 8, CW, onesD, sq, pst, rstd, rt, ["h%d" % k for k in range(8)], "A")
        for kb in range(8):
            b.tt("dve", hn[:], h[:, kb, sl], rstd[:], ALU.mult, ["h%d" % kb, "Arstd"], ["hn"])
            b.act(n[:, kb, :], hn[:], AF.Identity, ["hn", "gm0", "pvs"], ["n%d" % kb],
                  bias=pvs[:, 1, kb:kb + 1], scale=gm[:, 0, kb:kb + 1])
        nkeys = ["n%d" % k for k in range(8)]
        wkeys = ["winw%d" % k for k in range(8)]
        for ob in range(4):
            ps = PS[psn % 4]; pk = "ps%d" % (psn % 4); psn += 1
            for kb in range(8):
                b.mm(ps[:], win[:, kb, ob * 128:(ob + 1) * 128], n[:, kb, :], kb == 0, kb == 7, nkeys + wkeys, [pk])
            b.copy("act", uo[:, ob, :], ps[:], [pk], ["uo"])
        b.dma(uT[:, sl].rearrange("(ob p) n -> p ob n", p=128), uo[:], r=["uo"])
        for jv in range(4):
            psg = PS[psn % 4]; pkg = "ps%d" % (psn % 4); psn += 1
            for kb in range(8):
                b.mm(psg[:], win[:, kb, (8 + jv) * 128:(9 + jv) * 128], n[:, kb, :], kb == 0, kb == 7, nkeys + wkeys, [pkg])
            b.act(sig[:], psg[:], AF.Sigmoid, [pkg], ["sig"])
            psv = PS[psn % 4]; pkv = "ps%d" % (psn % 4); psn += 1
            for kb in range(8):
                b.mm(psv[:], win[:, kb, (4 + jv) * 128:(5 + jv) * 128], n[:, kb, :], kb == 0, kb == 7, nkeys + wkeys, [pkv])
            b.tt("dve", vo[:, jv, :], psv[:], sig[:], ALU.mult, [pkv, "sig"], ["vo"])
        b.dma(vbT[:, sl].rearrange("(ob p) n -> p ob n", p=128), vo[:], r=["vo"])
    rms_rstd(b, lambda kb: hc[:, kb, :], 8, 32, onesD, sq, pst, rstd, rt, ["hc"], "C")
    for kb in range(8):
        b.tt("dve", hn[:, :32], hc[:, kb, :], rstd[:, :32], ALU.mult, ["hc", "Crstd"], ["hn"])
        b.act(n[:, kb, :32], hn[:, :32], AF.Identity, ["hn", "gm1", "pvs"], ["n%d" % kb],
              bias=pvs[:, 3, kb:kb + 1], scale=gm[:, 1, kb:kb + 1])
    for ob in range(4):
        ps = PS[psn % 4]; pk = "ps%d" % (psn % 4); psn += 1
        for kb in range(8):
            b.mm(ps[:, :32], win[:, kb, ob * 128:(ob + 1) * 128], n[:, kb, :32], kb == 0, kb == 7,
                 ["n%d" % k for k in range(8)] + ["winw%d" % k for k in range(8)], [pk])
        b.copy("act", uo[:, ob, :32], ps[:, :32], [pk], ["uo"])
    b.dma(ucT.rearrange("(ob p) n -> p ob n", p=128), uo[:, :, :32], r=["uo"])
    return b


def host_LA(inp, mods):
    b = build_LA()
    x = inp["x"][0]
    ctx = inp["ctx"][0]
    pv = np.stack([vec_pb(inp["norm_mix_g"][0], 8), mod_vec(mods, 0, 0, 0), mod_vec(mods, 0, 1, 0),
                   mod_vec(mods, 0, 0, 1), mod_vec(mods, 0, 1, 1)], axis=1)
    w_in = np.ascontiguousarray(inp["w_in"][0])
    tok = np.arange(L)
    maps = []
    for k in range(NC):
        t = tok[k * TPC:(k + 1) * TPC]
        maps.append({
            "xT": np.ascontiguousarray(x[k * TPC:(k + 1) * TPC].T),
            "ridx": np.ascontiguousarray(np.broadcast_to((t[::64] // 64).astype(np.float32), (128, 32))),
            "cidx": np.ascontiguousarray(np.broadcast_to(np.arange(64, dtype=np.float32), (128, 64))),
            "cxT": np.ascontiguousarray(ctx[k * 32:(k + 1) * 32].T),
            "pv": np.ascontiguousarray(pv), "w_in": w_in})
    res = run(b, maps)
    hT = [r["hT"] for r in res]
    u = np.concatenate([r["uT"].T for r in res], axis=0)
    vb = np.concatenate([r["vbT"].T for r in res], axis=0)
    uc = np.concatenate([r["ucT"].T for r in res], axis=0)
    return hT, u, vb, uc


NSS = (CTX + L) // 8
NXS = L // 8
LB_CH = [(0, 32)] + [(32 + i * 512, 512) for i in range(4)]


def reduce_ang(b, out, ang, tmp, r, w):
    ki, kf, rb = tmp["ki"], tmp["kf"], tmp["rb"]
    tk = tmp["key"]
    ra = out
    b.ts("dve", ki, ang, 1.0 / TWO_PI, ALU.mult, r, [tk + "ki"])
    b.copy("dve", kf, ki, [tk + "ki"], [tk + "kf"])
    b.stt("dve", ra, kf, -C1, ang, ALU.mult, ALU.add, r + [tk + "kf"], w)
    b.stt("dve", rb, kf, -C2, ra, ALU.mult, ALU.add, [tk + "kf"] + w, [tk + "rb"])
    b.ts("dve", kf, rb, PI, ALU.is_gt, [tk + "rb"], [tk + "kf"], s2=-TWO_PI, op1=ALU.mult)
    b.tt("dve", ra, rb, kf, ALU.add, [tk + "rb", tk + "kf"], w)
    b.ts("dve", kf, ra, -PI, ALU.is_lt, w, [tk + "kf"], s2=TWO_PI, op1=ALU.mult)
    b.tt("dve", rb, ra, kf, ALU.add, w + [tk + "kf"], [tk + "rb"])
    b.ts("dve", ra, rb, -PI, ALU.max, [tk + "rb"], w, s2=PI, op1=ALU.min)


def build_LB():
    b = B()
    U = b.din("U", [8, 128, NSS])
    p_lre = b.din("lamre", [128, 8]); p_lim = b.din("lamim", [128, 8]); p_ls = b.din("lstep", [128, 8])
    p_bre = b.din("bre", [128, 8, 16]); p_bim = b.din("bim", [128, 8, 16])
    p_cre = b.din("cre", [128, 8, 16]); p_cim = b.din("cim", [128, 8, 16])
    p_mF = b.din("maskF", [128, 128]); p_mB = b.din("maskB", [128, 128]); p_id = b.din("ident", [128, 128])
    Y = b.dout("Y", [8, 128, NXS])

    def T(name, shape, dt=F32):
        return b.sb("s_" + name, shape, dt)

    lre = T("lre", [128, 8]); lim = T("lim", [128, 8]); ls = T("ls", [128, 8])
    bre = T("bre", [128, 8, 16]); bim = T("bim", [128, 8, 16]); cre = T("cre", [128, 8, 16]); cim = T("cim", [128, 8, 16])
    mF = T("mF", [128, 128]); mB = T("mB", [128, 128]); ident = T("ident", [128, 128])
    for t, src, k in [(lre, p_lre, "lre"), (lim, p_lim, "lim"), (ls, p_ls, "ls"), (bre, p_bre, "bre"), (bim, p_bim, "bim"),
                      (cre, p_cre, "cre"), (cim, p_cim, "cim"), (mF, p_mF, "mF"), (mB, p_mB, "mB"), (ident, p_id, "ident")]:
        b.dma(t[:], src, w=[k])
    dt_ = T("dt", [128, 8]); lr = T("lr", [128, 8]); a = T("a", [128, 8]); th = T("th", [128, 8])
    b.act(dt_[:], ls[:], AF.Exp, ["ls"], ["dt"])
    b.ts("dve", lr[:], lre[:], -1e-4, ALU.min, ["lre"], ["lr"])
    b.tt("dve", a[:], lr[:], dt_[:], ALU.mult, ["lr", "dt"], ["a"])
    b.tt("dve", th[:], lim[:], dt_[:], ALU.mult, ["lim", "dt"], ["th"])
    kiA = T("kiA", [128, 16], I32); kiD = T("kiD", [128, 16], I32); kA = T("kA", [128, 16]); kD = T("kD", [128, 16])
    b.iota(kiA[:], [[1, 16]], -7, 0, ["kiA"]); b.iota(kiD[:], [[-1, 16]], 8, 0, ["kiD"])
    b.copy("dve", kA[:], kiA[:], ["kiA"], ["kA"]); b.copy("dve", kD[:], kiD[:], ["kiD"], ["kD"])
    S3 = [128, 8, 16]
    tmp3 = {"ki": T("p_ki", S3, I32)[:], "kf": T("p_kf", S3)[:], "ra": T("p_ra", S3)[:], "rb": T("p_rb", S3)[:], "key": "p_"}
    ak = T("ak", S3); tk_ = T("tk", S3); mag = T("mag", S3); sn = T("sn", S3); cs = T("cs", S3)
    PW = {}
    for nm, kv in (("A", kA), ("D", kD)):
        kb_ = kv[:].unsqueeze(1).to_broadcast(S3)
        b.tt("dve", ak[:], a[:].unsqueeze(2).to_broadcast(S3), kb_, ALU.mult, ["a", "k" + nm], ["ak"])
        b.act(mag[:], ak[:], AF.Exp, ["ak"], ["mag"])
        b.tt("dve", tk_[:], th[:].unsqueeze(2).to_broadcast(S3), kb_, ALU.mult, ["th", "k" + nm], ["tk"])
        b.sin_of(sn[:], tk_[:], S3, tmp3, ["tk"], ["sn"])
        b.ts("dve", tk_[:], tk_[:], PI / 2, ALU.add, ["tk"], ["tk"])
        b.sin_of(cs[:], tk_[:], S3, tmp3, ["tk"], ["cs"])
        pr = T("PWr" + nm, S3); pi_ = T("PWi" + nm, S3)
        b.tt("dve", pr[:], mag[:], cs[:], ALU.mult, ["mag", "cs"], ["PWr" + nm])
        b.tt("dve", pi_[:], mag[:], sn[:], ALU.mult, ["mag", "sn"], ["PWi" + nm])
        PW[nm] = (pr, pi_, "PWr" + nm, "PWi" + nm)
    l1r = PW["A"][0][:, :, 8]; l1i = PW["A"][1][:, :, 8]
    nr = T("nr", [128, 8]); t1 = T("t1", [128, 8]); t2 = T("t2", [128, 8]); rden = T("rden", [128, 8])
    wr = T("wr", [128, 8]); wi = T("wi", [128, 8])
    b.ts("dve", nr[:], l1r, -1.0, ALU.add, ["PWrA"], ["nr"])
    b.tt("dve", t1[:], lr[:], lr[:], ALU.mult, ["lr"], ["t1"])
    b.tt("dve", t2[:], lim[:], lim[:], ALU.mult, ["lim"], ["t2"])
    b.tt("dve", t1[:], t1[:], t2[:], ALU.add, ["t1", "t2"], ["t1"])
    b.recip(rden[:], t1[:], ["t1"], ["rden"])
    b.tt("dve", t1[:], nr[:], lr[:], ALU.mult, ["nr", "lr"], ["t1"])
    b.tt("dve", t2[:], l1i, lim[:], ALU.mult, ["PWiA", "lim"], ["t2"])
    b.tt("dve", t1[:], t1[:], t2[:], ALU.add, ["t1", "t2"], ["t1"])
    b.tt("dve", wr[:], t1[:], rden[:], ALU.mult, ["t1", "rden"], ["wr"])
    b.tt("dve", t1[:], l1i, lr[:], ALU.mult, ["PWiA", "lr"], ["t1"])
    b.tt("dve", t2[:], nr[:], lim[:], ALU.mult, ["nr", "lim"], ["t2"])
    b.tt("dve", t1[:], t1[:], t2[:], ALU.subtract, ["t1", "t2"], ["t1"])
    b.tt("dve", wi[:], t1[:], rden[:], ALU.mult, ["t1", "rden"], ["wi"])
    bbr = T("bbr", S3); bbi = T("bbi", S3); t3 = T("t3", S3); t4 = T("t4", S3)
    wrb = wr[:].unsqueeze(2).to_broadcast(S3); wib = wi[:].unsqueeze(2).to_broadcast(S3)
    b.tt("dve", t3[:], wrb, bre[:], ALU.mult, ["wr", "bre"], ["t3"])
    b.tt("dve", t4[:], wib, bim[:], ALU.mult, ["wi", "bim"], ["t4"])
    b.tt("dve", bbr[:], t3[:], t4[:], ALU.subtract, ["t3", "t4"], ["bbr"])
    b.tt("dve", t3[:], wrb, bim[:], ALU.mult, ["wr", "bim"], ["t3"])
    b.tt("dve", t4[:], wib, bre[:], ALU.mult, ["wi", "bre"], ["t4"])
    b.tt("dve", bbi[:], t3[:], t4[:], ALU.add, ["t3", "t4"], ["bbi"])

    S4 = [128, 4, 8, 16]
    o1 = T("o1", S4); o2 = T("o2", S4); oR = T("oR", S4); oI = T("oI", S4)

    def cplx_outer(nm, ksl, d, Vr, Vi, vkeys):
        pr, pi_, kr, ki_ = PW[nm]
        dsl = slice(4 * d, 4 * d + 4)
        Pr = pr[:, dsl, ksl].unsqueeze(3).to_broadcast(S4); Pi = pi_[:, dsl, ksl].unsqueeze(3).to_broadcast(S4)
        vr = Vr[:, dsl, :].unsqueeze(2).to_broadcast(S4); vi = Vi[:, dsl, :].unsqueeze(2).to_broadcast(S4)
        b.tt("dve", o1[:], Pr, vr, ALU.mult, [kr, vkeys[0]], ["o1"])
        b.tt("dve", o2[:], Pi, vi, ALU.mult, [ki_, vkeys[1]], ["o2"])
        b.tt("dve", oR[:], o1[:], o2[:], ALU.subtract, ["o1", "o2"], ["oR"])
        b.tt("dve", o1[:], Pr, vi, ALU.mult, [kr, vkeys[1]], ["o1"])
        b.tt("dve", o2[:], Pi, vr, ALU.mult, [ki_, vkeys[0]], ["o2"])
        b.tt("dve", oI[:], o1[:], o2[:], ALU.add, ["o1", "o2"], ["oI"])

    def halves(dst, top, bot, neg_top, neg_bot, rk, wk):
        for (lo, hi, src, neg) in ((0, 64, top, neg_top), (64, 128, bot, neg_bot)):
            if neg:
                b.ts("dve", dst[lo:hi], src[lo:hi], -1.0, ALU.mult, rk, [wk])
            else:
                b.copy("dve", dst[lo:hi], src[lo:hi], rk, [wk])

    S4m = [128, 4, 128]
    BT1 = T("BT1", S4); BT2 = T("BT2", S4); TL = T("TL", S4); TR = T("TR", S4)
    Bc1 = T("Bc1", [128, 8, 128], BF16); Bc2 = T("Bc2", [128, 8, 128], BF16)
    CcT = T("CcT", [128, 8, 8, 16], BF16); Toep = T("Toep", [128, 8, 128], BF16)
    pT = b.psum("pT", [128, 128])
    for d in (0, 1):
        if d == 0:
            cplx_outer("D", slice(1, 9), 0, bbr, bbi, ["bbr", "bbi"])
        else:
            cplx_outer("A", slice(7, 15), 1, bbr, bbi, ["bbr", "bbi"])
        halves(BT1, oR, oI, False, False, ["oR", "oI"], "BT1")
        halves(BT2, oI, oR, True, False, ["oR", "oI"], "BT2")
        if d == 0:
            cplx_outer("D", slice(8, 16), 0, bbr, bbi, ["bbr", "bbi"])
            halves(TL, oR, oI, False, False, ["oR", "oI"], "TL")
        else:
            b.copy("dve", TL[:], BT1[:], ["BT1"], ["TL"])
        for gl in range(4):
            q = d * 4 + gl
            for (src, dst, sk, dk) in ((BT1, Bc1, "BT1", "Bc1"), (BT2, Bc2, "BT2", "Bc2")):
                b.transpose(pT[:], src[:, gl].rearrange("p a b -> p (a b)"), ident[:], [sk, "ident"], ["pT"])
                b.copy("act", dst[:, q, :], pT[:], ["pT"], [dk])
        if d == 0:
            cplx_outer("A", slice(8, 16), 0, cre, cim, ["cre", "cim"])
        else:
            cplx_outer("D", slice(0, 8), 1, cre, cim, ["cre", "cim"])
        halves(CcT[:, 4 * d:4 * d + 4], oR, oI, False, True, ["oR", "oI"], "CcT")
        if d == 0:
            cplx_outer("A", slice(7, 15), 0, cre, cim, ["cre", "cim"])
        else:
            cplx_outer("D", slice(8, 16), 1, cre, cim, ["cre", "cim"])
        halves(TR, oR, oI, False, True, ["oR", "oI"], "TR")
        for gl in range(4):
            q = d * 4 + gl
            b.mm(pT[:], TL[:, gl].rearrange("p a b -> p (a b)"), TR[:, gl].rearrange("p a b -> p (a b)"), True, True,
                 ["TL", "TR"], ["pT"])
            b.tt("dve", Toep[:, q, :], pT[:], (mF if d == 0 else mB)[:], ALU.mult, ["pT", "mF", "mB"], ["Toep"])
    r8 = T("r8", [128, 8]); th8 = T("th8", [128, 8]); th8r = T("th8r", [128, 8])
    b.act(r8[:], a[:], AF.Exp, ["a"], ["r8"], scale=8.0)
    b.ts("dve", th8[:], th[:], 8.0, ALU.mult, ["th"], ["th8"])
    tmp2 = {"ki": T("q_ki", [128, 8], I32)[:], "kf": T("q_kf", [128, 8])[:], "rb": T("q_rb", [128, 8])[:], "key": "q_"}
    reduce_ang(b, th8r[:], th8[:], tmp2, ["th8"], ["th8r"])
    NT = 33 * 64
    th64 = T("th64", [128, 8]); th64r = T("th64r", [128, 8])
    b.ts("dve", th64[:], th8r[:], 64.0, ALU.mult, ["th8r"], ["th64"])
    reduce_ang(b, th64r[:], th64[:], tmp2, ["th64"], ["th64r"])
    bvi = T("bvi", [128, 64], I32); bv = T("bv", [128, 64])
    b.iota(bvi[:], [[1, 64]], 0, 0, ["bvi"])
    b.copy("dve", bv[:], bvi[:], ["bvi"], ["bv"])
    SB_ = [128, 8, 64]; SA_ = [128, 8, 33]
    tmpB = {"ki": T("b_ki", SB_, I32), "kf": T("b_kf", SB_), "ra": T("b_ra", SB_), "rb": T("b_rb", SB_)}
    angB = T("angB", SB_); sB = T("sB", SB_); cB = T("cB", SB_); sA = T("sA", SA_); cA = T("cA", SA_)

    def small_tab(thv, thk, n_, sT, cT, nm):
        tm = {"ki": tmpB["ki"][:, :, :n_], "kf": tmpB["kf"][:, :, :n_], "ra": tmpB["ra"][:, :, :n_], "rb": tmpB["rb"][:, :, :n_], "key": "b_"}
        sh = [128, 8, n_]
        b.tt("dve", angB[:, :, :n_], thv[:].unsqueeze(2).to_broadcast(sh), bv[:, :n_].unsqueeze(1).to_broadcast(sh), ALU.mult,
             [thk, "bv"], ["angB"])
        b.sin_of(sT[:], angB[:, :, :n_], None, tm, ["angB"], ["s" + nm])
        b.ts("dve", angB[:, :, :n_], angB[:, :, :n_], PI / 2, ALU.add, ["angB"], ["angB"])
        b.sin_of(cT[:], angB[:, :, :n_], None, tm, ["angB"], ["c" + nm])
    small_tab(th8r, "th8r", 64, sB, cB, "B")
    small_tab(th64r, "th64r", 33, sA, cA, "A")
    sinTs = [T("sinT%d" % i, [128, NT]) for i in range(2)]
    cosTs = [T("cosT%d" % i, [128, NT]) for i in range(2)]
    e1 = T("e1", [128, NT]); e2 = T("e2", [128, NT])
    Uf = [T("Uf%d" % i, [128, NSS]) for i in range(2)]
    Ub = [T("Ub%d" % i, [128, NSS], BF16) for i in range(2)]
    sx1 = T("sx1", [128, NT]); sx2 = T("sx2", [128, NT])
    mm_ = [[T("m%d_%d" % (i, p_), [128, 512]) for i in range(4)] for p_ in range(2)]
    xt1s = [T("xt1_%d" % p_, [128, 512]) for p_ in range(2)]; xt2s = [T("xt2_%d" % p_, [128, 512]) for p_ in range(2)]
    Shs = [T("Sh%d" % p_, [128, 512], BF16) for p_ in range(2)]
    yo = [T("yo%d" % i, [128, 512]) for i in range(2)]
    PX = [b.psum("px%d" % i) for i in range(4)]
    PY = [b.psum("py%d" % i) for i in range(2)]
    yn = 0
    for q in range(8):
        ub, uf = Ub[q % 2], Uf[q % 2]
        uk = "Ub%d" % (q % 2)
        b.dma(uf[:], U[q], w=["Uf%d" % (q % 2)])
        b.copy("act", ub[:], uf[:], ["Uf%d" % (q % 2)], [uk])
        sinT = sinTs[q % 2]; cosT = cosTs[q % 2]
        skq = "sinT%d" % (q % 2); ckq = "cosT%d" % (q % 2)
        S3_ = [128, 33, 64]
        cAq = cA[:, q, :].unsqueeze(2).to_broadcast(S3_); sAq = sA[:, q, :].unsqueeze(2).to_broadcast(S3_)
        cBq = cB[:, q, :].unsqueeze(1).to_broadcast(S3_); sBq = sB[:, q, :].unsqueeze(1).to_broadcast(S3_)
        v3 = lambda t: t[:].rearrange("p (a c) -> p a c", c=64)
        b.tt("dve", v3(e1), cAq, cBq, ALU.mult, ["cA", "cB"], ["e1"])
        b.tt("dve", v3(e2), sAq, sBq, ALU.mult, ["sA", "sB"], ["e2"])
        b.tt("pool", cosT[:], e1[:], e2[:], ALU.subtract, ["e1", "e2"], [ckq])
        b.tt("dve", v3(e1), sAq, cBq, ALU.mult, ["sA", "cB"], ["e1"])
        b.tt("dve", v3(e2), cAq, sBq, ALU.mult, ["cA", "sB"], ["e2"])
        b.tt("pool", sinT[:], e1[:], e2[:], ALU.add, ["e1", "e2"], [skq])
        b.memset("pool", sx1[:, 0:1], 0.0, ["sx1"])
        b.memset("pool", sx2[:, 0:1], 0.0, ["sx2"])
        for ci, (c0, n) in enumerate(LB_CH):
            X1, X2 = PX[(2 * ci) % 4], PX[(2 * ci + 1) % 4]
            k1, k2 = "px%d" % ((2 * ci) % 4), "px%d" % ((2 * ci + 1) % 4)
            b.mm(X1[:, :n], Bc1[:, q, :], ub[:, c0:c0 + n], True, True, ["Bc1", uk], [k1])
            b.mm(X2[:, :n], Bc2[:, q, :], ub[:, c0:c0 + n], True, True, ["Bc2", uk], [k2])
            cD = cosT[:, c0 + 1:c0 + n + 1]; sD = sinT[:, c0 + 1:c0 + n + 1]
            pp_ = (q * len(LB_CH) + ci) % 2
            m = mm_[pp_]; xt1 = xt1s[pp_]; xt2 = xt2s[pp_]; Sh = Shs[pp_]
            mk = ["m%d_%d" % (i, pp_) for i in range(4)]
            x1k, x2k, shk = "xt1_%d" % pp_, "xt2_%d" % pp_, "Sh%d" % pp_
            b.tt("dve", m[0][:, :n], cD, X1[:, :n], ALU.mult, [ckq, k1], [mk[0]])
            b.tt("dve", m[1][:, :n], sD, X2[:, :n], ALU.mult, [skq, k2], [mk[1]])
            b.tt("dve", m[2][:, :n], cD, X2[:, :n], ALU.mult, [ckq, k2], [mk[2]])
            b.tt("dve", m[3][:, :n], sD, X1[:, :n], ALU.mult, [skq, k1], [mk[3]])
            b.tt("pool", xt1[:, :n], m[0][:, :n], m[1][:, :n], ALU.subtract, [mk[0], mk[1]], [x1k])
            b.tt("pool", xt2[:, :n], m[2][:, :n], m[3][:, :n], ALU.add, [mk[2], mk[3]], [x2k])
            r8b = r8[:, q:q + 1].to_broadcast([128, n])
            b.scan(sx1[:, c0 + 1:c0 + n + 1], r8b, xt1[:, :n], sx1[:, c0:c0 + 1], ["r8", x1k, "sx1"], ["sx1"])
            b.scan(sx2[:, c0 + 1:c0 + n + 1], r8b, xt2[:, :n], sx2[:, c0:c0 + 1], ["r8", x2k, "sx2"], ["sx2"])
            if c0 < 32:
                continue
            b.tt("dve", m[0][:, :n], cosT[:, c0:c0 + n], sx1[:, c0:c0 + n], ALU.mult, [ckq, "sx1"], [mk[0]])
            b.tt("dve", m[1][:, :n], sinT[:, c0:c0 + n], sx2[:, c0:c0 + n], ALU.mult, [skq, "sx2"], [mk[1]])
            b.tt("pool", Sh[:, :n], m[0][:, :n], m[1][:, :n], ALU.add, [mk[0], mk[1]], [shk])
            py = PY[yn % 2]; pk = "py%d" % (yn % 2); yt = yo[yn % 2]; yk = "yo%d" % (yn % 2); yn += 1
            b.mm(py[:, :n], Toep[:, q, :], ub[:, c0:c0 + n], True, False, ["Toep", uk], [pk])
            b.mm(py[:, :n], CcT[:, q].rearrange("p a b -> p (a b)"), Sh[:, :n], False, True, ["CcT", shk], [pk])
            b.copy("act", yt[:, :n], py[:, :n], [pk], [yk])
            b.dma(Y[q][:, c0 - 32:c0 - 32 + n], yt[:, :n], r=[yk])
    return b


def host_LB(inp, u, uc):
    b = build_LB()
    tau = np.arange(128) // 16
    maskF = (tau[None, :] >= tau[:, None]).astype(np.float32)
    maskB = (tau[:, None] >= tau[None, :]).astype(np.float32)
    ident = np.eye(128, dtype=np.float32)
    pidx = np.arange(128) % 64
    maps = []
    for k in range(NC):
        Uk = np.zeros((8, 128, NSS), np.float32)
        pr = {n: np.zeros((128, 8), np.float32) for n in ("lamre", "lamim", "lstep")}
        pb = {n: np.zeros((128, 8, 16), np.float32) for n in ("bre", "bim", "cre", "cim")}
        for d in range(2):
            for gl in range(4):
                g = 4 * k + gl
                q = d * 4 + gl
                cs = slice(g * 16, g * 16 + 16)
                if d == 0:
                    seq = np.concatenate([uc[:, cs], u[:, cs]], axis=0).reshape(NSS, 128)
                else:
                    seq = np.concatenate([u[:, cs], uc[:, cs]], axis=0).reshape(NSS, 128)[::-1]
                Uk[q] = seq.T
                pr["lamre"][:, q] = inp["s5_lam_re"][0, d, g][pidx]
                pr["lamim"][:, q] = inp["s5_lam_im"][0, d, g][pidx]
                pr["lstep"][:, q] = inp["s5_log_step"][0, d, g]
                pb["bre"][:, q, :] = inp["s5_b_re"][0, d, g][pidx, :]
                pb["bim"][:, q, :] = inp["s5_b_im"][0, d, g][pidx, :]
                pb["cre"][:, q, :] = inp["s5_c_re"][0, d, g].T[pidx, :]
                pb["cim"][:, q, :] = inp["s5_c_im"][0, d, g].T[pidx, :]
        mp = {"U": Uk, "maskF": maskF, "maskB": maskB, "ident": ident}
        mp.update(pr); mp.update(pb)
        maps.append(mp)
    res = run(b, maps)
    yA = np.zeros((L, 512), np.float32)
    yB = np.zeros((L, 512), np.float32)
    for k in range(NC):
        Yk = res[k]["Y"]
        for gl in range(4):
            g = 4 * k + gl
            cs = slice(g * 16, g * 16 + 16)
            yA[:, cs] = Yk[gl].T.reshape(NXS, 8, 16).reshape(L, 16)
            yB[:, cs] = Yk[4 + gl][:, ::-1].T.reshape(NXS, 8, 16).reshape(L, 16)
    return yA, yB


def build_LC():
    b = B()
    hT = b.din("hT", [D, TPC]); uT = b.din("uT", [512, TPC]); yAT = b.din("yAT", [512, TPC]); yBT = b.din("yBT", [512, TPC])
    vbp = b.din("vbp", [512, TPC + 30])
    g1d = b.din("g1", [128, 8]); pcd = b.din("pc", [128, 4, 4]); cwd = b.din("cw", [128, 4, 31])
    identd = b.din("ident", [128, 128])
    w_glu = b.din("w_glu", [512, 512]); w_out = b.din("w_out", [D, D])
    oT = b.dout("oT", [D, TPC])
    ones512 = make_consts(b, 512)

    def T(name, shape, dt=F32):
        return b.sb("s_" + name, shape, dt)
    g1 = T("g1", [128, 8]); pc = T("pc", [128, 4, 4]); cw = T("cw", [128, 4, 31])
    b.dma(g1[:], g1d, w=["g1"]); b.dma(pc[:], pcd, w=["pc"]); b.dma(cw[:], cwd, w=["cw"])
    ident = T("ident", [128, 128]); b.dma(ident[:], identd, w=["ident"])
    dg = T("dg", [128, 4, 31, 128], BF16)
    for j in range(4):
        for tap in range(31):
            b.act(dg[:, j, tap, :], ident[:], AF.Identity, ["ident", "cw"], ["dg%d" % j], scale=cw[:, j, tap:tap + 1])
    wg = T("wg", [128, 4, 512], BF16); wo = T("wo", [128, 8, 1024], BF16)
    stage = [T("stg%d" % i, [128, 1024]) for i in range(2)]
    load_weight_bf16(b, wg, w_glu, 4, 512, stage, "wg")
    load_weight_bf16(b, wo, w_out, 8, 1024, stage, "wo")
    wgk = ["wgw%d" % k for k in range(4)]; wok = ["wow%d" % k for k in range(8)]
    hchs = [T("hch%d" % i, [128, 8, CW]) for i in range(2)]; uchs = [T("uch%d" % i, [128, 4, CW]) for i in range(1)] * 2
    yAcs = [T("yAc%d" % i, [128, 4, CW]) for i in range(1)] * 2; yBcs = [T("yBc%d" % i, [128, 4, CW]) for i in range(1)] * 2
    vbcs = [T("vbc%d" % i, [128, 4, CW + 30]) for i in range(2)]
    vbbs = [T("vbb%d" % i, [128, 4, CW + 30], BF16) for i in range(2)]
    ya = T("ya", [128, 4, CW]); ya1f = T("ya1f", [128, 4, CW]); ya1b = T("ya1b", [128, 4, CW], BF16)
    cat = T("cat", [128, 8, CW], BF16); acc = T("acc", [128, 4, CW])
    accb = T("accb", [128, 4, CW], BF16); accsq = T("accsq", [128, 4, CW], BF16)
    tA = T("tA", [128, CW]); tB = T("tB", [128, CW]); tC = T("tC", [128, CW]); tD = T("tD", [128, CW])
    tM = T("tM", [128, CW]); tR = T("tR", [128, CW])
    NPS = 4
    PS = [b.psum("ps%d" % i) for i in range(NPS)]
    PCV = [b.psum("pcv%d" % i) for i in range(2)]
    psm = b.psum("psm"); pse = b.psum("pse")
    pn = 0
    for c in range(NCHUNK):
        sl = slice(c * CW, (c + 1) * CW)
        pr_ = c % 2
        hch, uch, yAc, yBc, vbc = hchs[pr_], uchs[pr_], yAcs[pr_], yBcs[pr_], vbcs[pr_]
        HK = ["hch%d_%d" % (pr_, k) for k in range(8)]
        UK, YAK, YBK, VK = "uch0", "yAc0", "yBc0", "vbc%d" % pr_
        vbb = vbbs[pr_]; VBK = "vbb%d" % pr_
        b.dma(hch[:], hT[:, sl].rearrange("(kb p) n -> p kb n", p=128), w=HK)
        b.dma(uch[:], uT[:, sl].rearrange("(kb p) n -> p kb n", p=128), w=[UK])
        b.dma(yAc[:], yAT[:, sl].rearrange("(kb p) n -> p kb n", p=128), w=[YAK])
        b.dma(yBc[:], yBT[:, sl].rearrange("(kb p) n -> p kb n", p=128), w=[YBK])
        b.dma(vbc[:], vbp[:, c * CW:c * CW + CW + 30].rearrange("(kb p) n -> p kb n", p=128), w=[VK])
        b.copy("act", vbb[:], vbc[:], [VK], [VBK])

        def conv_mm(j):
            pcv = PCV[j % 2]; pck = "pcv%d" % (j % 2)
            for tap in range(31):
                b.mm(pcv[:], dg[:, j, tap, :], vbb[:, j, tap:tap + CW], tap == 0, tap == 30, ["dg%d" % j, VBK], [pck])

        def conv_ev(j):
            pcv = PCV[j % 2]; pck = "pcv%d" % (j % 2)
            b.act(acc[:, j, :], pcv[:], AF.Identity, [pck, "pc"], ["acc%d" % j], bias=pc[:, 1, j:j + 1])
            b.act(accb[:, j, :], pcv[:], AF.Identity, [pck, "pc"], ["accb%d" % j], bias=pc[:, 1, j:j + 1])
            b.act(accsq[:, j, :], pcv[:], AF.Square, [pck, "pc"], ["accsq%d" % j], bias=pc[:, 1, j:j + 1])
        conv_mm(0); conv_mm(1)
        for j in range(4):
            b.tt("pool", tA[:], yAc[:, j, :], yBc[:, j, :], ALU.add, [YAK, YBK], ["tA"])
            b.stt("dve", ya[:, j, :], uch[:, j, :], pc[:, 0, j:j + 1], tA[:], ALU.mult, ALU.add, [UK, "pc", "tA"], ["ya%d" % j])
            b.act(tB[:], ya[:, j, :], AF.Square, ["ya%d" % j], ["tB"])
            b.ts("dve", tB[:], tB[:], 0.044715, ALU.mult, ["tB"], ["tB"], s2=1.0, op1=ALU.add)
            b.tt("dve", tB[:], tB[:], ya[:, j, :], ALU.mult, ["tB", "ya%d" % j], ["tB"])
            b.act(tC[:], tB[:], AF.Sigmoid, ["tB"], ["tC"], scale=1.5957691216057308)
            b.tt("dve", ya1f[:, j, :], ya[:, j, :], tC[:], ALU.mult, ["ya%d" % j, "tC"], ["ya1f%d" % j])
            b.copy("pool", ya1b[:, j, :], ya1f[:, j, :], ["ya1f%d" % j], ["ya1b%d" % j])
        conv_ev(0); conv_ev(1)
        conv_mm(2); conv_mm(3)
        yk = ["ya1b%d" % k for k in range(4)]
        for ob in range(4):
            ps = PS[pn % NPS]; pk = "ps%d" % (pn % NPS); pn += 1
            for kb in range(4):
                b.mm(ps[:], wg[:, kb, ob * 128:(ob + 1) * 128], ya1b[:, kb, :], kb == 0, kb == 3, wgk + yk, [pk])
            b.act(tC[:], ps[:], AF.Sigmoid, [pk], ["tC"])
            b.tt("dve", cat[:, ob, :], ya1f[:, ob, :], tC[:], ALU.mult, ["ya1f%d" % ob, "tC"], ["cat%d" % ob])
        conv_ev(2); conv_ev(3)
        for j in range(4):
            b.mm(psm[:], ones512[:], accb[:, j, :], j == 0, j == 3, ["onesD", "accb%d" % j], ["psm"])
        for j in range(4):
            b.mm(pse[:], ones512[:], accsq[:, j, :], j == 0, j == 3, ["onesD", "accsq%d" % j], ["pse"])
        b.copy("act", tM[:], psm[:], ["psm"], ["tM"])
        b.act(tD[:], psm[:], AF.Square, ["psm"], ["tD"])
        b.tt("dve", tD[:], pse[:], tD[:], ALU.subtract, ["pse", "tD"], ["tD"])
        b.act(tD[:], tD[:], AF.Sqrt, ["tD", "epsb"], ["tD"], bias=b.epsb[:, 0:1])
        b.recip(tR[:], tD[:], ["tD"], ["tR"])
        for j in range(4):
            b.tt("dve", tA[:], acc[:, j, :], tM[:], ALU.subtract, ["acc%d" % j, "tM"], ["tA"])
            b.tt("dve", tA[:], tA[:], tR[:], ALU.mult, ["tA", "tR"], ["tA"])
            b.act(cat[:, 4 + j, :], tA[:], AF.Silu, ["tA", "pc"], ["cat%d" % (4 + j)],
                  bias=pc[:, 3, j:j + 1], scale=pc[:, 2, j:j + 1])
        ck = ["cat%d" % k for k in range(8)]
        for ob in range(8):
            ps = PS[pn % NPS]; pk = "ps%d" % (pn % NPS); pn += 1
            for kb in range(8):
                b.mm(ps[:], wo[:, kb, ob * 128:(ob + 1) * 128], cat[:, kb, :], kb == 0, kb == 7, wok + ck, [pk])
            b.stt("dve", hch[:, ob, :], ps[:], g1[:, ob:ob + 1], hch[:, ob, :], ALU.mult, ALU.add,
                  [pk, "g1", HK[ob]], [HK[ob]])
        b.dma(oT[:, sl].rearrange("(kb p) n -> p kb n", p=128), hch[:], r=HK)
    return b


def fm(a):
    return np.ascontiguousarray(a.T)


def host_LC(inp, mods, hT, u, vb, yA, yB):
    b = build_LC()
    vpad = np.zeros((L + 30, 512), np.float32)
    vpad[15:15 + L] = vb
    pc = np.stack([vec_pb(inp["s5_d"][0], 4), vec_pb(inp["conv_b"][0], 4), vec_pb(inp["conv_ln_g"][0], 4),
                   vec_pb(inp["conv_ln_b"][0], 4)], axis=1)
    cw = np.ascontiguousarray(inp["conv_w"][0].T.reshape(4, 128, 31).transpose(1, 0, 2))
    g1 = mod_vec(mods, 0, 2, 0)
    maps = []
    for k in range(NC):
        ts_ = slice(k * TPC, (k + 1) * TPC)
        maps.append({"hT": hT[k], "uT": fm(u[ts_]), "yAT": fm(yA[ts_]), "yBT": fm(yB[ts_]),
                     "vbp": fm(vpad[k * TPC:(k + 1) * TPC + 30]), "g1": g1, "pc": np.ascontiguousarray(pc), "cw": cw,
                     "ident": np.eye(128, dtype=np.float32),
                     "w_glu": np.ascontiguousarray(inp["s5_w_glu"][0]), "w_out": np.ascontiguousarray(inp["w_out"][0])})
    res = run(b, maps)
    return [r["oT"] for r in res]


def build_LM(final):
    b = B()
    hT = b.din("hT", [D, TPC]); pvd = b.din("pv", [128, 4, 8]); fgd = b.din("fg", [128, 8])
    w1 = b.din("w1", [D, 4 * D]); w2 = b.din("w2", [4 * D, D])
    oT = b.dout("oT", [D, TPC])
    onesD = make_consts(b, D)

    def T(name, shape, dt=F32):
        return b.sb("s_" + name, shape, dt)
    h = T("h", [128, 8, TPC]); n = T("n", [128, 8, TPC], BF16)
    pv = T("pv", [128, 4, 8]); fg = T("fg", [128, 8]); gm = T("gm", [128, 8])
    b.dma(pv[:], pvd, w=["pv"]); b.dma(fg[:], fgd, w=["fg"])
    b.ts("dve", gm[:], pv[:, 2, :], 1.0, ALU.add, ["pv"], ["gma"])
    b.tt("dve", gm[:], gm[:], pv[:, 0, :], ALU.mult, ["gma", "pv"], ["gm"])
    sq = T("sq", [128, 8, CW], BF16); rstd = T("rstd", [128, CW]); rt = T("rt", [128, CW]); hn = T("hn", [128, CW])
    pst = b.psum("pst")

    def load_h(c):
        sl = slice(c * CW, (c + 1) * CW)
        b.dma(h[:, :, sl], hT[:, sl].rearrange("(kb p) n -> p kb n", p=128), w=["h%d_%d" % (kb, c) for kb in range(8)])

    def norm_chunk(c):
        sl = slice(c * CW, (c + 1) * CW)
        rms_rstd(b, lambda kb: h[:, kb, sl], 8, CW, onesD, sq, pst, rstd, rt, ["h%d_%d" % (k, c) for k in range(8)], "M")
        for kb in range(8):
            b.tt("dve", hn[:], h[:, kb, sl], rstd[:], ALU.mult, ["h%d_%d" % (kb, c), "Mrstd"], ["hn"])
            b.act(n[:, kb, sl], hn[:], AF.Identity, ["hn", "gm", "pv"], ["n%d_%d" % (kb, c)],
                  bias=pv[:, 1, kb:kb + 1], scale=gm[:, kb:kb + 1])
    w1g = [T("w1g%d" % i, [128, 8, 512], BF16) for i in range(2)]
    w2g = [T("w2g%d" % i, [128, 4, 1024], BF16) for i in range(2)]
    stage = [T("stg%d" % i, [128, 1024]) for i in range(3)]
    rl = [T("rl%d" % i, [128, CW]) for i in range(2)]
    h1 = [T("h1_%d" % i, [128, 4, CW], BF16) for i in range(2)]
    PH = [b.psum("ph%d" % i) for i in range(3)]
    PO = [b.psum("po%d" % i) for i in range(4)]
    cnt = {"sn": 0, "hn": 0, "on": 0}

    def load_group(gi):
        par = gi % 2
        for kb in range(8):
            st = stage[cnt["sn"] % 3]; sk = "stg%d" % (cnt["sn"] % 3); cnt["sn"] += 1
            b.dma(st[:, :512], w1[kb * 128:(kb + 1) * 128, gi * 512:(gi + 1) * 512], w=[sk])
            b.copy("act" if kb % 2 == 0 else "pool", w1g[par][:, kb, :], st[:, :512], [sk], ["w1g%d_%d" % (par, kb)])
        for hb in range(4):
            st = stage[cnt["sn"] % 3]; sk = "stg%d" % (cnt["sn"] % 3); cnt["sn"] += 1
            b.dma(st[:], w2[gi * 512 + hb * 128:gi * 512 + (hb + 1) * 128, :], w=[sk])
            b.copy("act" if hb % 2 == 0 else "pool", w2g[par][:, hb, :], st[:], [sk], ["w2g%d_%d" % (par, hb)])

    def hidden(i):
        gi, c = divmod(i, NCHUNK)
        par = gi % 2; hp = i % 2
        sl = slice(c * CW, (c + 1) * CW)
        w1k = ["w1g%d_%d" % (par, k) for k in range(8)]
        for hb in range(4):
            ps = PH[cnt["hn"] % 3]; pk = "ph%d" % (cnt["hn"] % 3); cnt["hn"] += 1
            for kb in range(8):
                b.mm(ps[:], w1g[par][:, kb, hb * 128:(hb + 1) * 128], n[:, kb, sl], kb == 0, kb == 7,
                     w1k + ["n%d_%d" % (k, c) for k in range(8)], [pk])
            b.act(rl[hb % 2][:], ps[:], AF.Relu, [pk], ["rl%d" % (hb % 2)])
            b.tt("pool", h1[hp][:, hb, :], rl[hb % 2][:], rl[hb % 2][:], ALU.mult, ["rl%d" % (hb % 2)], ["h1_%d_%d" % (hp, hb)])

    def outproj(i):
        gi, c = divmod(i, NCHUNK)
        par = gi % 2; hp = i % 2
        sl = slice(c * CW, (c + 1) * CW)
        w2k = ["w2g%d_%d" % (par, k) for k in range(4)]
        h1k = ["h1_%d_%d" % (hp, k) for k in range(4)]
        for ob in range(8):
            ps = PO[cnt["on"] % 4]; pk = "po%d" % (cnt["on"] % 4); cnt["on"] += 1
            for hb in range(4):
                b.mm(ps[:], w2g[par][:, hb, ob * 128:(ob + 1) * 128], h1[hp][:, hb, :], hb == 0, hb == 3, w2k + h1k, [pk])
            b.stt("dve", h[:, ob, sl], ps[:], pv[:, 3, ob:ob + 1], h[:, ob, sl], ALU.mult, ALU.add,
                  [pk, "pv", "h%d_%d" % (ob, c)], ["h%d_%d" % (ob, c)])

    och = [T("och%d" % i, [128, 8, CW]) for i in range(2)] if final else None

    def finalize_chunk(c):
        sl = slice(c * CW, (c + 1) * CW)
        hk = ["h%d_%d" % (k, c) for k in range(8)]
        if final:
            rms_rstd(b, lambda kb: h[:, kb, sl], 8, CW, onesD, sq, pst, rstd, rt, hk, "F")
            oc = och[c % 2]
            for kb in range(8):
                b.stt("dve", oc[:, kb, :], h[:, kb, sl], fg[:, kb:kb + 1], rstd[:], ALU.mult, ALU.mult,
                      ["h%d_%d" % (kb, c), "fg", "Frstd"], ["och%d" % (c % 2)])
            b.dma(oT[:, sl].rearrange("(kb p) n -> p kb n", p=128), oc[:], r=["och%d" % (c % 2)])
        else:
            b.dma(oT[:, sl].rearrange("(kb p) n -> p kb n", p=128), h[:, :, sl], r=hk)

    NIT = 8 * NCHUNK
    load_h(0)
    load_group(0)
    for c in range(1, NCHUNK):
        load_h(c)
    norm_chunk(0)
    hidden(0)
    load_group(1)
    for c in range(1, NCHUNK):
        norm_chunk(c)
    for i in range(NIT):
        gi, c = divmod(i, NCHUNK)
        if i + 1 < NIT:
            hidden(i + 1)
        outproj(i)
        if c == NCHUNK - 1 and gi + 2 < 8:
            load_group(gi + 2)
        if gi == 7:
            finalize_chunk(c)
    return b


def host_LM(inp, mods, hT, layer):
    b = build_LM(final=(layer == 1))
    pv = np.stack([vec_pb(inp["norm_mlp_g"][layer], 8), mod_vec(mods, layer, 3, 0), mod_vec(mods, layer, 4, 0),
                   mod_vec(mods, layer, 5, 0)], axis=1)
    fg = vec_pb(inp["final_g"], 8)
    w1 = np.ascontiguousarray(inp["mlp_w1"][layer]); w2 = np.ascontiguousarray(inp["mlp_w2"][layer])
    maps = [{"hT": hT[k], "pv": np.ascontiguousarray(pv), "fg": fg, "w1": w1, "w2": w2} for k in range(NC)]
    res = run(b, maps)
    return [r["oT"] for r in res]


HALO = 8
WH = TPC + 2 * HALO
LD_CH = [(i * 512, 512) for i in range(4)] + [(2048, 16)]


def build_LD():
    b = B()
    hpT = b.din("hpT", [D, WH]); vald = b.din("valid", [128, WH]); pvd = b.din("pv", [128, 4, 8])
    ppd = b.din("pp", [128, 2, 8]); pwd = b.din("pw", [4, 256, 256])
    oT = b.dout("oT", [D, TPC])
    onesD = make_consts(b, D)

    def T(name, shape, dt=F32):
        return b.sb("s_" + name, shape, dt)
    h = T("h", [128, 8, WH]); val = T("val", [128, WH]); rstdF = T("rstdF", [128, WH])
    pv = T("pv", [128, 4, 8]); pp = T("pp", [128, 2, 8]); gm = T("gm", [128, 8]); Av = T("Av", [128, 8]); Bv = T("Bv", [128, 8])
    for kb in range(8):
        b.dma(h[:, kb, :], hpT[kb * 128:(kb + 1) * 128, :], w=["h%d" % kb])
    b.dma(val[:], vald, w=["val"]); b.dma(pv[:], pvd, w=["pv"]); b.dma(pp[:], ppd, w=["pp"])
    pwb = T("pwb", [128, 4, 2, 256], BF16)
    stage = [T("stg%d" % i, [128, 256]) for i in range(2)]
    sn = 0
    for gi in range(4):
        for kbl in range(2):
            st = stage[sn % 2]; sk = "stg%d" % (sn % 2); sn += 1
            b.dma(st[:], pwd[gi][kbl * 128:(kbl + 1) * 128, :], w=[sk])
            b.copy("act", pwb[:, gi, kbl, :], st[:], [sk], ["pwb"])
    b.ts("dve", gm[:], pv[:, 2, :], 1.0, ALU.add, ["pv"], ["gma"])
    b.tt("dve", gm[:], gm[:], pv[:, 0, :], ALU.mult, ["gma", "pv"], ["gm"])
    b.tt("dve", Av[:], pp[:, 1, :], pv[:, 3, :], ALU.mult, ["pp", "pv"], ["Av"])
    b.tt("dve", Bv[:], pp[:, 0, :], Av[:], ALU.mult, ["pp", "Av"], ["Bv"])
    sq = T("sq", [128, 8, CW], BF16); rstd = T("rstd", [128, CW]); rt = T("rt", [128, CW])
    pst = b.psum("pst")
    for (c0, n_) in LD_CH:
        sl = slice(c0, c0 + n_)
        rms_rstd(b, lambda kb: h[:, kb, sl], 8, n_, onesD, sq, pst, rstd, rt, ["h%d" % k for k in range(8)], "D")
        b.copy("pool", rstdF[:, sl], rstd[:, :n_], ["Drstd"], ["rstdF"])
    sA = T("sA", [128, WH]); sB = T("sB", [128, WH])
    nmb = [T("nmb%d" % i, [128, WH]) for i in range(2)]
    icnt = T("icnt", [128, TPC]); tW = T("tW", [128, TPC])
    pgb = T("pgb", [128, 8, TPC], BF16)

    def chain(eng, src, srck, levels):
        cur, curk, ln = src, srck, WH
        bufs = [(sA, "sA"), (sB, "sB")]
        for lv in range(levels):
            sh = 1 << lv
            dst, dstk = bufs[lv % 2]
            b.tt(eng, dst[:, 0:ln - sh], cur[:, 0:ln - sh], cur[:, sh:ln], ALU.add, [curk], [dstk])
            cur, curk, ln = dst[:], dstk, ln - sh
        return cur, curk

    for blk in range(8):
        gi = blk // 2
        w = 2 << gi
        off = HALO - w // 2
        if blk % 2 == 0:
            ct, ck = chain("pool", val[:], "val", gi + 1)
            b.recip(icnt[:], ct[:, off:off + TPC], [ck], ["icnt"])
        nb = nmb[blk % 2]; nk = "nmb%d" % (blk % 2)
        b.tt("dve", nb[:], h[:, blk, :], rstdF[:], ALU.mult, ["h%d" % blk, "rstdF"], [nk])
        b.act(nb[:], nb[:], AF.Identity, [nk, "gm", "pv"], [nk], bias=pv[:, 1, blk:blk + 1], scale=gm[:, blk:blk + 1])
        b.tt("pool", nb[:], nb[:], val[:], ALU.mult, [nk, "val"], [nk])
        st, sk = chain("dve", nb[:], nk, gi + 1)
        b.tt("dve", tW[:], st[:, off:off + TPC], icnt[:], ALU.mult, [sk, "icnt"], ["tW"])
        b.tt("dve", pgb[:, blk, :], tW[:], nb[:, HALO:HALO + TPC], ALU.subtract, ["tW", nk], ["pgb%d" % blk])
    PS = [b.psum("ps%d" % i) for i in range(4)]
    yt = [T("yt%d" % i, [128, CW]) for i in range(2)]
    pn = 0
    for c in range(NCHUNK):
        sl = slice(c * CW, (c + 1) * CW)
        slh = slice(HALO + c * CW, HALO + (c + 1) * CW)
        for gi in range(4):
            for obl in range(2):
                ob = 2 * gi + obl
                ps = PS[pn % 4]; pk = "ps%d" % (pn % 4); y_ = yt[pn % 2]; ykk = "yt%d" % (pn % 2); pn += 1
                for kbl in range(2):
                    b.mm(ps[:], pwb[:, gi, kbl, obl * 128:(obl + 1) * 128], pgb[:, 2 * gi + kbl, sl], kbl == 0, kbl == 1,
                         ["pwb", "pgb%d" % (2 * gi), "pgb%d" % (2 * gi + 1)], [pk])
                b.act(y_[:], ps[:], AF.Identity, [pk, "Av", "Bv"], [ykk], bias=Bv[:, ob:ob + 1], scale=Av[:, ob:ob + 1])
                b.tt("dve", h[:, ob, slh], h[:, ob, slh], y_[:], ALU.add, ["h%d" % ob, ykk], ["h%d" % ob, "hf%d_%d" % (ob, c)])
        b.dma(oT[:, sl].rearrange("(kb p) n -> p kb n", p=128), h[:, :, slh], r=["hf%d_%d" % (k, c) for k in range(8)])
    return b


def host_LD(inp, mods, hT):
    b = build_LD()
    hall = np.concatenate(hT, axis=1)
    hpad = np.zeros((D, L + 2 * HALO), np.float32)
    hpad[:, HALO:HALO + L] = hall
    vfull = np.zeros((L + 2 * HALO,), np.float32)
    vfull[HALO:HALO + L] = 1.0
    pv = np.stack([vec_pb(inp["norm_mix_g"][1], 8), mod_vec(mods, 1, 0, 0), mod_vec(mods, 1, 1, 0), mod_vec(mods, 1, 2, 0)], axis=1)
    pp = np.stack([vec_pb(inp["pool_b"][0].reshape(-1), 8), vec_pb(inp["pool_scale"][0], 8)], axis=1)
    pw = np.ascontiguousarray(inp["pool_w"][0])
    maps = []
    for k in range(NC):
        maps.append({"hpT": np.ascontiguousarray(hpad[:, k * TPC:k * TPC + WH]),
                     "valid": np.ascontiguousarray(np.broadcast_to(vfull[k * TPC:k * TPC + WH], (128, WH))),
                     "pv": np.ascontiguousarray(pv), "pp": np.ascontiguousarray(pp), "pw": pw})
    res = run(b, maps)
    return [r["oT"] for r in res]


def kernel(**inp):
    inp = {k: np.asarray(v) for k, v in inp.items()}
    mods = host_L0(inp)
    hT, u, vb, uc = host_LA(inp, mods)
    yA, yB = host_LB(inp, u, uc)
    h1T = host_LC(inp, mods, hT, u, vb, yA, yB)
    h2T = host_LM(inp, mods, h1T, 0)
    h3T = host_LD(inp, mods, h2T)
    oT = host_LM(inp, mods, h3T, 1)
    out = np.concatenate([o.T for o in oT], axis=0)
    return np.ascontiguousarray(out[None]).astype(np.float32, copy=False)
```

```python
import math
import numpy as np
import concourse.bass as bass
import concourse.mybir as mybir
from concourse.bass_utils import run_bass_kernel_spmd

F32 = mybir.dt.float32
BF16 = mybir.dt.bfloat16
I32 = mybir.dt.int32
AF = mybir.ActivationFunctionType
ALU = mybir.AluOpType
AX = mybir.AxisListType

NC = 8
D = 1024
L = 16384
TPC = L // NC
CW = 512
NCHUNK = TPC // CW
CTX = 256
EPS = 1e-6
PI = math.pi
TWO_PI = 2 * math.pi
C1 = 6.28125
C2 = 2 * math.pi - 6.28125


class _Op:
    __slots__ = ("eng", "fn", "deps", "signal", "idx", "dma", "dsem", "dcount", "n")


class Sched:
    N_DMA_SEMS = 48

    def __init__(self, nc):
        self.nc = nc
        self.ops = []
        self.last_w = {}
        self.readers = {}
        self.E = {"pe": nc.tensor, "dve": nc.vector, "act": nc.scalar,
                  "pool": nc.gpsimd, "sp": nc.sync}

    def add(self, eng, fn, reads=(), writes=(), dma=False):
        op = _Op()
        op.eng = eng
        op.fn = fn
        op.dma = dma
        op.signal = False
        op.idx = None
        op.n = len(self.ops)
        deps = {}
        for r in reads:
            w = self.last_w.get(r)
            if w is not None:
                deps[w.n] = w
        for r in writes:
            w = self.last_w.get(r)
            if w is not None:
                deps[w.n] = w
            rd = self.readers.get(r)
            if rd:
                for lst in rd.values():
                    for o in lst:
                        deps[o.n] = o
        for r in reads:
            rd = self.readers.setdefault(r, {})
            if dma:
                rd.setdefault("dma", []).append(op)
            else:
                rd[eng] = [op]
        for r in writes:
            self.last_w[r] = op
            self.readers[r] = {}
        dl = []
        for d in deps.values():
            if d is op:
                continue
            if (not d.dma) and (not dma) and d.eng == eng and eng == "pe":
                continue
            d.signal = True
            dl.append(d)
        op.deps = dl
        self.ops.append(op)
        return op

    def emit(self):
        nc = self.nc
        esem = {e: nc.alloc_semaphore("sem_" + e) for e in self.E}
        ecount = {e: 0 for e in self.E}
        dsems = [nc.alloc_semaphore("dsem%d" % i) for i in range(self.N_DMA_SEMS)]
        dcum = [0] * self.N_DMA_SEMS
        dnext = 0
        waited = {e: {} for e in self.E}

        def do_wait(eng, key, sem, val):
            w = waited[eng]
            if w.get(key, 0) >= val:
                return
            w[key] = val
            self.E[eng].wait_ge(sem, val)

        for op in self.ops:
            eng = op.eng
            for d in op.deps:
                if d.dma:
                    do_wait(eng, ("d", d.dsem), dsems[d.dsem], d.dcount)
                else:
                    do_wait(eng, ("e", d.eng), esem[d.eng], d.idx)
            if op.dma:
                s = dnext
                dnext = (dnext + 1) % self.N_DMA_SEMS
                if dcum[s] > 0:
                    do_wait(eng, ("d", s), dsems[s], dcum[s])
                ins = op.fn()
                ins.then_inc(dsems[s], 16)
                dcum[s] += 16
                op.dsem = s
                op.dcount = dcum[s]
            else:
                ins = op.fn()
                if op.signal:
                    ecount[eng] += 1
                    op.idx = ecount[eng]
                    ins.then_inc(esem[eng], 1)
        for s, c in zip(dsems, dcum):
            if c:
                nc.sync.wait_ge(s, c)


class B:
    def __init__(self):
        self.nc = bass.Bass("TRN2", target_bir_lowering=False)
        self.S = Sched(self.nc)
        self.npsum = 0
        self.ins = {}
        self.outs = {}

    def din(self, name, shape, dt=F32):
        ap = self.nc.dram_tensor(name, list(shape), dt, kind="ExternalInput").ap()
        self.ins[name] = ap
        return ap

    def dout(self, name, shape, dt=F32):
        ap = self.nc.dram_tensor(name, list(shape), dt, kind="ExternalOutput").ap()
        self.outs[name] = ap
        return ap

    def sb(self, name, shape, dt=F32):
        return self.nc.alloc_sbuf_tensor(name, list(shape), dt)

    def psum(self, name, shape=(128, 512), dt=F32):
        return self.nc.alloc_psum_tensor(name, list(shape), dt)

    def dma(self, out, in_, r=(), w=(), **kw):
        nc = self.nc
        self.S.add("sp", lambda: nc.sync.dma_start(out=out, in_=in_, **kw), r, w, dma=True)

    def act(self, out, in_, func, r, w, bias=None, scale=None):
        nc = self.nc
        kw = {}
        if bias is not None:
            kw["bias"] = bias
        if scale is not None:
            kw["scale"] = scale
        self.S.add("act", lambda: nc.scalar.activation(out=out, in_=in_, func=func, **kw), r, w)

    def tt(self, eng, out, in0, in1, op, r, w):
        e = self.S.E[eng]
        self.S.add(eng, lambda: e.tensor_tensor(out=out, in0=in0, in1=in1, op=op), r, w)

    def ts(self, eng, out, in0, s1, op0, r, w, s2=None, op1=None):
        e = self.S.E[eng]
        if op1 is None:
            self.S.add(eng, lambda: e.tensor_scalar(out=out, in0=in0, scalar1=s1, scalar2=None, op0=op0), r, w)
        else:
            self.S.add(eng, lambda: e.tensor_scalar(out=out, in0=in0, scalar1=s1, scalar2=s2, op0=op0, op1=op1), r, w)

    def stt(self, eng, out, in0, scalar, in1, op0, op1, r, w):
        e = self.S.E[eng]
        self.S.add(eng, lambda: e.scalar_tensor_tensor(out=out, in0=in0, scalar=scalar, in1=in1, op0=op0, op1=op1), r, w)

    def copy(self, eng, out, in_, r, w):
        if eng == "act":
            nc = self.nc
            self.S.add("act", lambda: nc.scalar.copy(out=out, in_=in_), r, w)
        else:
            e = self.S.E[eng]
            self.S.add(eng, lambda: e.tensor_copy(out=out, in_=in_), r, w)

    def memset(self, eng, ap, val, w):
        e = self.S.E[eng]
        self.S.add(eng, lambda: e.memset(ap, val), (), w)

    def recip(self, out, in_, r, w):
        nc = self.nc
        self.S.add("dve", lambda: nc.vector.reciprocal(out=out, in_=in_), r, w)

    def mm(self, out, lhsT, rhs, start, stop, r, w):
        nc = self.nc
        self.S.add("pe", lambda: nc.tensor.matmul(out, lhsT=lhsT, rhs=rhs, start=start, stop=stop), r, w)

    def transpose(self, out, in_, ident, r, w):
        nc = self.nc
        self.S.add("pe", lambda: nc.tensor.transpose(out=out, in_=in_, identity=ident), r, w)

    def scan(self, out, d0, d1, init, r, w):
        nc = self.nc
        self.S.add("dve", lambda: nc.vector.tensor_tensor_scan(out=out, data0=d0, data1=d1, initial=init,
                                                             op0=ALU.mult, op1=ALU.add), r, w)

    def iota(self, out, pattern, base, cm, w):
        nc = self.nc
        self.S.add("pool", lambda: nc.gpsimd.iota(out, pattern=pattern, base=base, channel_multiplier=cm), (), w)

    def finish(self):
        self.S.emit()
        return self.nc

    def sin_of(self, out, ang, shape, tmp, r, w):
        ki, kf, ra, rb = tmp["ki"], tmp["kf"], tmp["ra"], tmp["rb"]
        tk = tmp["key"]
        self.ts("dve", ki, ang, 1.0 / TWO_PI, ALU.mult, r, [tk + "ki"])
        self.copy("dve", kf, ki, [tk + "ki"], [tk + "kf"])
        self.stt("dve", ra, kf, -C1, ang, ALU.mult, ALU.add, r + [tk + "kf"], [tk + "ra"])
        self.stt("dve", rb, kf, -C2, ra, ALU.mult, ALU.add, [tk + "kf", tk + "ra"], [tk + "rb"])
        self.ts("dve", kf, rb, PI, ALU.is_gt, [tk + "rb"], [tk + "kf"], s2=-TWO_PI, op1=ALU.mult)
        self.tt("dve", ra, rb, kf, ALU.add, [tk + "rb", tk + "kf"], [tk + "ra"])
        self.ts("dve", kf, ra, -PI, ALU.is_lt, [tk + "ra"], [tk + "kf"], s2=TWO_PI, op1=ALU.mult)
        self.tt("dve", rb, ra, kf, ALU.add, [tk + "ra", tk + "kf"], [tk + "rb"])
        self.ts("dve", ra, rb, -PI, ALU.max, [tk + "rb"], [tk + "ra"], s2=PI, op1=ALU.min)
        self.act(out, ra, AF.Sin, [tk + "ra"], w)


def run(b, in_maps):
    b.finish()
    res = run_bass_kernel_spmd(b.nc, in_maps, core_ids=list(range(NC)))
    return res.results


def vec_pb(v, nblk):
    return np.ascontiguousarray(np.asarray(v, np.float32).reshape(nblk, 128).T)


def build_L0():
    b = B()
    w = b.din("w", [D, 1536])
    bb = b.din("b", [128, 12])
    cc = b.din("cc", [128, 8, 2])
    o = b.dout("o", [128, 12, 2])
    wsb = b.sb("wsb", [128, 8, 1536])
    bsb = b.sb("bsb", [128, 12])
    ccs = b.sb("ccs", [128, 8, 2])
    sc = b.sb("sc", [128, 8, 2])
    osb = b.sb("osb", [128, 12, 2])
    ps = b.psum("ps", [128, 12, 2])
    for kb in range(8):
        b.dma(wsb[:, kb, :], w[kb * 128:(kb + 1) * 128, :], w=["w%d" % kb])
    b.dma(bsb[:], bb, w=["bsb"])
    b.dma(ccs[:], cc, w=["ccs"])
    b.act(sc[:], ccs[:], AF.Silu, ["ccs"], ["sc"])
    for cb in range(12):
        for kb in range(8):
            b.mm(ps[:, cb, :], wsb[:, kb, cb * 128:(cb + 1) * 128], sc[:, kb, :], kb == 0, kb == 7,
                 ["w%d" % kb, "sc"], ["ps"])
    b.tt("dve", osb[:], ps[:], bsb[:].unsqueeze(2).to_broadcast([128, 12, 2]), ALU.add, ["ps", "bsb"], ["osb"])
    b.dma(o, osb[:], r=["osb"])
    return b


def host_L0(inp):
    b = build_L0()
    w_ada = inp["w_ada"]
    b_ada = inp["b_ada"]
    cc = np.stack([vec_pb(inp["c"][0], 8), vec_pb(inp["c_ctx"], 8)], axis=-1)
    maps = []
    for k in range(NC):
        i, q = k // 4, k % 4
        maps.append({"w": np.ascontiguousarray(w_ada[i][:, q * 1536:(q + 1) * 1536]),
                     "b": vec_pb(b_ada[i][q * 1536:(q + 1) * 1536], 12), "cc": cc})
    res = run(b, maps)
    mods = np.zeros((2, 6144, 2), np.float32)
    for k in range(NC):
        i, q = k // 4, k % 4
        o = res[k]["o"]
        mods[i, q * 1536:(q + 1) * 1536, :] = o.transpose(1, 0, 2).reshape(1536, 2)
    return mods


def mod_vec(mods, layer, which, j):
    return vec_pb(mods[layer, which * D:(which + 1) * D, j], 8)


def rms_rstd(b, h_ap_fn, nblk, n, onesD, sq, pst, rstd, tmp, rkeys, tag, inv_dim_in_ones=True):
    for kb in range(nblk):
        b.act(sq[:, kb, :n], h_ap_fn(kb), AF.Square, rkeys, [tag + "sq%d" % kb])
    for kb in range(nblk):
        b.mm(pst[:, :n], onesD[:], sq[:, kb, :n], kb == 0, kb == nblk - 1, [tag + "sq%d" % kb, "onesD"], [tag + "pst"])
    b.act(tmp[:, :n], pst[:, :n], AF.Sqrt, [tag + "pst", "epsb"], [tag + "tmp"], bias=b.epsb[:, 0:1])
    b.recip(rstd[:, :n], tmp[:, :n], [tag + "tmp"], [tag + "rstd"])


def make_consts(b, dim):
    onesD = b.sb("onesD", [128, 128], BF16)
    b.memset("pool", onesD[:], 1.0 / dim, ["onesD"])
    epsb = b.sb("epsb", [128, 1])
    b.memset("pool", epsb[:], EPS, ["epsb"])
    b.epsb = epsb
    return onesD


def load_weight_bf16(b, dst, src, nkb, ncols, stage, tag, eng_cycle=("act", "pool")):
    for kb in range(nkb):
        st = stage[kb % len(stage)]
        sk = tag + "st%d" % (kb % len(stage))
        b.dma(st[:, :ncols], src[kb * 128:(kb + 1) * 128, :], w=[sk])
        b.copy(eng_cycle[kb % len(eng_cycle)], dst[:, kb, :], st[:, :ncols], [sk], [tag + "w%d" % kb])


def build_LA():
    b = B()
    xT = b.din("xT", [D, TPC])
    ridx = b.din("ridx", [128, 32])
    cidx = b.din("cidx", [128, 64])
    cxT = b.din("cxT", [D, 32])
    pv = b.din("pv", [128, 5, 8])
    w_in = b.din("w_in", [D, 1536])
    hT = b.dout("hT", [D, TPC])
    uT = b.dout("uT", [512, TPC])
    vbT = b.dout("vbT", [512, TPC])
    ucT = b.dout("ucT", [512, 32])

    onesD = make_consts(b, D)
    h = b.sb("h", [128, 8, TPC])
    ri = b.sb("ri", [128, 32])
    ci = b.sb("ci", [128, 64])
    hc = b.sb("hc", [128, 8, 32])
    pvs = b.sb("pvs", [128, 5, 8])
    gm = b.sb("gm", [128, 2, 8])
    win = b.sb("win", [128, 8, 1536], BF16)
    stage = [b.sb("stg%d" % i, [128, 1536]) for i in range(2)]
    for kb in range(8):
        b.dma(h[:, kb, :], xT[kb * 128:(kb + 1) * 128, :], w=["h%d" % kb])
    b.dma(ri[:], ridx, w=["ri"])
    b.dma(ci[:], cidx, w=["ci"])
    b.dma(hc[:], cxT.rearrange("(kb p) n -> p kb n", p=128), w=["hc"])
    b.dma(pvs[:], pv, w=["pvs"])
    load_weight_bf16(b, win, w_in, 8, 1536, stage, "win")
    b.ts("dve", gm[:, 0, :], pvs[:, 2, :], 1.0, ALU.add, ["pvs"], ["gm0a"])
    b.tt("dve", gm[:, 0, :], gm[:, 0, :], pvs[:, 0, :], ALU.mult, ["gm0a", "pvs"], ["gm0"])
    b.ts("dve", gm[:, 1, :], pvs[:, 4, :], 1.0, ALU.add, ["pvs"], ["gm1a"])
    b.tt("dve", gm[:, 1, :], gm[:, 1, :], pvs[:, 0, :], ALU.mult, ["gm1a", "pvs"], ["gm1"])
    ki0 = b.sb("ki0", [128, 2], I32)
    kf0 = b.sb("kf0", [128, 2])
    om = b.sb("om", [128, 2])
    b.iota(ki0[:], [[128, 2]], 0, 1, ["ki0"])
    b.copy("dve", kf0[:], ki0[:], ["ki0"], ["kf0"])
    b.act(om[:], kf0[:], AF.Exp, ["kf0"], ["om"], scale=-math.log(10000.0) / 256.0)

    tki = b.sb("t_ki", [128, 64], I32); tkf = b.sb("t_kf", [128, 64]); tra = b.sb("t_ra", [128, 64]); trb = b.sb("t_rb", [128, 64])
    ang = b.sb("ang", [128, 64])
    rowtab = b.sb("rowtab", [128, 4, 32])
    coltab = b.sb("coltab", [128, 4, 64])
    for blk in range(4):
        j = blk % 2
        ph = PI / 2 if blk >= 2 else 0.0
        for (idx, ik, n_, tab, tk) in ((ri, "ri", 32, rowtab, "rowtab"), (ci, "ci", 64, coltab, "coltab")):
            tmp = {"ki": tki[:, :n_], "kf": tkf[:, :n_], "ra": tra[:, :n_], "rb": trb[:, :n_], "key": "t_"}
            b.ts("dve", ang[:, :n_], idx[:], om[:, j:j + 1], ALU.mult, [ik, "om"], ["ang"], s2=ph, op1=ALU.add)
            b.sin_of(tab[:, blk, :], ang[:, :n_], None, tmp, ["ang"], [tk])
    sq = b.sb("sq", [128, 8, CW], BF16)
    pst = b.psum("pst")
    rstd = b.sb("rstd", [128, CW])
    rt = b.sb("rt", [128, CW])
    hn = b.sb("hn", [128, CW])
    n = b.sb("n", [128, 8, CW], BF16)
    PS = [b.psum("ps%d" % i) for i in range(4)]
    uo = b.sb("uo", [128, 4, CW])
    vo = b.sb("vo", [128, 4, CW])
    sig = b.sb("sig", [128, CW])
    psn = 0

    for c in range(NCHUNK):
        sl = slice(c * CW, (c + 1) * CW)
        b.tt("pool", h[:, 0:4, sl].rearrange("p b (r c) -> p b r c", c=64), h[:, 0:4, sl].rearrange("p b (r c) -> p b r c", c=64),
             rowtab[:, :, 8 * c:8 * c + 8].unsqueeze(3).to_broadcast([128, 4, 8, 64]), ALU.add,
             ["h0", "h1", "h2", "h3", "rowtab"], ["h0", "h1", "h2", "h3"] + ["hf%d_%d" % (k, c) for k in range(4)])
        b.tt("dve", h[:, 4:8, sl].rearrange("p b (r c) -> p b r c", c=64), h[:, 4:8, sl].rearrange("p b (r c) -> p b r c", c=64),
             coltab[:].unsqueeze(2).to_broadcast([128, 4, 8, 64]), ALU.add,
             ["h4", "h5", "h6", "h7", "coltab"], ["h4", "h5", "h6", "h7"] + ["hf%d_%d" % (k, c) for k in range(4, 8)])
        b.dma(hT[:, sl].rearrange("(kb p) n -> p kb n", p=128), h[:, :, sl], r=["hf%d_%d" % (k, c) for k in range(8)])
        rms_rstd(b, lambda kb: h[:, kb, sl], 8, CW, onesD, sq, pst, rstd, rt, ["h%d" % k for k in range(8)], "A")
        for kb in range(8):
            b.tt("dve", hn[:], h[:, kb, sl], rstd[:], ALU.mult, ["h%d" % kb, "Arstd"], ["hn"])
            b.act(n[:, kb, :], hn[:], AF.Identity, ["hn", "gm0", "pvs"], ["n%d" % kb],
                  bias=pvs[:, 1, kb:kb + 1], scale=gm[:, 0, kb:kb + 1])
        nkeys = ["n%d" % k for k in range(8)]
        wkeys = ["winw%d" % k for k in range(8)]
        for ob in range(4):
            ps = PS[psn % 4]; pk = "ps%d" % (psn % 4); psn += 1
            for kb in range(8):
                b.mm(ps[:], win[:, kb, ob * 128:(ob + 1) * 128], n[:, kb, :], kb == 0, kb == 7, nkeys + wkeys, [pk])
            b.copy("act", uo[:, ob, :], ps[:], [pk], ["uo"])
        b.dma(uT[:, sl].rearrange("(ob p) n -> p ob n", p=128), uo[:], r=["uo"])
        for jv in range(4):
            psg = PS[psn % 4]; pkg = "ps%d" % (psn % 4); psn += 1
            for kb in range(8):
                b.mm(psg[:], win[:, kb, (8 + jv) * 128:(9 + jv) * 128], n[:, kb, :], kb == 0, kb == 7, nkeys + wkeys, [pkg])
            b.act(sig[:], psg[:], AF.Sigmoid, [pkg], ["sig"])
            psv = PS[psn % 4]; pkv = "ps%d" % (psn % 4); psn += 1
            for kb in range(8):
                b.mm(psv[:], win[:, kb, (4 + jv) * 128:(5 + jv) * 128], n[:, kb, :], kb == 0, kb == 7, nkeys + wkeys, [pkv])
            b.tt("dve", vo[:, jv, :], psv[:], sig[:], ALU.mult, [pkv, "sig"], ["vo"])
        b.dma(vbT[:, sl].rearrange("(ob p) n -> p ob n", p=128), vo[:], r=["vo"])
    rms_rstd(b, lambda kb: hc[:, kb, :], 8, 32, onesD, sq, pst, rstd, rt, ["hc"], "C")
    for kb in range(8):
        b.tt("dve", hn[:, :32], hc[:, kb, :], rstd[:, :32], ALU.mult, ["hc", "Crstd"], ["hn"])
        b.act(n[:, kb, :32], hn[:, :32], AF.Identity, ["hn", "gm1", "pvs"], ["n%d" % kb],
              bias=pvs[:, 3, kb:kb + 1], scale=gm[:, 1, kb:kb + 1])
    for ob in range(4):
        ps = PS[psn % 4]; pk = "ps%d" % (psn % 4); psn += 1
        for kb in range(8):
            b.mm(ps[:, :32], win[:, kb, ob * 128:(ob + 1) * 128], n[:, kb, :32], kb == 0, kb == 7,
                 ["n%d" % k for k in range(8)] + ["winw%d" % k for k in range(8)], [pk])
        b.copy("act", uo[:, ob, :32], ps[:, :32], [pk], ["uo"])
    b.dma(ucT.rearrange("(ob p) n -> p ob n", p=128), uo[:, :, :32], r=["uo"])
    return b


def host_LA(inp, mods):
    b = build_LA()
    x = inp["x"][0]
    ctx = inp["ctx"][0]
    pv = np.stack([vec_pb(inp["norm_mix_g"][0], 8), mod_vec(mods, 0, 0, 0), mod_vec(mods, 0, 1, 0),
                   mod_vec(mods, 0, 0, 1), mod_vec(mods, 0, 1, 1)], axis=1)
    w_in = np.ascontiguousarray(inp["w_in"][0])
    tok = np.arange(L)
    maps = []
    for k in range(NC):
        t = tok[k * TPC:(k + 1) * TPC]
        maps.append({
            "xT": np.ascontiguousarray(x[k * TPC:(k + 1) * TPC].T),
            "ridx": np.ascontiguousarray(np.broadcast_to((t[::64] // 64).astype(np.float32), (128, 32))),
            "cidx": np.ascontiguousarray(np.broadcast_to(np.arange(64, dtype=np.float32), (128, 64))),
            "cxT": np.ascontiguousarray(ctx[k * 32:(k + 1) * 32].T),
            "pv": np.ascontiguousarray(pv), "w_in": w_in})
    res = run(b, maps)
    hT = [r["hT"] for r in res]
    u = np.concatenate([r["uT"].T for r in res], axis=0)
    vb = np.concatenate([r["vbT"].T for r in res], axis=0)
    uc = np.concatenate([r["ucT"].T for r in res], axis=0)
    return hT, u, vb, uc


NSS = (CTX + L) // 8
NXS = L // 8
LB_CH = [(0, 32)] + [(32 + i * 512, 512) for i in range(4)]


def reduce_ang(b, out, ang, tmp, r, w):
    ki, kf, rb = tmp["ki"], tmp["kf"], tmp["rb"]
    tk = tmp["key"]
    ra = out
    b.ts("dve", ki, ang, 1.0 / TWO_PI, ALU.mult, r, [tk + "ki"])
    b.copy("dve", kf, ki, [tk + "ki"], [tk + "kf"])
    b.stt("dve", ra, kf, -C1, ang, ALU.mult, ALU.add, r + [tk + "kf"], w)
    b.stt("dve", rb, kf, -C2, ra, ALU.mult, ALU.add, [tk + "kf"] + w, [tk + "rb"])
    b.ts("dve", kf, rb, PI, ALU.is_gt, [tk + "rb"], [tk + "kf"], s2=-TWO_PI, op1=ALU.mult)
    b.tt("dve", ra, rb, kf, ALU.add, [tk + "rb", tk + "kf"], w)
    b.ts("dve", kf, ra, -PI, ALU.is_lt, w, [tk + "kf"], s2=TWO_PI, op1=ALU.mult)
    b.tt("dve", rb, ra, kf, ALU.add, w + [tk + "kf"], [tk + "rb"])
    b.ts("dve", ra, rb, -PI, ALU.max, [tk + "rb"], w, s2=PI, op1=ALU.min)


def build_LB():
    b = B()
    U = b.din("U", [8, 128, NSS])
    p_lre = b.din("lamre", [128, 8]); p_lim = b.din("lamim", [128, 8]); p_ls = b.din("lstep", [128, 8])
    p_bre = b.din("bre", [128, 8, 16]); p_bim = b.din("bim", [128, 8, 16])
    p_cre = b.din("cre", [128, 8, 16]); p_cim = b.din("cim", [128, 8, 16])
    p_mF = b.din("maskF", [128, 128]); p_mB = b.din("maskB", [128, 128]); p_id = b.din("ident", [128, 128])
    Y = b.dout("Y", [8, 128, NXS])

    def T(name, shape, dt=F32):
        return b.sb("s_" + name, shape, dt)

    lre = T("lre", [128, 8]); lim = T("lim", [128, 8]); ls = T("ls", [128, 8])
    bre = T("bre", [128, 8, 16]); bim = T("bim", [128, 8, 16]); cre = T("cre", [128, 8, 16]); cim = T("cim", [128, 8, 16])
    mF = T("mF", [128, 128]); mB = T("mB", [128, 128]); ident = T("ident", [128, 128])
    for t, src, k in [(lre, p_lre, "lre"), (lim, p_lim, "lim"), (ls, p_ls, "ls"), (bre, p_bre, "bre"), (bim, p_bim, "bim"),
                      (cre, p_cre, "cre"), (cim, p_cim, "cim"), (mF, p_mF, "mF"), (mB, p_mB, "mB"), (ident, p_id, "ident")]:
        b.dma(t[:], src, w=[k])
    dt_ = T("dt", [128, 8]); lr = T("lr", [128, 8]); a = T("a", [128, 8]); th = T("th", [128, 8])
    b.act(dt_[:], ls[:], AF.Exp, ["ls"], ["dt"])
    b.ts("dve", lr[:], lre[:], -1e-4, ALU.min, ["lre"], ["lr"])
    b.tt("dve", a[:], lr[:], dt_[:], ALU.mult, ["lr", "dt"], ["a"])
    b.tt("dve", th[:], lim[:], dt_[:], ALU.mult, ["lim", "dt"], ["th"])
    kiA = T("kiA", [128, 16], I32); kiD = T("kiD", [128, 16], I32); kA = T("kA", [128, 16]); kD = T("kD", [128, 16])
    b.iota(kiA[:], [[1, 16]], -7, 0, ["kiA"]); b.iota(kiD[:], [[-1, 16]], 8, 0, ["kiD"])
    b.copy("dve", kA[:], kiA[:], ["kiA"], ["kA"]); b.copy("dve", kD[:], kiD[:], ["kiD"], ["kD"])
    S3 = [128, 8, 16]
    tmp3 = {"ki": T("p_ki", S3, I32)[:], "kf": T("p_kf", S3)[:], "ra": T("p_ra", S3)[:], "rb": T("p_rb", S3)[:], "key": "p_"}
    ak = T("ak", S3); tk_ = T("tk", S3); mag = T("mag", S3); sn = T("sn", S3); cs = T("cs", S3)
    PW = {}
    for nm, kv in (("A", kA), ("D", kD)):
        kb_ = kv[:].unsqueeze(1).to_broadcast(S3)
        b.tt("dve", ak[:], a[:].unsqueeze(2).to_broadcast(S3), kb_, ALU.mult, ["a", "k" + nm], ["ak"])
        b.act(mag[:], ak[:], AF.Exp, ["ak"], ["mag"])
        b.tt("dve", tk_[:], th[:].unsqueeze(2).to_broadcast(S3), kb_, ALU.mult, ["th", "k" + nm], ["tk"])
        b.sin_of(sn[:], tk_[:], S3, tmp3, ["tk"], ["sn"])
        b.ts("dve", tk_[:], tk_[:], PI / 2, ALU.add, ["tk"], ["tk"])
        b.sin_of(cs[:], tk_[:], S3, tmp3, ["tk"], ["cs"])
        pr = T("PWr" + nm, S3); pi_ = T("PWi" + nm, S3)
        b.tt("dve", pr[:], mag[:], cs[:], ALU.mult, ["mag", "cs"], ["PWr" + nm])
        b.tt("dve", pi_[:], mag[:], sn[:], ALU.mult, ["mag", "sn"], ["PWi" + nm])
        PW[nm] = (pr, pi_, "PWr" + nm, "PWi" + nm)
    l1r = PW["A"][0][:, :, 8]; l1i = PW["A"][1][:, :, 8]
    nr = T("nr", [128, 8]); t1 = T("t1", [128, 8]); t2 = T("t2", [128, 8]); rden = T("rden", [128, 8])
    wr = T("wr", [128, 8]); wi = T("wi", [128, 8])
    b.ts("dve", nr[:], l1r, -1.0, ALU.add, ["PWrA"], ["nr"])
    b.tt("dve", t1[:], lr[:], lr[:], ALU.mult, ["lr"], ["t1"])
    b.tt("dve", t2[:], lim[:], lim[:], ALU.mult, ["lim"], ["t2"])
    b.tt("dve", t1[:], t1[:], t2[:], ALU.add, ["t1", "t2"], ["t1"])
    b.recip(rden[:], t1[:], ["t1"], ["rden"])
    b.tt("dve", t1[:], nr[:], lr[:], ALU.mult, ["nr", "lr"], ["t1"])
    b.tt("dve", t2[:], l1i, lim[:], ALU.mult, ["PWiA", "lim"], ["t2"])
    b.tt("dve", t1[:], t1[:], t2[:], ALU.add, ["t1", "t2"], ["t1"])
    b.tt("dve", wr[:], t1[:], rden[:], ALU.mult, ["t1", "rden"], ["wr"])
    b.tt("dve", t1[:], l1i, lr[:], ALU.mult, ["PWiA", "lr"], ["t1"])
    b.tt("dve", t2[:], nr[:], lim[:], ALU.mult, ["nr", "lim"], ["t2"])
    b.tt("dve", t1[:], t1[:], t2[:], ALU.subtract, ["t1", "t2"], ["t1"])
    b.tt("dve", wi[:], t1[:], rden[:], ALU.mult, ["t1", "rden"], ["wi"])
    bbr = T("bbr", S3); bbi = T("bbi", S3); t3 = T("t3", S3); t4 = T("t4", S3)
    wrb = wr[:].unsqueeze(2).to_broadcast(S3); wib = wi[:].unsqueeze(2).to_broadcast(S3)
    b.tt("dve", t3[:], wrb, bre[:], ALU.mult, ["wr", "bre"], ["t3"])
    b.tt("dve", t4[:], wib, bim[:], ALU.mult, ["wi", "bim"], ["t4"])
    b.tt("dve", bbr[:], t3[:], t4[:], ALU.subtract, ["t3", "t4"], ["bbr"])
    b.tt("dve", t3[:], wrb, bim[:], ALU.mult, ["wr", "bim"], ["t3"])
    b.tt("dve", t4[:], wib, bre[:], ALU.mult, ["wi", "bre"], ["t4"])
    b.tt("dve", bbi[:], t3[:], t4[:], ALU.add, ["t3", "t4"], ["bbi"])

    S4 = [128, 4, 8, 16]
    o1 = T("o1", S4); o2 = T("o2", S4); oR = T("oR", S4); oI = T("oI", S4)

    def cplx_outer(nm, ksl, d, Vr, Vi, vkeys):
        pr, pi_, kr, ki_ = PW[nm]
        dsl = slice(4 * d, 4 * d + 4)
        Pr = pr[:, dsl, ksl].unsqueeze(3).to_broadcast(S4); Pi = pi_[:, dsl, ksl].unsqueeze(3).to_broadcast(S4)
        vr = Vr[:, dsl, :].unsqueeze(2).to_broadcast(S4); vi = Vi[:, dsl, :].unsqueeze(2).to_broadcast(S4)
        b.tt("dve", o1[:], Pr, vr, ALU.mult, [kr, vkeys[0]], ["o1"])
        b.tt("dve", o2[:], Pi, vi, ALU.mult, [ki_, vkeys[1]], ["o2"])
        b.tt("dve", oR[:], o1[:], o2[:], ALU.subtract, ["o1", "o2"], ["oR"])
        b.tt("dve", o1[:], Pr, vi, ALU.mult, [kr, vkeys[1]], ["o1"])
        b.tt("dve", o2[:], Pi, vr, ALU.mult, [ki_, vkeys[0]], ["o2"])
        b.tt("dve", oI[:], o1[:], o2[:], ALU.add, ["o1", "o2"], ["oI"])

    def halves(dst, top, bot, neg_top, neg_bot, rk, wk):
        for (lo, hi, src, neg) in ((0, 64, top, neg_top), (64, 128, bot, neg_bot)):
            if neg:
                b.ts("dve", dst[lo:hi], src[lo:hi], -1.0, ALU.mult, rk, [wk])
            else:
                b.copy("dve", dst[lo:hi], src[lo:hi], rk, [wk])

    S4m = [128, 4, 128]
    BT1 = T("BT1", S4); BT2 = T("BT2", S4); TL = T("TL", S4); TR = T("TR", S4)
    Bc1 = T("Bc1", [128, 8, 128], BF16); Bc2 = T("Bc2", [128, 8, 128], BF16)
    CcT = T("CcT", [128, 8, 8, 16], BF16); Toep = T("Toep", [128, 8, 128], BF16)
    pT = b.psum("pT", [128, 128])
    for d in (0, 1):
        if d == 0:
            cplx_outer("D", slice(1, 9), 0, bbr, bbi, ["bbr", "bbi"])
        else:
            cplx_outer("A", slice(7, 15), 1, bbr, bbi, ["bbr", "bbi"])
        halves(BT1, oR, oI, False, False, ["oR", "oI"], "BT1")
        halves(BT2, oI, oR, True, False, ["oR", "oI"], "BT2")
        if d == 0:
            cplx_outer("D", slice(8, 16), 0, bbr, bbi, ["bbr", "bbi"])
            halves(TL, oR, oI, False, False, ["oR", "oI"], "TL")
        else:
            b.copy("dve", TL[:], BT1[:], ["BT1"], ["TL"])
        for gl in range(4):
            q = d * 4 + gl
            for (src, dst, sk, dk) in ((BT1, Bc1, "BT1", "Bc1"), (BT2, Bc2, "BT2", "Bc2")):
                b.transpose(pT[:], src[:, gl].rearrange("p a b -> p (a b)"), ident[:], [sk, "ident"], ["pT"])
                b.copy("act", dst[:, q, :], pT[:], ["pT"], [dk])
        if d == 0:
            cplx_outer("A", slice(8, 16), 0, cre, cim, ["cre", "cim"])
        else:
            cplx_outer("D", slice(0, 8), 1, cre, cim, ["cre", "cim"])
        halves(CcT[:, 4 * d:4 * d + 4], oR, oI, False, True, ["oR", "oI"], "CcT")
        if d == 0:
            cplx_outer("A", slice(7, 15), 0, cre, cim, ["cre", "cim"])
        else:
            cplx_outer("D", slice(8, 16), 1, cre, cim, ["cre", "cim"])
        halves(TR, oR, oI, False, True, ["oR", "oI"], "TR")
        for gl in range(4):
            q = d * 4 + gl
            b.mm(pT[:], TL[:, gl].rearrange("p a b -> p (a b)"), TR[:, gl].rearrange("p a b -> p (a b)"), True, True,
                 ["TL", "TR"], ["pT"])
            b.tt("dve", Toep[:, q, :], pT[:], (mF if d == 0 else mB)[:], ALU.mult, ["pT", "mF", "mB"], ["Toep"])
    r8 = T("r8", [128, 8]); th8 = T("th8", [128, 8]); th8r = T("th8r", [128, 8])
    b.act(r8[:], a[:], AF.Exp, ["a"], ["r8"], scale=8.0)
    b.ts("dve", th8[:], th[:], 8.0, ALU.mult, ["th"], ["th8"])
    tmp2 = {"ki": T("q_ki", [128, 8], I32)[:], "kf": T("q_kf", [128, 8])[:], "rb": T("q_rb", [128, 8])[:], "key": "q_"}
    reduce_ang(b, th8r[:], th8[:], tmp2, ["th8"], ["th8r"])
    NT = 33 * 64
    th64 = T("th64", [128, 8]); th64r = T("th64r", [128, 8])
    b.ts("dve", th64[:], th8r[:], 64.0, ALU.mult, ["th8r"], ["th64"])
    reduce_ang(b, th64r[:], th64[:], tmp2, ["th64"], ["th64r"])
    bvi = T("bvi", [128, 64], I32); bv = T("bv", [128, 64])
    b.iota(bvi[:], [[1, 64]], 0, 0, ["bvi"])
    b.copy("dve", bv[:], bvi[:], ["bvi"], ["bv"])
    SB_ = [128, 8, 64]; SA_ = [128, 8, 33]
    tmpB = {"ki": T("b_ki", SB_, I32), "kf": T("b_kf", SB_), "ra": T("b_ra", SB_), "rb": T("b_rb", SB_)}
    angB = T("angB", SB_); sB = T("sB", SB_); cB = T("cB", SB_); sA = T("sA", SA_); cA = T("cA", SA_)

    def small_tab(thv, thk, n_, sT, cT, nm):
        tm = {"ki": tmpB["ki"][:, :, :n_], "kf": tmpB["kf"][:, :, :n_], "ra": tmpB["ra"][:, :, :n_], "rb": tmpB["rb"][:, :, :n_], "key": "b_"}
        sh = [128, 8, n_]
        b.tt("dve", angB[:, :, :n_], thv[:].unsqueeze(2).to_broadcast(sh), bv[:, :n_].unsqueeze(1).to_broadcast(sh), ALU.mult,
             [thk, "bv"], ["angB"])
        b.sin_of(sT[:], angB[:, :, :n_], None, tm, ["angB"], ["s" + nm])
        b.ts("dve", angB[:, :, :n_], angB[:, :, :n_], PI / 2, ALU.add, ["angB"], ["angB"])
        b.sin_of(cT[:], angB[:, :, :n_], None, tm, ["angB"], ["c" + nm])
    small_tab(th8r, "th8r", 64, sB, cB, "B")
    small_tab(th64r, "th64r", 33, sA, cA, "A")
    sinTs = [T("sinT%d" % i, [128, NT]) for i in range(2)]
    cosTs = [T("cosT%d" % i, [128, NT]) for i in range(2)]
    e1 = T("e1", [128, NT]); e2 = T("e2", [128, NT])
    Uf = [T("Uf%d" % i, [128, NSS]) for i in range(2)]
    Ub = [T("Ub%d" % i, [128, NSS], BF16) for i in range(2)]
    sx1 = T("sx1", [128, NT]); sx2 = T("sx2", [128, NT])
    mm_ = [[T("m%d_%d" % (i, p_), [128, 512]) for i in range(4)] for p_ in range(2)]
    xt1s = [T("xt1_%d" % p_, [128, 512]) for p_ in range(2)]; xt2s = [T("xt2_%d" % p_, [128, 512]) for p_ in range(2)]
    Shs = [T("Sh%d" % p_, [128, 512], BF16) for p_ in range(2)]
    yo = [T("yo%d" % i, [128, 512]) for i in range(2)]
    PX = [b.psum("px%d" % i) for i in range(4)]
    PY = [b.psum("py%d" % i) for i in range(2)]
    yn = 0
    for q in range(8):
        ub, uf = Ub[q % 2], Uf[q % 2]
        uk = "Ub%d" % (q % 2)
        b.dma(uf[:], U[q], w=["Uf%d" % (q % 2)])
        b.copy("act", ub[:], uf[:], ["Uf%d" % (q % 2)], [uk])
        sinT = sinTs[q % 2]; cosT = cosTs[q % 2]
        skq = "sinT%d" % (q % 2); ckq = "cosT%d" % (q % 2)
        S3_ = [128, 33, 64]
        cAq = cA[:, q, :].unsqueeze(2).to_broadcast(S3_); sAq = sA[:, q, :].unsqueeze(2).to_broadcast(S3_)
        cBq = cB[:, q, :].unsqueeze(1).to_broadcast(S3_); sBq = sB[:, q, :].unsqueeze(1).to_broadcast(S3_)
        v3 = lambda t: t[:].rearrange("p (a c) -> p a c", c=64)
        b.tt("dve", v3(e1), cAq, cBq, ALU.mult, ["cA", "cB"], ["e1"])
        b.tt("dve", v3(e2), sAq, sBq, ALU.mult, ["sA", "sB"], ["e2"])
        b.tt("pool", cosT[:], e1[:], e2[:], ALU.subtract, ["e1", "e2"], [ckq])
        b.tt("dve", v3(e1), sAq, cBq, ALU.mult, ["sA", "cB"], ["e1"])
        b.tt("dve", v3(e2), cAq, sBq, ALU.mult, ["cA", "sB"], ["e2"])
        b.tt("pool", sinT[:], e1[:], e2[:], ALU.add, ["e1", "e2"], [skq])
        b.memset("pool", sx1[:, 0:1], 0.0, ["sx1"])
        b.memset("pool", sx2[:, 0:1], 0.0, ["sx2"])
        for ci, (c0, n) in enumerate(LB_CH):
            X1, X2 = PX[(2 * ci) % 4], PX[(2 * ci + 1) % 4]
            k1, k2 = "px%d" % ((2 * ci) % 4), "px%d" % ((2 * ci + 1) % 4)
            b.mm(X1[:, :n], Bc1[:, q, :], ub[:, c0:c0 + n], True, True, ["Bc1", uk], [k1])
            b.mm(X2[:, :n], Bc2[:, q, :], ub[:, c0:c0 + n], True, True, ["Bc2", uk], [k2])
            cD = cosT[:, c0 + 1:c0 + n + 1]; sD = sinT[:, c0 + 1:c0 + n + 1]
            pp_ = (q * len(LB_CH) + ci) % 2
            m = mm_[pp_]; xt1 = xt1s[pp_]; xt2 = xt2s[pp_]; Sh = Shs[pp_]
            mk = ["m%d_%d" % (i, pp_) for i in range(4)]
            x1k, x2k, shk = "xt1_%d" % pp_, "xt2_%d" % pp_, "Sh%d" % pp_
            b.tt("dve", m[0][:, :n], cD, X1[:, :n], ALU.mult, [ckq, k1], [mk[0]])
            b.tt("dve", m[1][:, :n], sD, X2[:, :n], ALU.mult, [skq, k2], [mk[1]])
            b.tt("dve", m[2][:, :n], cD, X2[:, :n], ALU.mult, [ckq, k2], [mk[2]])
            b.tt("dve", m[3][:, :n], sD, X1[:, :n], ALU.mult, [skq, k1], [mk[3]])
            b.tt("pool", xt1[:, :n], m[0][:, :n], m[1][:, :n], ALU.subtract, [mk[0], mk[1]], [x1k])
            b.tt("pool", xt2[:, :n], m[2][:, :n], m[3][:, :n], ALU.add, [mk[2], mk[3]], [x2k])
            r8b = r8[:, q:q + 1].to_broadcast([128, n])
            b.scan(sx1[:, c0 + 1:c0 + n + 1], r8b, xt1[:, :n], sx1[:, c0:c0 + 1], ["r8", x1k, "sx1"], ["sx1"])
            b.scan(sx2[:, c0 + 1:c0 + n + 1], r8b, xt2[:, :n], sx2[:, c0:c0 + 1], ["r8", x2k, "sx2"], ["sx2"])
            if c0 < 32:
                continue
            b.tt("dve", m[0][:, :n], cosT[:, c0:c0 + n], sx1[:, c0:c0 + n], ALU.mult, [ckq, "sx1"], [mk[0]])
            b.tt("dve", m[1][:, :n], sinT[:, c0:c0 + n], sx2[:, c0:c0 + n], ALU.mult, [skq, "sx2"], [mk[1]])
            b.tt("pool", Sh[:, :n], m[0][:, :n], m[1][:, :n], ALU.add, [mk[0], mk[1]], [shk])
            py = PY[yn % 2]; pk = "py%d" % (yn % 2); yt = yo[yn % 2]; yk = "yo%d" % (yn % 2); yn += 1
            b.mm(py[:, :n], Toep[:, q, :], ub[:, c0:c0 + n], True, False, ["Toep", uk], [pk])
            b.mm(py[:, :n], CcT[:, q].rearrange("p a b -> p (a b)"), Sh[:, :n], False, True, ["CcT", shk], [pk])
            b.copy("act", yt[:, :n], py[:, :n], [pk], [yk])
            b.dma(Y[q][:, c0 - 32:c0 - 32 + n], yt[:, :n], r=[yk])
    return b


def host_LB(inp, u, uc):
    b = build_LB()
    tau = np.arange(128) // 16
    maskF = (tau[None, :] >= tau[:, None]).astype(np.float32)
    maskB = (tau[:, None] >= tau[None, :]).astype(np.float32)
    ident = np.eye(128, dtype=np.float32)
    pidx = np.arange(128) % 64
    maps = []
    for k in range(NC):
        Uk = np.zeros((8, 128, NSS), np.float32)
        pr = {n: np.zeros((128, 8), np.float32) for n in ("lamre", "lamim", "lstep")}
        pb = {n: np.zeros((128, 8, 16), np.float32) for n in ("bre", "bim", "cre", "cim")}
        for d in range(2):
            for gl in range(4):
                g = 4 * k + gl
                q = d * 4 + gl
                cs = slice(g * 16, g * 16 + 16)
                if d == 0:
                    seq = np.concatenate([uc[:, cs], u[:, cs]], axis=0).reshape(NSS, 128)
                else:
                    seq = np.concatenate([u[:, cs], uc[:, cs]], axis=0).reshape(NSS, 128)[::-1]
                Uk[q] = seq.T
                pr["lamre"][:, q] = inp["s5_lam_re"][0, d, g][pidx]
                pr["lamim"][:, q] = inp["s5_lam_im"][0, d, g][pidx]
                pr["lstep"][:, q] = inp["s5_log_step"][0, d, g]
                pb["bre"][:, q, :] = inp["s5_b_re"][0, d, g][pidx, :]
                pb["bim"][:, q, :] = inp["s5_b_im"][0, d, g][pidx, :]
                pb["cre"][:, q, :] = inp["s5_c_re"][0, d, g].T[pidx, :]
                pb["cim"][:, q, :] = inp["s5_c_im"][0, d, g].T[pidx, :]
        mp = {"U": Uk, "maskF": maskF, "maskB": maskB, "ident": ident}
        mp.update(pr); mp.update(pb)
        maps.append(mp)
    res = run(b, maps)
    yA = np.zeros((L, 512), np.float32)
    yB = np.zeros((L, 512), np.float32)
    for k in range(NC):
        Yk = res[k]["Y"]
        for gl in range(4):
            g = 4 * k + gl
            cs = slice(g * 16, g * 16 + 16)
            yA[:, cs] = Yk[gl].T.reshape(NXS, 8, 16).reshape(L, 16)
            yB[:, cs] = Yk[4 + gl][:, ::-1].T.reshape(NXS, 8, 16).reshape(L, 16)
    return yA, yB


def build_LC():
    b = B()
    hT = b.din("hT", [D, TPC]); uT = b.din("uT", [512, TPC]); yAT = b.din("yAT", [512, TPC]); yBT = b.din("yBT", [512, TPC])
    vbp = b.din("vbp", [512, TPC + 30])
    g1d = b.din("g1", [128, 8]); pcd = b.din("pc", [128, 4, 4]); cwd = b.din("cw", [128, 4, 31])
    identd = b.din("ident", [128, 128])
    w_glu = b.din("w_glu", [512, 512]); w_out = b.din("w_out", [D, D])
    oT = b.dout("oT", [D, TPC])
    ones512 = make_consts(b, 512)

    def T(name, shape, dt=F32):
        return b.sb("s_" + name, shape, dt)
    g1 = T("g1", [128, 8]); pc = T("pc", [128, 4, 4]); cw = T("cw", [128, 4, 31])
    b.dma(g1[:], g1d, w=["g1"]); b.dma(pc[:], pcd, w=["pc"]); b.dma(cw[:], cwd, w=["cw"])
    ident = T("ident", [128, 128]); b.dma(ident[:], identd, w=["ident"])
    dg = T("dg", [128, 4, 31, 128], BF16)
    for j in range(4):
        for tap in range(31):
            if tap % 2 == 0:
                b.act(dg[:, j, tap, :], ident[:], AF.Identity, ["ident", "cw"], ["dg%d_%d" % (j, tap)], scale=cw[:, j, tap:tap + 1])
            else:
                b.ts("dve", dg[:, j, tap, :], ident[:], cw[:, j, tap:tap + 1], ALU.mult, ["ident", "cw"], ["dg%d_%d" % (j, tap)])
    wg = T("wg", [128, 4, 512], BF16); wo = T("wo", [128, 8, 1024], BF16)
    stage = [T("stg%d" % i, [128, 1024]) for i in range(2)]
    wgk = ["wgw%d" % k for k in range(4)]; wok = ["wow%d" % k for k in range(8)]
    hchs = [T("hch%d" % i, [128, 8, CW]) for i in range(2)]; uchs = [T("uch%d" % i, [128, 4, CW]) for i in range(1)] * 2
    yAcs = [T("yAc%d" % i, [128, 4, CW]) for i in range(1)] * 2; yBcs = [T("yBc%d" % i, [128, 4, CW]) for i in range(1)] * 2
    vbcs = [T("vbc%d" % i, [128, 4, CW + 30]) for i in range(2)]
    vbbs = [T("vbb%d" % i, [128, 4, CW + 30], BF16) for i in range(2)]
    ya = T("ya", [128, 4, CW]); ya1f = T("ya1f", [128, 4, CW]); ya1b = T("ya1b", [128, 4, CW], BF16)
    cat = T("cat", [128, 8, CW], BF16); acc = T("acc", [128, 4, CW])
    accb = T("accb", [128, 4, CW], BF16); accsq = T("accsq", [128, 4, CW], BF16)
    tA = T("tA", [128, CW]); tB = T("tB", [128, CW]); tC = T("tC", [128, CW]); tD = T("tD", [128, CW])
    tM = T("tM", [128, CW]); tR = T("tR", [128, CW])
    NPS = 4
    PS = [b.psum("ps%d" % i) for i in range(NPS)]
    PCV = [b.psum("pcv%d" % i) for i in range(2)]
    psm = b.psum("psm"); pse = b.psum("pse")
    pn = 0
    for c in range(NCHUNK):
        sl = slice(c * CW, (c + 1) * CW)
        pr_ = c % 2
        hch, uch, yAc, yBc, vbc = hchs[pr_], uchs[pr_], yAcs[pr_], yBcs[pr_], vbcs[pr_]
        HK = ["hch%d_%d" % (pr_, k) for k in range(8)]
        UK, YAK, YBK, VK = "uch0", "yAc0", "yBc0", "vbc%d" % pr_
        vbb = vbbs[pr_]; VBK = "vbb%d" % pr_
        b.dma(vbc[:], vbp[:, c * CW:c * CW + CW + 30].rearrange("(kb p) n -> p kb n", p=128), w=[VK])
        b.dma(uch[:], uT[:, sl].rearrange("(kb p) n -> p kb n", p=128), w=[UK])
        b.dma(yAc[:], yAT[:, sl].rearrange("(kb p) n -> p kb n", p=128), w=[YAK])
        b.dma(yBc[:], yBT[:, sl].rearrange("(kb p) n -> p kb n", p=128), w=[YBK])
        if c == 0:
            load_weight_bf16(b, wg, w_glu, 4, 512, stage, "wg")
            load_weight_bf16(b, wo, w_out, 8, 1024, stage, "wo")
        b.dma(hch[:], hT[:, sl].rearrange("(kb p) n -> p kb n", p=128), w=HK)
        b.copy("act", vbb[:], vbc[:], [VK], [VBK])

        def conv_mm(j):
            pcv = PCV[j % 2]; pck = "pcv%d" % (j % 2)
            for tap in range(31):
                b.mm(pcv[:], dg[:, j, tap, :], vbb[:, j, tap:tap + CW], tap == 0, tap == 30, ["dg%d_%d" % (j, tap), VBK], [pck])

        def conv_ev(j):
            pcv = PCV[j % 2]; pck = "pcv%d" % (j % 2)
            b.act(acc[:, j, :], pcv[:], AF.Identity, [pck, "pc"], ["acc%d" % j], bias=pc[:, 1, j:j + 1])
            b.act(accb[:, j, :], pcv[:], AF.Identity, [pck, "pc"], ["accb%d" % j], bias=pc[:, 1, j:j + 1])
            b.act(accsq[:, j, :], pcv[:], AF.Square, [pck, "pc"], ["accsq%d" % j], bias=pc[:, 1, j:j + 1])
        conv_mm(0); conv_mm(1)
        for j in range(4):
            b.tt("pool", tA[:], yAc[:, j, :], yBc[:, j, :], ALU.add, [YAK, YBK], ["tA"])
            b.stt("dve", ya[:, j, :], uch[:, j, :], pc[:, 0, j:j + 1], tA[:], ALU.mult, ALU.add, [UK, "pc", "tA"], ["ya%d" % j])
            b.act(tB[:], ya[:, j, :], AF.Square, ["ya%d" % j], ["tB"])
            b.ts("dve", tB[:], tB[:], 0.044715, ALU.mult, ["tB"], ["tB"], s2=1.0, op1=ALU.add)
            b.tt("dve", tB[:], tB[:], ya[:, j, :], ALU.mult, ["tB", "ya%d" % j], ["tB"])
            b.act(tC[:], tB[:], AF.Sigmoid, ["tB"], ["tC"], scale=1.5957691216057308)
            b.tt("dve", ya1f[:, j, :], ya[:, j, :], tC[:], ALU.mult, ["ya%d" % j, "tC"], ["ya1f%d" % j])
            b.copy("pool", ya1b[:, j, :], ya1f[:, j, :], ["ya1f%d" % j], ["ya1b%d" % j])
        conv_ev(0); conv_ev(1)
        conv_mm(2); conv_mm(3)
        yk = ["ya1b%d" % k for k in range(4)]
        for ob in range(4):
            ps = PS[pn % NPS]; pk = "ps%d" % (pn % NPS); pn += 1
            for kb in range(4):
                b.mm(ps[:], wg[:, kb, ob * 128:(ob + 1) * 128], ya1b[:, kb, :], kb == 0, kb == 3, wgk + yk, [pk])
            b.act(tC[:], ps[:], AF.Sigmoid, [pk], ["tC"])
            b.tt("dve", cat[:, ob, :], ya1f[:, ob, :], tC[:], ALU.mult, ["ya1f%d" % ob, "tC"], ["cat%d" % ob])
        conv_ev(2); conv_ev(3)
        for j in range(4):
            b.mm(psm[:], ones512[:], accb[:, j, :], j == 0, j == 3, ["onesD", "accb%d" % j], ["psm"])
        for j in range(4):
            b.mm(pse[:], ones512[:], accsq[:, j, :], j == 0, j == 3, ["onesD", "accsq%d" % j], ["pse"])
        b.copy("act", tM[:], psm[:], ["psm"], ["tM"])
        b.act(tD[:], psm[:], AF.Square, ["psm"], ["tD"])
        b.tt("dve", tD[:], pse[:], tD[:], ALU.subtract, ["pse", "tD"], ["tD"])
        b.act(tD[:], tD[:], AF.Sqrt, ["tD", "epsb"], ["tD"], bias=b.epsb[:, 0:1])
        b.recip(tR[:], tD[:], ["tD"], ["tR"])
        for j in range(4):
            b.tt("dve", tA[:], acc[:, j, :], tM[:], ALU.subtract, ["acc%d" % j, "tM"], ["tA"])
            b.tt("dve", tA[:], tA[:], tR[:], ALU.mult, ["tA", "tR"], ["tA"])
            b.act(cat[:, 4 + j, :], tA[:], AF.Silu, ["tA", "pc"], ["cat%d" % (4 + j)],
                  bias=pc[:, 3, j:j + 1], scale=pc[:, 2, j:j + 1])
        ck = ["cat%d" % k for k in range(8)]
        for ob in range(8):
            ps = PS[pn % NPS]; pk = "ps%d" % (pn % NPS); pn += 1
            for kb in range(8):
                b.mm(ps[:], wo[:, kb, ob * 128:(ob + 1) * 128], cat[:, kb, :], kb == 0, kb == 7, wok + ck, [pk])
            b.stt("dve", hch[:, ob, :], ps[:], g1[:, ob:ob + 1], hch[:, ob, :], ALU.mult, ALU.add,
                  [pk, "g1", HK[ob]], [HK[ob]])
        b.dma(oT[:, sl].rearrange("(kb p) n -> p kb n", p=128), hch[:], r=HK)
    return b


def fm(a):
    return np.ascontiguousarray(a.T)


def host_LC(inp, mods, hT, u, vb, yA, yB):
    b = build_LC()
    vpad = np.zeros((L + 30, 512), np.float32)
    vpad[15:15 + L] = vb
    pc = np.stack([vec_pb(inp["s5_d"][0], 4), vec_pb(inp["conv_b"][0], 4), vec_pb(inp["conv_ln_g"][0], 4),
                   vec_pb(inp["conv_ln_b"][0], 4)], axis=1)
    cw = np.ascontiguousarray(inp["conv_w"][0].T.reshape(4, 128, 31).transpose(1, 0, 2))
    g1 = mod_vec(mods, 0, 2, 0)
    maps = []
    for k in range(NC):
        ts_ = slice(k * TPC, (k + 1) * TPC)
        maps.append({"hT": hT[k], "uT": fm(u[ts_]), "yAT": fm(yA[ts_]), "yBT": fm(yB[ts_]),
                     "vbp": fm(vpad[k * TPC:(k + 1) * TPC + 30]), "g1": g1, "pc": np.ascontiguousarray(pc), "cw": cw,
                     "ident": np.eye(128, dtype=np.float32),
                     "w_glu": np.ascontiguousarray(inp["s5_w_glu"][0]), "w_out": np.ascontiguousarray(inp["w_out"][0])})
    res = run(b, maps)
    return [r["oT"] for r in res]


def build_LM(final):
    b = B()
    hT = b.din("hT", [D, TPC]); pvd = b.din("pv", [128, 4, 8]); fgd = b.din("fg", [128, 8])
    w1 = b.din("w1", [D, 4 * D]); w2 = b.din("w2", [4 * D, D])
    oT = b.dout("oT", [D, TPC])
    onesD = make_consts(b, D)

    def T(name, shape, dt=F32):
        return b.sb("s_" + name, shape, dt)
    h = T("h", [128, 8, TPC]); n = T("n", [128, 8, TPC], BF16)
    pv = T("pv", [128, 4, 8]); fg = T("fg", [128, 8]); gm = T("gm", [128, 8])
    b.dma(pv[:], pvd, w=["pv"]); b.dma(fg[:], fgd, w=["fg"])
    b.ts("dve", gm[:], pv[:, 2, :], 1.0, ALU.add, ["pv"], ["gma"])
    b.tt("dve", gm[:], gm[:], pv[:, 0, :], ALU.mult, ["gma", "pv"], ["gm"])
    sq = T("sq", [128, 8, CW], BF16); rstd = T("rstd", [128, CW]); rt = T("rt", [128, CW]); hn = T("hn", [128, CW])
    pst = b.psum("pst")

    def load_h(c):
        sl = slice(c * CW, (c + 1) * CW)
        b.dma(h[:, :, sl], hT[:, sl].rearrange("(kb p) n -> p kb n", p=128), w=["h%d_%d" % (kb, c) for kb in range(8)])

    def norm_chunk(c):
        sl = slice(c * CW, (c + 1) * CW)
        rms_rstd(b, lambda kb: h[:, kb, sl], 8, CW, onesD, sq, pst, rstd, rt, ["h%d_%d" % (k, c) for k in range(8)], "M")
        for kb in range(8):
            b.tt("dve", hn[:], h[:, kb, sl], rstd[:], ALU.mult, ["h%d_%d" % (kb, c), "Mrstd"], ["hn"])
            b.act(n[:, kb, sl], hn[:], AF.Identity, ["hn", "gm", "pv"], ["n%d_%d" % (kb, c)],
                  bias=pv[:, 1, kb:kb + 1], scale=gm[:, kb:kb + 1])
    w1g = [T("w1g%d" % i, [128, 8, 512], BF16) for i in range(2)]
    w2g = [T("w2g%d" % i, [128, 4, 1024], BF16) for i in range(2)]
    stage = [T("stg%d" % i, [128, 1024]) for i in range(3)]
    rl = [T("rl%d" % i, [128, CW]) for i in range(2)]
    h1 = [T("h1_%d" % i, [128, 4, CW], BF16) for i in range(2)]
    PH = [b.psum("ph%d" % i) for i in range(3)]
    PO = [b.psum("po%d" % i) for i in range(4)]
    cnt = {"sn": 0, "hn": 0, "on": 0}
    evt = [T("evt%d" % i, [128, CW]) for i in range(2)]

    def load_group(gi):
        par = gi % 2
        for kb in range(8):
            st = stage[cnt["sn"] % 3]; sk = "stg%d" % (cnt["sn"] % 3); cnt["sn"] += 1
            b.dma(st[:, :512], w1[kb * 128:(kb + 1) * 128, gi * 512:(gi + 1) * 512], w=[sk])
            b.copy("act" if kb % 2 == 0 else "pool", w1g[par][:, kb, :], st[:, :512], [sk], ["w1g%d_%d" % (par, kb)])
        for hb in range(4):
            st = stage[cnt["sn"] % 3]; sk = "stg%d" % (cnt["sn"] % 3); cnt["sn"] += 1
            b.dma(st[:], w2[gi * 512 + hb * 128:gi * 512 + (hb + 1) * 128, :], w=[sk])
            b.copy("act" if hb % 2 == 0 else "pool", w2g[par][:, hb, :], st[:], [sk], ["w2g%d_%d" % (par, hb)])

    def hidden(i):
        gi, c = divmod(i, NCHUNK)
        par = gi % 2; hp = i % 2
        sl = slice(c * CW, (c + 1) * CW)
        w1k = ["w1g%d_%d" % (par, k) for k in range(8)]
        for hb in range(4):
            ps = PH[cnt["hn"] % 3]; pk = "ph%d" % (cnt["hn"] % 3); cnt["hn"] += 1
            for kb in range(8):
                b.mm(ps[:], w1g[par][:, kb, hb * 128:(hb + 1) * 128], n[:, kb, sl], kb == 0, kb == 7,
                     w1k + ["n%d_%d" % (k, c) for k in range(8)], [pk])
            b.act(rl[hb % 2][:], ps[:], AF.Relu, [pk], ["rl%d" % (hb % 2)])
            b.tt("pool", h1[hp][:, hb, :], rl[hb % 2][:], rl[hb % 2][:], ALU.mult, ["rl%d" % (hb % 2)], ["h1_%d_%d" % (hp, hb)])

    def outproj(i):
        gi, c = divmod(i, NCHUNK)
        par = gi % 2; hp = i % 2
        sl = slice(c * CW, (c + 1) * CW)
        w2k = ["w2g%d_%d" % (par, k) for k in range(4)]
        h1k = ["h1_%d_%d" % (hp, k) for k in range(4)]
        for ob in range(8):
            ps = PO[cnt["on"] % 4]; pk = "po%d" % (cnt["on"] % 4); cnt["on"] += 1
            for hb in range(4):
                b.mm(ps[:], w2g[par][:, hb, ob * 128:(ob + 1) * 128], h1[hp][:, hb, :], hb == 0, hb == 3, w2k + h1k, [pk])
            b.stt("dve", h[:, ob, sl], ps[:], pv[:, 3, ob:ob + 1], h[:, ob, sl], ALU.mult, ALU.add,
                  [pk, "pv", "h%d_%d" % (ob, c)], ["h%d_%d" % (ob, c)])

    och = [T("och%d" % i, [128, 8, CW]) for i in range(2)] if final else None

    def finalize_chunk(c):
        sl = slice(c * CW, (c + 1) * CW)
        hk = ["h%d_%d" % (k, c) for k in range(8)]
        if final:
            rms_rstd(b, lambda kb: h[:, kb, sl], 8, CW, onesD, sq, pst, rstd, rt, hk, "F")
            oc = och[c % 2]
            for kb in range(8):
                b.stt("dve", oc[:, kb, :], h[:, kb, sl], fg[:, kb:kb + 1], rstd[:], ALU.mult, ALU.mult,
                      ["h%d_%d" % (kb, c), "fg", "Frstd"], ["och%d" % (c % 2)])
            b.dma(oT[:, sl].rearrange("(kb p) n -> p kb n", p=128), oc[:], r=["och%d" % (c % 2)])
        else:
            b.dma(oT[:, sl].rearrange("(kb p) n -> p kb n", p=128), h[:, :, sl], r=hk)

    NIT = 8 * NCHUNK
    load_h(0)
    load_group(0)
    for c in range(1, NCHUNK):
        load_h(c)
    norm_chunk(0)
    hidden(0)
    load_group(1)
    for c in range(1, NCHUNK):
        norm_chunk(c)
    for i in range(NIT):
        gi, c = divmod(i, NCHUNK)
        if i + 1 < NIT:
            hidden(i + 1)
        outproj(i)
        if c == NCHUNK - 1 and gi + 2 < 8:
            load_group(gi + 2)
        if gi == 7:
            finalize_chunk(c)
    return b


def host_LM(inp, mods, hT, layer):
    b = build_LM(final=(layer == 1))
    pv = np.stack([vec_pb(inp["norm_mlp_g"][layer], 8), mod_vec(mods, layer, 3, 0), mod_vec(mods, layer, 4, 0),
                   mod_vec(mods, layer, 5, 0)], axis=1)
    fg = vec_pb(inp["final_g"], 8)
    w1 = np.ascontiguousarray(inp["mlp_w1"][layer]); w2 = np.ascontiguousarray(inp["mlp_w2"][layer])
    maps = [{"hT": hT[k], "pv": np.ascontiguousarray(pv), "fg": fg, "w1": w1, "w2": w2} for k in range(NC)]
    res = run(b, maps)
    return [r["oT"] for r in res]


HALO = 8
WH = TPC + 2 * HALO
LD_CH = [(i * 512, 512) for i in range(4)] + [(2048, 16)]


def build_LD():
    b = B()
    hpT = b.din("hpT", [D, WH]); vald = b.din("valid", [128, WH]); pvd = b.din("pv", [128, 4, 8])
    ppd = b.din("pp", [128, 2, 8]); pwd = b.din("pw", [4, 256, 256])
    oT = b.dout("oT", [D, TPC])
    onesD = make_consts(b, D)

    def T(name, shape, dt=F32):
        return b.sb("s_" + name, shape, dt)
    h = T("h", [128, 8, WH]); val = T("val", [128, WH]); rstdF = T("rstdF", [128, WH])
    pv = T("pv", [128, 4, 8]); pp = T("pp", [128, 2, 8]); gm = T("gm", [128, 8]); Av = T("Av", [128, 8]); Bv = T("Bv", [128, 8])
    for kb in range(8):
        b.dma(h[:, kb, :], hpT[kb * 128:(kb + 1) * 128, :], w=["h%d" % kb])
    b.dma(val[:], vald, w=["val"]); b.dma(pv[:], pvd, w=["pv"]); b.dma(pp[:], ppd, w=["pp"])
    pwb = T("pwb", [128, 4, 2, 256], BF16)
    stage = [T("stg%d" % i, [128, 256]) for i in range(2)]
    sn = 0
    for gi in range(4):
        for kbl in range(2):
            st = stage[sn % 2]; sk = "stg%d" % (sn % 2); sn += 1
            b.dma(st[:], pwd[gi][kbl * 128:(kbl + 1) * 128, :], w=[sk])
            b.copy("act", pwb[:, gi, kbl, :], st[:], [sk], ["pwb"])
    b.ts("dve", gm[:], pv[:, 2, :], 1.0, ALU.add, ["pv"], ["gma"])
    b.tt("dve", gm[:], gm[:], pv[:, 0, :], ALU.mult, ["gma", "pv"], ["gm"])
    b.tt("dve", Av[:], pp[:, 1, :], pv[:, 3, :], ALU.mult, ["pp", "pv"], ["Av"])
    b.tt("dve", Bv[:], pp[:, 0, :], Av[:], ALU.mult, ["pp", "Av"], ["Bv"])
    sq = T("sq", [128, 8, CW], BF16); rstd = T("rstd", [128, CW]); rt = T("rt", [128, CW])
    pst = b.psum("pst")
    for (c0, n_) in LD_CH:
        sl = slice(c0, c0 + n_)
        rms_rstd(b, lambda kb: h[:, kb, sl], 8, n_, onesD, sq, pst, rstd, rt, ["h%d" % k for k in range(8)], "D")
        b.copy("pool", rstdF[:, sl], rstd[:, :n_], ["Drstd"], ["rstdF"])
    sA = T("sA", [128, WH]); sB = T("sB", [128, WH])
    nmb = [T("nmb%d" % i, [128, WH]) for i in range(2)]
    icnt = T("icnt", [128, TPC]); tW = T("tW", [128, TPC])
    pgb = T("pgb", [128, 8, TPC], BF16)

    def chain(eng, src, srck, levels):
        cur, curk, ln = src, srck, WH
        bufs = [(sA, "sA"), (sB, "sB")]
        for lv in range(levels):
            sh = 1 << lv
            dst, dstk = bufs[lv % 2]
            b.tt(eng, dst[:, 0:ln - sh], cur[:, 0:ln - sh], cur[:, sh:ln], ALU.add, [curk], [dstk])
            cur, curk, ln = dst[:], dstk, ln - sh
        return cur, curk

    for blk in range(8):
        gi = blk // 2
        w = 2 << gi
        off = HALO - w // 2
        if blk % 2 == 0:
            ct, ck = chain("pool", val[:], "val", gi + 1)
            b.recip(icnt[:], ct[:, off:off + TPC], [ck], ["icnt"])
        nb = nmb[blk % 2]; nk = "nmb%d" % (blk % 2)
        b.tt("dve", nb[:], h[:, blk, :], rstdF[:], ALU.mult, ["h%d" % blk, "rstdF"], [nk])
        b.act(nb[:], nb[:], AF.Identity, [nk, "gm", "pv"], [nk], bias=pv[:, 1, blk:blk + 1], scale=gm[:, blk:blk + 1])
        b.tt("pool", nb[:], nb[:], val[:], ALU.mult, [nk, "val"], [nk])
        st, sk = chain("dve", nb[:], nk, gi + 1)
        b.tt("dve", tW[:], st[:, off:off + TPC], icnt[:], ALU.mult, [sk, "icnt"], ["tW"])
        b.tt("dve", pgb[:, blk, :], tW[:], nb[:, HALO:HALO + TPC], ALU.subtract, ["tW", nk], ["pgb%d" % blk])
    PS = [b.psum("ps%d" % i) for i in range(4)]
    yt = [T("yt%d" % i, [128, CW]) for i in range(2)]
    pn = 0
    for c in range(NCHUNK):
        sl = slice(c * CW, (c + 1) * CW)
        slh = slice(HALO + c * CW, HALO + (c + 1) * CW)
        for gi in range(4):
            for obl in range(2):
                ob = 2 * gi + obl
                ps = PS[pn % 4]; pk = "ps%d" % (pn % 4); y_ = yt[pn % 2]; ykk = "yt%d" % (pn % 2); pn += 1
                for kbl in range(2):
                    b.mm(ps[:], pwb[:, gi, kbl, obl * 128:(obl + 1) * 128], pgb[:, 2 * gi + kbl, sl], kbl == 0, kbl == 1,
                         ["pwb", "pgb%d" % (2 * gi), "pgb%d" % (2 * gi + 1)], [pk])
                b.act(y_[:], ps[:], AF.Identity, [pk, "Av", "Bv"], [ykk], bias=Bv[:, ob:ob + 1], scale=Av[:, ob:ob + 1])
                b.tt("dve", h[:, ob, slh], h[:, ob, slh], y_[:], ALU.add, ["h%d" % ob, ykk], ["h%d" % ob, "hf%d_%d" % (ob, c)])
        b.dma(oT[:, sl].rearrange("(kb p) n -> p kb n", p=128), h[:, :, slh], r=["hf%d_%d" % (k, c) for k in range(8)])
    return b


def host_LD(inp, mods, hT):
    b = build_LD()
    hall = np.concatenate(hT, axis=1)
    hpad = np.zeros((D, L + 2 * HALO), np.float32)
    hpad[:, HALO:HALO + L] = hall
    vfull = np.zeros((L + 2 * HALO,), np.float32)
    vfull[HALO:HALO + L] = 1.0
    pv = np.stack([vec_pb(inp["norm_mix_g"][1], 8), mod_vec(mods, 1, 0, 0), mod_vec(mods, 1, 1, 0), mod_vec(mods, 1, 2, 0)], axis=1)
    pp = np.stack([vec_pb(inp["pool_b"][0].reshape(-1), 8), vec_pb(inp["pool_scale"][0], 8)], axis=1)
    pw = np.ascontiguousarray(inp["pool_w"][0])
    maps = []
    for k in range(NC):
        maps.append({"hpT": np.ascontiguousarray(hpad[:, k * TPC:k * TPC + WH]),
                     "valid": np.ascontiguousarray(np.broadcast_to(vfull[k * TPC:k * TPC + WH], (128, WH))),
                     "pv": np.ascontiguousarray(pv), "pp": np.ascontiguousarray(pp), "pw": pw})
    res = run(b, maps)
    return [r["oT"] for r in res]


def kernel(**inp):
    inp = {k: np.asarray(v) for k, v in inp.items()}
    mods = host_L0(inp)
    hT, u, vb, uc = host_LA(inp, mods)
    yA, yB = host_LB(inp, u, uc)
    h1T = host_LC(inp, mods, hT, u, vb, yA, yB)
    h2T = host_LM(inp, mods, h1T, 0)
    h3T = host_LD(inp, mods, h2T)
    oT = host_LM(inp, mods, h3T, 1)
    out = np.concatenate([o.T for o in oT], axis=0)
    return np.ascontiguousarray(out[None]).astype(np.float32, copy=False)
```

```python
import math
import numpy as np
import concourse.bass as bass
import concourse.mybir as mybir
from concourse.bass_utils import run_bass_kernel_spmd

F32 = mybir.dt.float32
BF16 = mybir.dt.bfloat16
I32 = mybir.dt.int32
AF = mybir.ActivationFunctionType
ALU = mybir.AluOpType
AX = mybir.AxisListType

NC = 8
D = 1024
L = 16384
TPC = L // NC
CW = 512
NCHUNK = TPC // CW
CTX = 256
EPS = 1e-6
PI = math.pi
TWO_PI = 2 * math.pi
C1 = 6.28125
C2 = 2 * math.pi - 6.28125


class _Op:
    __slots__ = ("eng", "fn", "deps", "signal", "idx", "dma", "dsem", "dcount", "n")


class Sched:
    N_DMA_SEMS = 48

    def __init__(self, nc):
        self.nc = nc
        self.ops = []
        self.last_w = {}
        self.readers = {}
        self.E = {"pe": nc.tensor, "dve": nc.vector, "act": nc.scalar,
                  "pool": nc.gpsimd, "sp": nc.sync}

    def add(self, eng, fn, reads=(), writes=(), dma=False):
        op = _Op()
        op.eng = eng
        op.fn = fn
        op.dma = dma
        op.signal = False
        op.idx = None
        op.n = len(self.ops)
        deps = {}
        for r in reads:
            w = self.last_w.get(r)
            if w is not None:
                deps[w.n] = w
        for r in writes:
            w = self.last_w.get(r)
            if w is not None:
                deps[w.n] = w
            rd = self.readers.get(r)
            if rd:
                for lst in rd.values():
                    for o in lst:
                        deps[o.n] = o
        for r in reads:
            rd = self.readers.setdefault(r, {})
            if dma:
                rd.setdefault("dma", []).append(op)
            else:
                rd[eng] = [op]
        for r in writes:
            self.last_w[r] = op
            self.readers[r] = {}
        dl = []
        for d in deps.values():
            if d is op:
                continue
            if (not d.dma) and (not dma) and d.eng == eng and eng == "pe":
                continue
            d.signal = True
            dl.append(d)
        op.deps = dl
        self.ops.append(op)
        return op

    def emit(self):
        nc = self.nc
        esem = {e: nc.alloc_semaphore("sem_" + e) for e in self.E}
        ecount = {e: 0 for e in self.E}
        dsems = [nc.alloc_semaphore("dsem%d" % i) for i in range(self.N_DMA_SEMS)]
        dcum = [0] * self.N_DMA_SEMS
        dnext = 0
        waited = {e: {} for e in self.E}

        def do_wait(eng, key, sem, val):
            w = waited[eng]
            if w.get(key, 0) >= val:
                return
            w[key] = val
            self.E[eng].wait_ge(sem, val)

        for op in self.ops:
            eng = op.eng
            for d in op.deps:
                if d.dma:
                    do_wait(eng, ("d", d.dsem), dsems[d.dsem], d.dcount)
                else:
                    do_wait(eng, ("e", d.eng), esem[d.eng], d.idx)
            if op.dma:
                s = dnext
                dnext = (dnext + 1) % self.N_DMA_SEMS
                if dcum[s] > 0:
                    do_wait(eng, ("d", s), dsems[s], dcum[s])
                ins = op.fn()
                ins.then_inc(dsems[s], 16)
                dcum[s] += 16
                op.dsem = s
                op.dcount = dcum[s]
            else:
                ins = op.fn()
                if op.signal:
                    ecount[eng] += 1
                    op.idx = ecount[eng]
                    ins.then_inc(esem[eng], 1)
        for s, c in zip(dsems, dcum):
            if c:
                nc.sync.wait_ge(s, c)


class B:
    def __init__(self):
        self.nc = bass.Bass("TRN2", target_bir_lowering=False)
        self.S = Sched(self.nc)
        self.npsum = 0
        self.ins = {}
        self.outs = {}

    def din(self, name, shape, dt=F32):
        ap = self.nc.dram_tensor(name, list(shape), dt, kind="ExternalInput").ap()
        self.ins[name] = ap
        return ap

    def dout(self, name, shape, dt=F32):
        ap = self.nc.dram_tensor(name, list(shape), dt, kind="ExternalOutput").ap()
        self.outs[name] = ap
        return ap

    def sb(self, name, shape, dt=F32):
        return self.nc.alloc_sbuf_tensor(name, list(shape), dt)

    def psum(self, name, shape=(128, 512), dt=F32):
        return self.nc.alloc_psum_tensor(name, list(shape), dt)

    def dma(self, out, in_, r=(), w=(), **kw):
        nc = self.nc
        self.S.add("sp", lambda: nc.sync.dma_start(out=out, in_=in_, **kw), r, w, dma=True)

    def act(self, out, in_, func, r, w, bias=None, scale=None):
        nc = self.nc
        kw = {}
        if bias is not None:
            kw["bias"] = bias
        if scale is not None:
            kw["scale"] = scale
        self.S.add("act", lambda: nc.scalar.activation(out=out, in_=in_, func=func, **kw), r, w)

    def tt(self, eng, out, in0, in1, op, r, w):
        e = self.S.E[eng]
        self.S.add(eng, lambda: e.tensor_tensor(out=out, in0=in0, in1=in1, op=op), r, w)

    def ts(self, eng, out, in0, s1, op0, r, w, s2=None, op1=None):
        e = self.S.E[eng]
        if op1 is None:
            self.S.add(eng, lambda: e.tensor_scalar(out=out, in0=in0, scalar1=s1, scalar2=None, op0=op0), r, w)
        else:
            self.S.add(eng, lambda: e.tensor_scalar(out=out, in0=in0, scalar1=s1, scalar2=s2, op0=op0, op1=op1), r, w)

    def stt(self, eng, out, in0, scalar, in1, op0, op1, r, w):
        e = self.S.E[eng]
        self.S.add(eng, lambda: e.scalar_tensor_tensor(out=out, in0=in0, scalar=scalar, in1=in1, op0=op0, op1=op1), r, w)

    def copy(self, eng, out, in_, r, w):
        if eng == "act":
            nc = self.nc
            self.S.add("act", lambda: nc.scalar.copy(out=out, in_=in_), r, w)
        else:
            e = self.S.E[eng]
            self.S.add(eng, lambda: e.tensor_copy(out=out, in_=in_), r, w)

    def memset(self, eng, ap, val, w):
        e = self.S.E[eng]
        self.S.add(eng, lambda: e.memset(ap, val), (), w)

    def recip(self, out, in_, r, w):
        nc = self.nc
        self.S.add("dve", lambda: nc.vector.reciprocal(out=out, in_=in_), r, w)

    def mm(self, out, lhsT, rhs, start, stop, r, w):
        nc = self.nc
        self.S.add("pe", lambda: nc.tensor.matmul(out, lhsT=lhsT, rhs=rhs, start=start, stop=stop), r, w)

    def transpose(self, out, in_, ident, r, w):
        nc = self.nc
        self.S.add("pe", lambda: nc.tensor.transpose(out=out, in_=in_, identity=ident), r, w)

    def scan(self, out, d0, d1, init, r, w):
        nc = self.nc
        self.S.add("dve", lambda: nc.vector.tensor_tensor_scan(out=out, data0=d0, data1=d1, initial=init,
                                                             op0=ALU.mult, op1=ALU.add), r, w)

    def iota(self, out, pattern, base, cm, w):
        nc = self.nc
        self.S.add("pool", lambda: nc.gpsimd.iota(out, pattern=pattern, base=base, channel_multiplier=cm), (), w)

    def finish(self):
        self.S.emit()
        return self.nc

    def sin_of(self, out, ang, shape, tmp, r, w):
        ki, kf, ra, rb = tmp["ki"], tmp["kf"], tmp["ra"], tmp["rb"]
        tk = tmp["key"]
        self.ts("dve", ki, ang, 1.0 / TWO_PI, ALU.mult, r, [tk + "ki"])
        self.copy("dve", kf, ki, [tk + "ki"], [tk + "kf"])
        self.stt("dve", ra, kf, -C1, ang, ALU.mult, ALU.add, r + [tk + "kf"], [tk + "ra"])
        self.stt("dve", rb, kf, -C2, ra, ALU.mult, ALU.add, [tk + "kf", tk + "ra"], [tk + "rb"])
        self.ts("dve", kf, rb, PI, ALU.is_gt, [tk + "rb"], [tk + "kf"], s2=-TWO_PI, op1=ALU.mult)
        self.tt("dve", ra, rb, kf, ALU.add, [tk + "rb", tk + "kf"], [tk + "ra"])
        self.ts("dve", kf, ra, -PI, ALU.is_lt, [tk + "ra"], [tk + "kf"], s2=TWO_PI, op1=ALU.mult)
        self.tt("dve", rb, ra, kf, ALU.add, [tk + "ra", tk + "kf"], [tk + "rb"])
        self.ts("dve", ra, rb, -PI, ALU.max, [tk + "rb"], [tk + "ra"], s2=PI, op1=ALU.min)
        self.act(out, ra, AF.Sin, [tk + "ra"], w)


def run(b, in_maps):
    b.finish()
    res = run_bass_kernel_spmd(b.nc, in_maps, core_ids=list(range(NC)))
    return res.results


def vec_pb(v, nblk):
    return np.ascontiguousarray(np.asarray(v, np.float32).reshape(nblk, 128).T)


def build_L0():
    b = B()
    w = b.din("w", [D, 1536])
    bb = b.din("b", [128, 12])
    cc = b.din("cc", [128, 8, 2])
    o = b.dout("o", [128, 12, 2])
    wsb = b.sb("wsb", [128, 8, 1536])
    bsb = b.sb("bsb", [128, 12])
    ccs = b.sb("ccs", [128, 8, 2])
    sc = b.sb("sc", [128, 8, 2])
    osb = b.sb("osb", [128, 12, 2])
    ps = b.psum("ps", [128, 12, 2])
    for kb in range(8):
        b.dma(wsb[:, kb, :], w[kb * 128:(kb + 1) * 128, :], w=["w%d" % kb])
    b.dma(bsb[:], bb, w=["bsb"])
    b.dma(ccs[:], cc, w=["ccs"])
    b.act(sc[:], ccs[:], AF.Silu, ["ccs"], ["sc"])
    for cb in range(12):
        for kb in range(8):
            b.mm(ps[:, cb, :], wsb[:, kb, cb * 128:(cb + 1) * 128], sc[:, kb, :], kb == 0, kb == 7,
                 ["w%d" % kb, "sc"], ["ps"])
    b.tt("dve", osb[:], ps[:], bsb[:].unsqueeze(2).to_broadcast([128, 12, 2]), ALU.add, ["ps", "bsb"], ["osb"])
    b.dma(o, osb[:], r=["osb"])
    return b


def host_L0(inp):
    b = build_L0()
    w_ada = inp["w_ada"]
    b_ada = inp["b_ada"]
    cc = np.stack([vec_pb(inp["c"][0], 8), vec_pb(inp["c_ctx"], 8)], axis=-1)
    maps = []
    for k in range(NC):
        i, q = k // 4, k % 4
        maps.append({"w": np.ascontiguousarray(w_ada[i][:, q * 1536:(q + 1) * 1536]),
                     "b": vec_pb(b_ada[i][q * 1536:(q + 1) * 1536], 12), "cc": cc})
    res = run(b, maps)
    mods = np.zeros((2, 6144, 2), np.float32)
    for k in range(NC):
        i, q = k // 4, k % 4
        o = res[k]["o"]
        mods[i, q * 1536:(q + 1) * 1536, :] = o.transpose(1, 0, 2).reshape(1536, 2)
    return mods


def mod_vec(mods, layer, which, j):
    return vec_pb(mods[layer, which * D:(which + 1) * D, j], 8)


def rms_rstd(b, h_ap_fn, nblk, n, onesD, sq, pst, rstd, tmp, rkeys, tag, inv_dim_in_ones=True):
    for kb in range(nblk):
        b.act(sq[:, kb, :n], h_ap_fn(kb), AF.Square, rkeys, [tag + "sq%d" % kb])
    for kb in range(nblk):
        b.mm(pst[:, :n], onesD[:], sq[:, kb, :n], kb == 0, kb == nblk - 1, [tag + "sq%d" % kb, "onesD"], [tag + "pst"])
    b.act(tmp[:, :n], pst[:, :n], AF.Sqrt, [tag + "pst", "epsb"], [tag + "tmp"], bias=b.epsb[:, 0:1])
    b.recip(rstd[:, :n], tmp[:, :n], [tag + "tmp"], [tag + "rstd"])


def make_consts(b, dim):
    onesD = b.sb("onesD", [128, 128], BF16)
    b.memset("pool", onesD[:], 1.0 / dim, ["onesD"])
    epsb = b.sb("epsb", [128, 1])
    b.memset("pool", epsb[:], EPS, ["epsb"])
    b.epsb = epsb
    return onesD


def load_weight_bf16(b, dst, src, nkb, ncols, stage, tag, eng_cycle=("act", "pool")):
    for kb in range(nkb):
        st = stage[kb % len(stage)]
        sk = tag + "st%d" % (kb % len(stage))
        b.dma(st[:, :ncols], src[kb * 128:(kb + 1) * 128, :], w=[sk])
        b.copy(eng_cycle[kb % len(eng_cycle)], dst[:, kb, :], st[:, :ncols], [sk], [tag + "w%d" % kb])


def build_LA():
    b = B()
    xT = b.din("xT", [D, TPC])
    ridx = b.din("ridx", [128, 32])
    cidx = b.din("cidx", [128, 64])
    cxT = b.din("cxT", [D, 32])
    pv = b.din("pv", [128, 5, 8])
    w_in = b.din("w_in", [D, 1536])
    hT = b.dout("hT", [D, TPC])
    uT = b.dout("uT", [512, TPC])
    vbT = b.dout("vbT", [512, TPC])
    ucT = b.dout("ucT", [512, 32])

    onesD = make_consts(b, D)
    h = b.sb("h", [128, 8, TPC])
    ri = b.sb("ri", [128, 32])
    ci = b.sb("ci", [128, 64])
    hc = b.sb("hc", [128, 8, 32])
    pvs = b.sb("pvs", [128, 5, 8])
    gm = b.sb("gm", [128, 2, 8])
    win = b.sb("win", [128, 8, 1536], BF16)
    stage = [b.sb("stg%d" % i, [128, 1536]) for i in range(2)]
    def load_x(c):
        sl_ = slice(c * CW, (c + 1) * CW)
        b.dma(h[:, :, sl_], xT[:, sl_].rearrange("(kb p) n -> p kb n", p=128), w=["h%d_%d" % (kb, c) for kb in range(8)])
    load_x(0)
    b.dma(ri[:], ridx, w=["ri"])
    b.dma(ci[:], cidx, w=["ci"])
    b.dma(hc[:], cxT.rearrange("(kb p) n -> p kb n", p=128), w=["hc"])
    b.dma(pvs[:], pv, w=["pvs"])
    load_weight_bf16(b, win, w_in, 8, 1536, stage, "win")
    for c in range(1, NCHUNK):
        load_x(c)
    b.ts("dve", gm[:, 0, :], pvs[:, 2, :], 1.0, ALU.add, ["pvs"], ["gm0a"])
    b.tt("dve", gm[:, 0, :], gm[:, 0, :], pvs[:, 0, :], ALU.mult, ["gm0a", "pvs"], ["gm0"])
    b.ts("dve", gm[:, 1, :], pvs[:, 4, :], 1.0, ALU.add, ["pvs"], ["gm1a"])
    b.tt("dve", gm[:, 1, :], gm[:, 1, :], pvs[:, 0, :], ALU.mult, ["gm1a", "pvs"], ["gm1"])
    ki0 = b.sb("ki0", [128, 2], I32)
    kf0 = b.sb("kf0", [128, 2])
    om = b.sb("om", [128, 2])
    b.iota(ki0[:], [[128, 2]], 0, 1, ["ki0"])
    b.copy("dve", kf0[:], ki0[:], ["ki0"], ["kf0"])
    b.act(om[:], kf0[:], AF.Exp, ["kf0"], ["om"], scale=-math.log(10000.0) / 256.0)

    tki = b.sb("t_ki", [128, 64], I32); tkf = b.sb("t_kf", [128, 64]); tra = b.sb("t_ra", [128, 64]); trb = b.sb("t_rb", [128, 64])
    ang = b.sb("ang", [128, 64])
    rowtab = b.sb("rowtab", [128, 4, 32])
    coltab = b.sb("coltab", [128, 4, 64])
    for blk in range(4):
        j = blk % 2
        ph = PI / 2 if blk >= 2 else 0.0
        for (idx, ik, n_, tab, tk) in ((ri, "ri", 32, rowtab, "rowtab"), (ci, "ci", 64, coltab, "coltab")):
            tmp = {"ki": tki[:, :n_], "kf": tkf[:, :n_], "ra": tra[:, :n_], "rb": trb[:, :n_], "key": "t_"}
            b.ts("dve", ang[:, :n_], idx[:], om[:, j:j + 1], ALU.mult, [ik, "om"], ["ang"], s2=ph, op1=ALU.add)
            b.sin_of(tab[:, blk, :], ang[:, :n_], None, tmp, ["ang"], [tk])
    sqs = [b.sb("sq%d" % i, [128, 8, CW], BF16) for i in range(2)]
    sq = sqs[0]
    pst = b.psum("pst")
    rstds = [b.sb("rstd%d" % i, [128, CW]) for i in range(2)]
    rstd = rstds[0]
    rt = b.sb("rt", [128, CW])
    hn = b.sb("hn", [128, CW])
    ns = [b.sb("n%d" % i, [128, 8, CW], BF16) for i in range(2)]
    n = ns[0]
    PS = [b.psum("ps%d" % i) for i in range(4)]
    uo = b.sb("uo", [128, 4, CW])
    vo = b.sb("vo", [128, 4, CW])
    sig = b.sb("sig", [128, CW])
    psn = 0

    for c in range(NCHUNK):
        sl = slice(c * CW, (c + 1) * CW)
        b.tt("pool", h[:, 0:4, sl].rearrange("p b (r c) -> p b r c", c=64), h[:, 0:4, sl].rearrange("p b (r c) -> p b r c", c=64),
             rowtab[:, :, 8 * c:8 * c + 8].unsqueeze(3).to_broadcast([128, 4, 8, 64]), ALU.add,
             ["h%d_%d" % (k, c) for k in range(4)] + ["rowtab"], ["h%d_%d" % (k, c) for k in range(4)])
        b.tt("dve", h[:, 4:8, sl].rearrange("p b (r c) -> p b r c", c=64), h[:, 4:8, sl].rearrange("p b (r c) -> p b r c", c=64),
             coltab[:].unsqueeze(2).to_broadcast([128, 4, 8, 64]), ALU.add,
             ["h%d_%d" % (k, c) for k in range(4, 8)] + ["coltab"], ["h%d_%d" % (k, c) for k in range(4, 8)])
        b.dma(hT[:, sl].rearrange("(kb p) n -> p kb n", p=128), h[:, :, sl], r=["h%d_%d" % (k, c) for k in range(8)])
        pc_ = c % 2
        n = ns[pc_]; sq = sqs[pc_]; rstd = rstds[pc_]
        tg = "A%d" % pc_
        rms_rstd(b, lambda kb: h[:, kb, sl], 8, CW, onesD, sq, pst, rstd, rt, ["h%d_%d" % (k, c) for k in range(8)], tg)
        for kb in range(8):
            b.tt("dve", hn[:], h[:, kb, sl], rstd[:], ALU.mult, ["h%d_%d" % (kb, c), tg + "rstd"], ["hn"])
            b.act(n[:, kb, :], hn[:], AF.Identity, ["hn", "gm0", "pvs"], ["n%d_%d" % (pc_, kb)],
                  bias=pvs[:, 1, kb:kb + 1], scale=gm[:, 0, kb:kb + 1])
        nkeys = ["n%d_%d" % (pc_, k) for k in range(8)]
        wkeys = ["winw%d" % k for k in range(8)]
        for ob in range(4):
            ps = PS[psn % 4]; pk = "ps%d" % (psn % 4); psn += 1
            for kb in range(8):
                b.mm(ps[:], win[:, kb, ob * 128:(ob + 1) * 128], n[:, kb, :], kb == 0, kb == 7, nkeys + wkeys, [pk])
            b.copy("act", uo[:, ob, :], ps[:], [pk], ["uo"])
        b.dma(uT[:, sl].rearrange("(ob p) n -> p ob n", p=128), uo[:], r=["uo"])
        for jv in range(4):
            psg = PS[psn % 4]; pkg = "ps%d" % (psn % 4); psn += 1
            for kb in range(8):
                b.mm(psg[:], win[:, kb, (8 + jv) * 128:(9 + jv) * 128], n[:, kb, :], kb == 0, kb == 7, nkeys + wkeys, [pkg])
            b.act(sig[:], psg[:], AF.Sigmoid, [pkg], ["sig"])
            psv = PS[psn % 4]; pkv = "ps%d" % (psn % 4); psn += 1
            for kb in range(8):
                b.mm(psv[:], win[:, kb, (4 + jv) * 128:(5 + jv) * 128], n[:, kb, :], kb == 0, kb == 7, nkeys + wkeys, [pkv])
            b.tt("dve", vo[:, jv, :], psv[:], sig[:], ALU.mult, [pkv, "sig"], ["vo"])
        b.dma(vbT[:, sl].rearrange("(ob p) n -> p ob n", p=128), vo[:], r=["vo"])
    n = ns[0]; sq = sqs[0]; rstd = rstds[0]
    rms_rstd(b, lambda kb: hc[:, kb, :], 8, 32, onesD, sq, pst, rstd, rt, ["hc"], "A0")
    for kb in range(8):
        b.tt("dve", hn[:, :32], hc[:, kb, :], rstd[:, :32], ALU.mult, ["hc", "A0rstd"], ["hn"])
        b.act(n[:, kb, :32], hn[:, :32], AF.Identity, ["hn", "gm1", "pvs"], ["n0_%d" % kb],
              bias=pvs[:, 3, kb:kb + 1], scale=gm[:, 1, kb:kb + 1])
    for ob in range(4):
        ps = PS[psn % 4]; pk = "ps%d" % (psn % 4); psn += 1
        for kb in range(8):
            b.mm(ps[:, :32], win[:, kb, ob * 128:(ob + 1) * 128], n[:, kb, :32], kb == 0, kb == 7,
                 ["n0_%d" % k for k in range(8)] + ["winw%d" % k for k in range(8)], [pk])
        b.copy("act", uo[:, ob, :32], ps[:, :32], [pk], ["uo"])
    b.dma(ucT.rearrange("(ob p) n -> p ob n", p=128), uo[:, :, :32], r=["uo"])
    return b


def host_LA(inp, mods):
    b = build_LA()
    x = inp["x"][0]
    ctx = inp["ctx"][0]
    pv = np.stack([vec_pb(inp["norm_mix_g"][0], 8), mod_vec(mods, 0, 0, 0), mod_vec(mods, 0, 1, 0),
                   mod_vec(mods, 0, 0, 1), mod_vec(mods, 0, 1, 1)], axis=1)
    w_in = np.ascontiguousarray(inp["w_in"][0])
    tok = np.arange(L)
    maps = []
    for k in range(NC):
        t = tok[k * TPC:(k + 1) * TPC]
        maps.append({
            "xT": np.ascontiguousarray(x[k * TPC:(k + 1) * TPC].T),
            "ridx": np.ascontiguousarray(np.broadcast_to((t[::64] // 64).astype(np.float32), (128, 32))),
            "cidx": np.ascontiguousarray(np.broadcast_to(np.arange(64, dtype=np.float32), (128, 64))),
            "cxT": np.ascontiguousarray(ctx[k * 32:(k + 1) * 32].T),
            "pv": np.ascontiguousarray(pv), "w_in": w_in})
    res = run(b, maps)
    hT = [r["hT"] for r in res]
    u = np.concatenate([r["uT"].T for r in res], axis=0)
    vb = np.concatenate([r["vbT"].T for r in res], axis=0)
    uc = np.concatenate([r["ucT"].T for r in res], axis=0)
    return hT, u, vb, uc


NSS = (CTX + L) // 8
NXS = L // 8
LB_CH = [(0, 32)] + [(32 + i * 512, 512) for i in range(4)]


def reduce_ang(b, out, ang, tmp, r, w):
    ki, kf, rb = tmp["ki"], tmp["kf"], tmp["rb"]
    tk = tmp["key"]
    ra = out
    b.ts("dve", ki, ang, 1.0 / TWO_PI, ALU.mult, r, [tk + "ki"])
    b.copy("dve", kf, ki, [tk + "ki"], [tk + "kf"])
    b.stt("dve", ra, kf, -C1, ang, ALU.mult, ALU.add, r + [tk + "kf"], w)
    b.stt("dve", rb, kf, -C2, ra, ALU.mult, ALU.add, [tk + "kf"] + w, [tk + "rb"])
    b.ts("dve", kf, rb, PI, ALU.is_gt, [tk + "rb"], [tk + "kf"], s2=-TWO_PI, op1=ALU.mult)
    b.tt("dve", ra, rb, kf, ALU.add, [tk + "rb", tk + "kf"], w)
    b.ts("dve", kf, ra, -PI, ALU.is_lt, w, [tk + "kf"], s2=TWO_PI, op1=ALU.mult)
    b.tt("dve", rb, ra, kf, ALU.add, w + [tk + "kf"], [tk + "rb"])
    b.ts("dve", ra, rb, -PI, ALU.max, [tk + "rb"], w, s2=PI, op1=ALU.min)


def build_LB():
    b = B()
    U = b.din("U", [8, 128, NSS])
    p_lre = b.din("lamre", [128, 8]); p_lim = b.din("lamim", [128, 8]); p_ls = b.din("lstep", [128, 8])
    p_bre = b.din("bre", [128, 8, 16]); p_bim = b.din("bim", [128, 8, 16])
    p_cre = b.din("cre", [128, 8, 16]); p_cim = b.din("cim", [128, 8, 16])
    p_mF = b.din("maskF", [128, 128]); p_mB = b.din("maskB", [128, 128]); p_id = b.din("ident", [128, 128])
    Y = b.dout("Y", [8, 128, NXS])

    def T(name, shape, dt=F32):
        return b.sb("s_" + name, shape, dt)

    lre = T("lre", [128, 8]); lim = T("lim", [128, 8]); ls = T("ls", [128, 8])
    bre = T("bre", [128, 8, 16]); bim = T("bim", [128, 8, 16]); cre = T("cre", [128, 8, 16]); cim = T("cim", [128, 8, 16])
    mF = T("mF", [128, 128]); mB = T("mB", [128, 128]); ident = T("ident", [128, 128])
    for t, src, k in [(lre, p_lre, "lre"), (lim, p_lim, "lim"), (ls, p_ls, "ls"), (bre, p_bre, "bre"), (bim, p_bim, "bim"),
                      (cre, p_cre, "cre"), (cim, p_cim, "cim"), (mF, p_mF, "mF"), (mB, p_mB, "mB"), (ident, p_id, "ident")]:
        b.dma(t[:], src, w=[k])
    dt_ = T("dt", [128, 8]); lr = T("lr", [128, 8]); a = T("a", [128, 8]); th = T("th", [128, 8])
    b.act(dt_[:], ls[:], AF.Exp, ["ls"], ["dt"])
    b.ts("dve", lr[:], lre[:], -1e-4, ALU.min, ["lre"], ["lr"])
    b.tt("dve", a[:], lr[:], dt_[:], ALU.mult, ["lr", "dt"], ["a"])
    b.tt("dve", th[:], lim[:], dt_[:], ALU.mult, ["lim", "dt"], ["th"])
    kiA = T("kiA", [128, 16], I32); kiD = T("kiD", [128, 16], I32); kA = T("kA", [128, 16]); kD = T("kD", [128, 16])
    b.iota(kiA[:], [[1, 16]], -7, 0, ["kiA"]); b.iota(kiD[:], [[-1, 16]], 8, 0, ["kiD"])
    b.copy("dve", kA[:], kiA[:], ["kiA"], ["kA"]); b.copy("dve", kD[:], kiD[:], ["kiD"], ["kD"])
    S3 = [128, 8, 16]
    tmp3 = {"ki": T("p_ki", S3, I32)[:], "kf": T("p_kf", S3)[:], "ra": T("p_ra", S3)[:], "rb": T("p_rb", S3)[:], "key": "p_"}
    ak = T("ak", S3); tk_ = T("tk", S3); mag = T("mag", S3); sn = T("sn", S3); cs = T("cs", S3)
    PW = {}
    for nm, kv in (("A", kA), ("D", kD)):
        kb_ = kv[:].unsqueeze(1).to_broadcast(S3)
        b.tt("dve", ak[:], a[:].unsqueeze(2).to_broadcast(S3), kb_, ALU.mult, ["a", "k" + nm], ["ak"])
        b.act(mag[:], ak[:], AF.Exp, ["ak"], ["mag"])
        b.tt("dve", tk_[:], th[:].unsqueeze(2).to_broadcast(S3), kb_, ALU.mult, ["th", "k" + nm], ["tk"])
        b.sin_of(sn[:], tk_[:], S3, tmp3, ["tk"], ["sn"])
        b.ts("dve", tk_[:], tk_[:], PI / 2, ALU.add, ["tk"], ["tk"])
        b.sin_of(cs[:], tk_[:], S3, tmp3, ["tk"], ["cs"])
        pr = T("PWr" + nm, S3); pi_ = T("PWi" + nm, S3)
        b.tt("dve", pr[:], mag[:], cs[:], ALU.mult, ["mag", "cs"], ["PWr" + nm])
        b.tt("dve", pi_[:], mag[:], sn[:], ALU.mult, ["mag", "sn"], ["PWi" + nm])
        PW[nm] = (pr, pi_, "PWr" + nm, "PWi" + nm)
    l1r = PW["A"][0][:, :, 8]; l1i = PW["A"][1][:, :, 8]
    nr = T("nr", [128, 8]); t1 = T("t1", [128, 8]); t2 = T("t2", [128, 8]); rden = T("rden", [128, 8])
    wr = T("wr", [128, 8]); wi = T("wi", [128, 8])
    b.ts("dve", nr[:], l1r, -1.0, ALU.add, ["PWrA"], ["nr"])
    b.tt("dve", t1[:], lr[:], lr[:], ALU.mult, ["lr"], ["t1"])
    b.tt("dve", t2[:], lim[:], lim[:], ALU.mult, ["lim"], ["t2"])
    b.tt("dve", t1[:], t1[:], t2[:], ALU.add, ["t1", "t2"], ["t1"])
    b.recip(rden[:], t1[:], ["t1"], ["rden"])
    b.tt("dve", t1[:], nr[:], lr[:], ALU.mult, ["nr", "lr"], ["t1"])
    b.tt("dve", t2[:], l1i, lim[:], ALU.mult, ["PWiA", "lim"], ["t2"])
    b.tt("dve", t1[:], t1[:], t2[:], ALU.add, ["t1", "t2"], ["t1"])
    b.tt("dve", wr[:], t1[:], rden[:], ALU.mult, ["t1", "rden"], ["wr"])
    b.tt("dve", t1[:], l1i, lr[:], ALU.mult, ["PWiA", "lr"], ["t1"])
    b.tt("dve", t2[:], nr[:], lim[:], ALU.mult, ["nr", "lim"], ["t2"])
    b.tt("dve", t1[:], t1[:], t2[:], ALU.subtract, ["t1", "t2"], ["t1"])
    b.tt("dve", wi[:], t1[:], rden[:], ALU.mult, ["t1", "rden"], ["wi"])
    bbr = T("bbr", S3); bbi = T("bbi", S3); t3 = T("t3", S3); t4 = T("t4", S3)
    wrb = wr[:].unsqueeze(2).to_broadcast(S3); wib = wi[:].unsqueeze(2).to_broadcast(S3)
    b.tt("dve", t3[:], wrb, bre[:], ALU.mult, ["wr", "bre"], ["t3"])
    b.tt("dve", t4[:], wib, bim[:], ALU.mult, ["wi", "bim"], ["t4"])
    b.tt("dve", bbr[:], t3[:], t4[:], ALU.subtract, ["t3", "t4"], ["bbr"])
    b.tt("dve", t3[:], wrb, bim[:], ALU.mult, ["wr", "bim"], ["t3"])
    b.tt("dve", t4[:], wib, bre[:], ALU.mult, ["wi", "bre"], ["t4"])
    b.tt("dve", bbi[:], t3[:], t4[:], ALU.add, ["t3", "t4"], ["bbi"])

    S4 = [128, 4, 8, 16]
    o1 = T("o1", S4); o2 = T("o2", S4); oR = T("oR", S4); oI = T("oI", S4)

    def cplx_outer(nm, ksl, d, Vr, Vi, vkeys):
        pr, pi_, kr, ki_ = PW[nm]
        dsl = slice(4 * d, 4 * d + 4)
        Pr = pr[:, dsl, ksl].unsqueeze(3).to_broadcast(S4); Pi = pi_[:, dsl, ksl].unsqueeze(3).to_broadcast(S4)
        vr = Vr[:, dsl, :].unsqueeze(2).to_broadcast(S4); vi = Vi[:, dsl, :].unsqueeze(2).to_broadcast(S4)
        b.tt("dve", o1[:], Pr, vr, ALU.mult, [kr, vkeys[0]], ["o1"])
        b.tt("dve", o2[:], Pi, vi, ALU.mult, [ki_, vkeys[1]], ["o2"])
        b.tt("dve", oR[:], o1[:], o2[:], ALU.subtract, ["o1", "o2"], ["oR"])
        b.tt("dve", o1[:], Pr, vi, ALU.mult, [kr, vkeys[1]], ["o1"])
        b.tt("dve", o2[:], Pi, vr, ALU.mult, [ki_, vkeys[0]], ["o2"])
        b.tt("dve", oI[:], o1[:], o2[:], ALU.add, ["o1", "o2"], ["oI"])

    def halves(dst, top, bot, neg_top, neg_bot, rk, wk):
        for (lo, hi, src, neg) in ((0, 64, top, neg_top), (64, 128, bot, neg_bot)):
            if neg:
                b.ts("dve", dst[lo:hi], src[lo:hi], -1.0, ALU.mult, rk, [wk])
            else:
                b.copy("dve", dst[lo:hi], src[lo:hi], rk, [wk])

    S4m = [128, 4, 128]
    BT1 = T("BT1", S4); BT2 = T("BT2", S4); TL = T("TL", S4); TR = T("TR", S4)
    Bc1 = T("Bc1", [128, 8, 128], BF16); Bc2 = T("Bc2", [128, 8, 128], BF16)
    CcT = T("CcT", [128, 8, 8, 16], BF16); Toep = T("Toep", [128, 8, 128], BF16)
    pT = b.psum("pT", [128, 128])
    for d in (0, 1):
        if d == 0:
            cplx_outer("D", slice(1, 9), 0, bbr, bbi, ["bbr", "bbi"])
        else:
            cplx_outer("A", slice(7, 15), 1, bbr, bbi, ["bbr", "bbi"])
        halves(BT1, oR, oI, False, False, ["oR", "oI"], "BT1")
        halves(BT2, oI, oR, True, False, ["oR", "oI"], "BT2")
        if d == 0:
            cplx_outer("D", slice(8, 16), 0, bbr, bbi, ["bbr", "bbi"])
            halves(TL, oR, oI, False, False, ["oR", "oI"], "TL")
        else:
            b.copy("dve", TL[:], BT1[:], ["BT1"], ["TL"])
        for gl in range(4):
            q = d * 4 + gl
            for (src, dst, sk, dk) in ((BT1, Bc1, "BT1", "Bc1"), (BT2, Bc2, "BT2", "Bc2")):
                b.transpose(pT[:], src[:, gl].rearrange("p a b -> p (a b)"), ident[:], [sk, "ident"], ["pT"])
                b.copy("act", dst[:, q, :], pT[:], ["pT"], [dk])
        if d == 0:
            cplx_outer("A", slice(8, 16), 0, cre, cim, ["cre", "cim"])
        else:
            cplx_outer("D", slice(0, 8), 1, cre, cim, ["cre", "cim"])
        halves(CcT[:, 4 * d:4 * d + 4], oR, oI, False, True, ["oR", "oI"], "CcT")
        if d == 0:
            cplx_outer("A", slice(7, 15), 0, cre, cim, ["cre", "cim"])
        else:
            cplx_outer("D", slice(8, 16), 1, cre, cim, ["cre", "cim"])
        halves(TR, oR, oI, False, True, ["oR", "oI"], "TR")
        for gl in range(4):
            q = d * 4 + gl
            b.mm(pT[:], TL[:, gl].rearrange("p a b -> p (a b)"), TR[:, gl].rearrange("p a b -> p (a b)"), True, True,
                 ["TL", "TR"], ["pT"])
            b.tt("dve", Toep[:, q, :], pT[:], (mF if d == 0 else mB)[:], ALU.mult, ["pT", "mF", "mB"], ["Toep"])
    r8 = T("r8", [128, 8]); th8 = T("th8", [128, 8]); th8r = T("th8r", [128, 8])
    b.act(r8[:], a[:], AF.Exp, ["a"], ["r8"], scale=8.0)
    b.ts("dve", th8[:], th[:], 8.0, ALU.mult, ["th"], ["th8"])
    tmp2 = {"ki": T("q_ki", [128, 8], I32)[:], "kf": T("q_kf", [128, 8])[:], "rb": T("q_rb", [128, 8])[:], "key": "q_"}
    reduce_ang(b, th8r[:], th8[:], tmp2, ["th8"], ["th8r"])
    NT = 33 * 64
    th64 = T("th64", [128, 8]); th64r = T("th64r", [128, 8])
    b.ts("dve", th64[:], th8r[:], 64.0, ALU.mult, ["th8r"], ["th64"])
    reduce_ang(b, th64r[:], th64[:], tmp2, ["th64"], ["th64r"])
    bvi = T("bvi", [128, 64], I32); bv = T("bv", [128, 64])
    b.iota(bvi[:], [[1, 64]], 0, 0, ["bvi"])
    b.copy("dve", bv[:], bvi[:], ["bvi"], ["bv"])
    SB_ = [128, 8, 64]; SA_ = [128, 8, 33]
    tmpB = {"ki": T("b_ki", SB_, I32), "kf": T("b_kf", SB_), "ra": T("b_ra", SB_), "rb": T("b_rb", SB_)}
    angB = T("angB", SB_); sB = T("sB", SB_); cB = T("cB", SB_); sA = T("sA", SA_); cA = T("cA", SA_)

    def small_tab(thv, thk, n_, sT, cT, nm):
        tm = {"ki": tmpB["ki"][:, :, :n_], "kf": tmpB["kf"][:, :, :n_], "ra": tmpB["ra"][:, :, :n_], "rb": tmpB["rb"][:, :, :n_], "key": "b_"}
        sh = [128, 8, n_]
        b.tt("dve", angB[:, :, :n_], thv[:].unsqueeze(2).to_broadcast(sh), bv[:, :n_].unsqueeze(1).to_broadcast(sh), ALU.mult,
             [thk, "bv"], ["angB"])
        b.sin_of(sT[:], angB[:, :, :n_], None, tm, ["angB"], ["s" + nm])
        b.ts("dve", angB[:, :, :n_], angB[:, :, :n_], PI / 2, ALU.add, ["angB"], ["angB"])
        b.sin_of(cT[:], angB[:, :, :n_], None, tm, ["angB"], ["c" + nm])
    small_tab(th8r, "th8r", 64, sB, cB, "B")
    small_tab(th64r, "th64r", 33, sA, cA, "A")
    sinTs = [T("sinT%d" % i, [128, NT]) for i in range(2)]
    cosTs = [T("cosT%d" % i, [128, NT]) for i in range(2)]
    e1 = T("e1", [128, NT]); e2 = T("e2", [128, NT])
    Uf = [T("Uf%d" % i, [128, NSS]) for i in range(2)]
    Ub = [T("Ub%d" % i, [128, NSS], BF16) for i in range(2)]
    sx1 = T("sx1", [128, NT]); sx2 = T("sx2", [128, NT])
    mm_ = [[T("m%d_%d" % (i, p_), [128, 512]) for i in range(4)] for p_ in range(2)]
    xt1s = [T("xt1_%d" % p_, [128, 512]) for p_ in range(2)]; xt2s = [T("xt2_%d" % p_, [128, 512]) for p_ in range(2)]
    Shs = [T("Sh%d" % p_, [128, 512], BF16) for p_ in range(2)]
    yo = [T("yo%d" % i, [128, 512]) for i in range(2)]
    PX = [b.psum("px%d" % i) for i in range(4)]
    PY = [b.psum("py%d" % i) for i in range(2)]
    yn = 0
    for q in range(8):
        ub, uf = Ub[q % 2], Uf[q % 2]
        uk = "Ub%d" % (q % 2)
        b.dma(uf[:], U[q], w=["Uf%d" % (q % 2)])
        b.copy("act", ub[:], uf[:], ["Uf%d" % (q % 2)], [uk])
        sinT = sinTs[q % 2]; cosT = cosTs[q % 2]
        skq = "sinT%d" % (q % 2); ckq = "cosT%d" % (q % 2)
        S3_ = [128, 33, 64]
        cAq = cA[:, q, :].unsqueeze(2).to_broadcast(S3_); sAq = sA[:, q, :].unsqueeze(2).to_broadcast(S3_)
        cBq = cB[:, q, :].unsqueeze(1).to_broadcast(S3_); sBq = sB[:, q, :].unsqueeze(1).to_broadcast(S3_)
        v3 = lambda t: t[:].rearrange("p (a c) -> p a c", c=64)
        b.tt("dve", v3(e1), cAq, cBq, ALU.mult, ["cA", "cB"], ["e1"])
        b.tt("dve", v3(e2), sAq, sBq, ALU.mult, ["sA", "sB"], ["e2"])
        b.tt("pool", cosT[:], e1[:], e2[:], ALU.subtract, ["e1", "e2"], [ckq])
        b.tt("dve", v3(e1), sAq, cBq, ALU.mult, ["sA", "cB"], ["e1"])
        b.tt("dve", v3(e2), cAq, sBq, ALU.mult, ["cA", "sB"], ["e2"])
        b.tt("pool", sinT[:], e1[:], e2[:], ALU.add, ["e1", "e2"], [skq])
        b.memset("pool", sx1[:, 0:1], 0.0, ["sx1"])
        b.memset("pool", sx2[:, 0:1], 0.0, ["sx2"])
        for ci, (c0, n) in enumerate(LB_CH):
            X1, X2 = PX[(2 * ci) % 4], PX[(2 * ci + 1) % 4]
            k1, k2 = "px%d" % ((2 * ci) % 4), "px%d" % ((2 * ci + 1) % 4)
            b.mm(X1[:, :n], Bc1[:, q, :], ub[:, c0:c0 + n], True, True, ["Bc1", uk], [k1])
            b.mm(X2[:, :n], Bc2[:, q, :], ub[:, c0:c0 + n], True, True, ["Bc2", uk], [k2])
            cD = cosT[:, c0 + 1:c0 + n + 1]; sD = sinT[:, c0 + 1:c0 + n + 1]
            pp_ = (q * len(LB_CH) + ci) % 2
            m = mm_[pp_]; xt1 = xt1s[pp_]; xt2 = xt2s[pp_]; Sh = Shs[pp_]
            mk = ["m%d_%d" % (i, pp_) for i in range(4)]
            x1k, x2k, shk = "xt1_%d" % pp_, "xt2_%d" % pp_, "Sh%d" % pp_
            b.tt("dve", m[0][:, :n], cD, X1[:, :n], ALU.mult, [ckq, k1], [mk[0]])
            b.tt("dve", m[1][:, :n], sD, X2[:, :n], ALU.mult, [skq, k2], [mk[1]])
            b.tt("dve", m[2][:, :n], cD, X2[:, :n], ALU.mult, [ckq, k2], [mk[2]])
            b.tt("dve", m[3][:, :n], sD, X1[:, :n], ALU.mult, [skq, k1], [mk[3]])
            b.tt("pool", xt1[:, :n], m[0][:, :n], m[1][:, :n], ALU.subtract, [mk[0], mk[1]], [x1k])
            b.tt("pool", xt2[:, :n], m[2][:, :n], m[3][:, :n], ALU.add, [mk[2], mk[3]], [x2k])
            r8b = r8[:, q:q + 1].to_broadcast([128, n])
            b.scan(sx1[:, c0 + 1:c0 + n + 1], r8b, xt1[:, :n], sx1[:, c0:c0 + 1], ["r8", x1k, "sx1"], ["sx1"])
            b.scan(sx2[:, c0 + 1:c0 + n + 1], r8b, xt2[:, :n], sx2[:, c0:c0 + 1], ["r8", x2k, "sx2"], ["sx2"])
            if c0 < 32:
                continue
            b.tt("dve", m[0][:, :n], cosT[:, c0:c0 + n], sx1[:, c0:c0 + n], ALU.mult, [ckq, "sx1"], [mk[0]])
            b.tt("dve", m[1][:, :n], sinT[:, c0:c0 + n], sx2[:, c0:c0 + n], ALU.mult, [skq, "sx2"], [mk[1]])
            b.tt("pool", Sh[:, :n], m[0][:, :n], m[1][:, :n], ALU.add, [mk[0], mk[1]], [shk])
            py = PY[yn % 2]; pk = "py%d" % (yn % 2); yt = yo[yn % 2]; yk = "yo%d" % (yn % 2); yn += 1
            b.mm(py[:, :n], Toep[:, q, :], ub[:, c0:c0 + n], True, False, ["Toep", uk], [pk])
            b.mm(py[:, :n], CcT[:, q].rearrange("p a b -> p (a b)"), Sh[:, :n], False, True, ["CcT", shk], [pk])
            b.copy("act", yt[:, :n], py[:, :n], [pk], [yk])
            b.dma(Y[q][:, c0 - 32:c0 - 32 + n], yt[:, :n], r=[yk])
    return b


def host_LB(inp, u, uc):
    b = build_LB()
    tau = np.arange(128) // 16
    maskF = (tau[None, :] >= tau[:, None]).astype(np.float32)
    maskB = (tau[:, None] >= tau[None, :]).astype(np.float32)
    ident = np.eye(128, dtype=np.float32)
    pidx = np.arange(128) % 64
    maps = []
    for k in range(NC):
        Uk = np.zeros((8, 128, NSS), np.float32)
        pr = {n: np.zeros((128, 8), np.float32) for n in ("lamre", "lamim", "lstep")}
        pb = {n: np.zeros((128, 8, 16), np.float32) for n in ("bre", "bim", "cre", "cim")}
        for d in range(2):
            for gl in range(4):
                g = 4 * k + gl
                q = d * 4 + gl
                cs = slice(g * 16, g * 16 + 16)
                if d == 0:
                    seq = np.concatenate([uc[:, cs], u[:, cs]], axis=0).reshape(NSS, 128)
                else:
                    seq = np.concatenate([u[:, cs], uc[:, cs]], axis=0).reshape(NSS, 128)[::-1]
                Uk[q] = seq.T
                pr["lamre"][:, q] = inp["s5_lam_re"][0, d, g][pidx]
                pr["lamim"][:, q] = inp["s5_lam_im"][0, d, g][pidx]
                pr["lstep"][:, q] = inp["s5_log_step"][0, d, g]
                pb["bre"][:, q, :] = inp["s5_b_re"][0, d, g][pidx, :]
                pb["bim"][:, q, :] = inp["s5_b_im"][0, d, g][pidx, :]
                pb["cre"][:, q, :] = inp["s5_c_re"][0, d, g].T[pidx, :]
                pb["cim"][:, q, :] = inp["s5_c_im"][0, d, g].T[pidx, :]
        mp = {"U": Uk, "maskF": maskF, "maskB": maskB, "ident": ident}
        mp.update(pr); mp.update(pb)
        maps.append(mp)
    res = run(b, maps)
    yA = np.zeros((L, 512), np.float32)
    yB = np.zeros((L, 512), np.float32)
    for k in range(NC):
        Yk = res[k]["Y"]
        for gl in range(4):
            g = 4 * k + gl
            cs = slice(g * 16, g * 16 + 16)
            yA[:, cs] = Yk[gl].T.reshape(NXS, 8, 16).reshape(L, 16)
            yB[:, cs] = Yk[4 + gl][:, ::-1].T.reshape(NXS, 8, 16).reshape(L, 16)
    return yA, yB


def build_LC():
    b = B()
    hT = b.din("hT", [D, TPC]); uT = b.din("uT", [512, TPC]); yAT = b.din("yAT", [512, TPC]); yBT = b.din("yBT", [512, TPC])
    vbp = b.din("vbp", [512, TPC + 30])
    g1d = b.din("g1", [128, 8]); pcd = b.din("pc", [128, 4, 4]); cwd = b.din("cw", [128, 4, 31])
    identd = b.din("ident", [128, 128])
    w_glu = b.din("w_glu", [512, 512]); w_out = b.din("w_out", [D, D])
    oT = b.dout("oT", [D, TPC])
    ones512 = make_consts(b, 512)

    def T(name, shape, dt=F32):
        return b.sb("s_" + name, shape, dt)
    g1 = T("g1", [128, 8]); pc = T("pc", [128, 4, 4]); cw = T("cw", [128, 4, 31])
    b.dma(g1[:], g1d, w=["g1"]); b.dma(pc[:], pcd, w=["pc"]); b.dma(cw[:], cwd, w=["cw"])
    ident = T("ident", [128, 128]); b.dma(ident[:], identd, w=["ident"])
    dg = T("dg", [128, 4, 31, 128], BF16)
    for j in range(4):
        for tap in range(31):
            if tap % 2 == 0:
                b.act(dg[:, j, tap, :], ident[:], AF.Identity, ["ident", "cw"], ["dg%d_%d" % (j, tap)], scale=cw[:, j, tap:tap + 1])
            else:
                b.ts("dve", dg[:, j, tap, :], ident[:], cw[:, j, tap:tap + 1], ALU.mult, ["ident", "cw"], ["dg%d_%d" % (j, tap)])
    wg = T("wg", [128, 4, 512], BF16); wo = T("wo", [128, 8, 1024], BF16)
    stage = [T("stg%d" % i, [128, 1024]) for i in range(2)]
    wgk = ["wgw%d" % k for k in range(4)]; wok = ["wow%d" % k for k in range(8)]
    hchs = [T("hch%d" % i, [128, 8, CW]) for i in range(2)]; uchs = [T("uch%d" % i, [128, 4, CW]) for i in range(1)] * 2
    yAcs = [T("yAc%d" % i, [128, 4, CW]) for i in range(1)] * 2; yBcs = [T("yBc%d" % i, [128, 4, CW]) for i in range(1)] * 2
    vbcs = [T("vbc%d" % i, [128, 4, CW + 30]) for i in range(2)]
    vbbs = [T("vbb%d" % i, [128, 4, CW + 30], BF16) for i in range(2)]
    ya = T("ya", [128, 4, CW]); ya1f = T("ya1f", [128, 4, CW]); ya1b = T("ya1b", [128, 4, CW], BF16)
    cat = T("cat", [128, 8, CW], BF16); acc = T("acc", [128, 4, CW])
    accb = T("accb", [128, 4, CW], BF16); accsq = T("accsq", [128, 4, CW], BF16)
    tA = T("tA", [128, CW]); tB = T("tB", [128, CW]); tC = T("tC", [128, CW]); tD = T("tD", [128, CW])
    tM = T("tM", [128, CW]); tR = T("tR", [128, CW])
    NPS = 4
    PS = [b.psum("ps%d" % i) for i in range(NPS)]
    PCV = [b.psum("pcv%d" % i) for i in range(2)]
    psm = b.psum("psm"); pse = b.psum("pse")
    pn = 0
    for c in range(NCHUNK):
        sl = slice(c * CW, (c + 1) * CW)
        pr_ = c % 2
        hch, uch, yAc, yBc, vbc = hchs[pr_], uchs[pr_], yAcs[pr_], yBcs[pr_], vbcs[pr_]
        HK = ["hch%d_%d" % (pr_, k) for k in range(8)]
        UK, YAK, YBK, VK = "uch0", "yAc0", "yBc0", "vbc%d" % pr_
        vbb = vbbs[pr_]; VBK = "vbb%d" % pr_
        b.dma(vbc[:], vbp[:, c * CW:c * CW + CW + 30].rearrange("(kb p) n -> p kb n", p=128), w=[VK])
        b.dma(uch[:], uT[:, sl].rearrange("(kb p) n -> p kb n", p=128), w=[UK])
        b.dma(yAc[:], yAT[:, sl].rearrange("(kb p) n -> p kb n", p=128), w=[YAK])
        b.dma(yBc[:], yBT[:, sl].rearrange("(kb p) n -> p kb n", p=128), w=[YBK])
        if c == 0:
            load_weight_bf16(b, wg, w_glu, 4, 512, stage, "wg")
            load_weight_bf16(b, wo, w_out, 8, 1024, stage, "wo")
        b.dma(hch[:], hT[:, sl].rearrange("(kb p) n -> p kb n", p=128), w=HK)
        b.copy("act", vbb[:], vbc[:], [VK], [VBK])

        def conv_mm(j):
            pcv = PCV[j % 2]; pck = "pcv%d" % (j % 2)
            for tap in range(31):
                b.mm(pcv[:], dg[:, j, tap, :], vbb[:, j, tap:tap + CW], tap == 0, tap == 30, ["dg%d_%d" % (j, tap), VBK], [pck])

        def conv_ev(j):
            pcv = PCV[j % 2]; pck = "pcv%d" % (j % 2)
            b.act(acc[:, j, :], pcv[:], AF.Identity, [pck, "pc"], ["acc%d" % j], bias=pc[:, 1, j:j + 1])
            b.act(accb[:, j, :], pcv[:], AF.Identity, [pck, "pc"], ["accb%d" % j], bias=pc[:, 1, j:j + 1])
            b.act(accsq[:, j, :], pcv[:], AF.Square, [pck, "pc"], ["accsq%d" % j], bias=pc[:, 1, j:j + 1])
        conv_mm(0); conv_mm(1)
        for j in range(4):
            b.tt("pool", tA[:], yAc[:, j, :], yBc[:, j, :], ALU.add, [YAK, YBK], ["tA"])
            b.stt("dve", ya[:, j, :], uch[:, j, :], pc[:, 0, j:j + 1], tA[:], ALU.mult, ALU.add, [UK, "pc", "tA"], ["ya%d" % j])
            b.act(tB[:], ya[:, j, :], AF.Square, ["ya%d" % j], ["tB"])
            b.ts("dve", tB[:], tB[:], 0.044715, ALU.mult, ["tB"], ["tB"], s2=1.0, op1=ALU.add)
            b.tt("dve", tB[:], tB[:], ya[:, j, :], ALU.mult, ["tB", "ya%d" % j], ["tB"])
            b.act(tC[:], tB[:], AF.Sigmoid, ["tB"], ["tC"], scale=1.5957691216057308)
            b.tt("dve", ya1f[:, j, :], ya[:, j, :], tC[:], ALU.mult, ["ya%d" % j, "tC"], ["ya1f%d" % j])
            b.copy("pool", ya1b[:, j, :], ya1f[:, j, :], ["ya1f%d" % j], ["ya1b%d" % j])
        conv_ev(0); conv_ev(1)
        conv_mm(2); conv_mm(3)
        yk = ["ya1b%d" % k for k in range(4)]
        for ob in range(4):
            ps = PS[pn % NPS]; pk = "ps%d" % (pn % NPS); pn += 1
            for kb in range(4):
                b.mm(ps[:], wg[:, kb, ob * 128:(ob + 1) * 128], ya1b[:, kb, :], kb == 0, kb == 3, wgk + yk, [pk])
            b.act(tC[:], ps[:], AF.Sigmoid, [pk], ["tC"])
            b.tt("dve", cat[:, ob, :], ya1f[:, ob, :], tC[:], ALU.mult, ["ya1f%d" % ob, "tC"], ["cat%d" % ob])
        conv_ev(2); conv_ev(3)
        for j in range(4):
            b.mm(psm[:], ones512[:], accb[:, j, :], j == 0, j == 3, ["onesD", "accb%d" % j], ["psm"])
        for j in range(4):
            b.mm(pse[:], ones512[:], accsq[:, j, :], j == 0, j == 3, ["onesD", "accsq%d" % j], ["pse"])
        b.copy("act", tM[:], psm[:], ["psm"], ["tM"])
        b.act(tD[:], psm[:], AF.Square, ["psm"], ["tD"])
        b.tt("dve", tD[:], pse[:], tD[:], ALU.subtract, ["pse", "tD"], ["tD"])
        b.act(tD[:], tD[:], AF.Sqrt, ["tD", "epsb"], ["tD"], bias=b.epsb[:, 0:1])
        b.recip(tR[:], tD[:], ["tD"], ["tR"])
        for j in range(4):
            b.tt("dve", tA[:], acc[:, j, :], tM[:], ALU.subtract, ["acc%d" % j, "tM"], ["tA"])
            b.tt("dve", tA[:], tA[:], tR[:], ALU.mult, ["tA", "tR"], ["tA"])
            b.act(cat[:, 4 + j, :], tA[:], AF.Silu, ["tA", "pc"], ["cat%d" % (4 + j)],
                  bias=pc[:, 3, j:j + 1], scale=pc[:, 2, j:j + 1])
        ck = ["cat%d" % k for k in range(8)]
        for ob in range(8):
            ps = PS[pn % NPS]; pk = "ps%d" % (pn % NPS); pn += 1
            for kb in range(8):
                b.mm(ps[:], wo[:, kb, ob * 128:(ob + 1) * 128], cat[:, kb, :], kb == 0, kb == 7, wok + ck, [pk])
            b.stt("dve", hch[:, ob, :], ps[:], g1[:, ob:ob + 1], hch[:, ob, :], ALU.mult, ALU.add,
                  [pk, "g1", HK[ob]], [HK[ob]])
        b.dma(oT[:, sl].rearrange("(kb p) n -> p kb n", p=128), hch[:], r=HK)
    return b


def fm(a):
    return np.ascontiguousarray(a.T)


def host_LC(inp, mods, hT, u, vb, yA, yB):
    b = build_LC()
    vpad = np.zeros((L + 30, 512), np.float32)
    vpad[15:15 + L] = vb
    pc = np.stack([vec_pb(inp["s5_d"][0], 4), vec_pb(inp["conv_b"][0], 4), vec_pb(inp["conv_ln_g"][0], 4),
                   vec_pb(inp["conv_ln_b"][0], 4)], axis=1)
    cw = np.ascontiguousarray(inp["conv_w"][0].T.reshape(4, 128, 31).transpose(1, 0, 2))
    g1 = mod_vec(mods, 0, 2, 0)
    maps = []
    for k in range(NC):
        ts_ = slice(k * TPC, (k + 1) * TPC)
        maps.append({"hT": hT[k], "uT": fm(u[ts_]), "yAT": fm(yA[ts_]), "yBT": fm(yB[ts_]),
                     "vbp": fm(vpad[k * TPC:(k + 1) * TPC + 30]), "g1": g1, "pc": np.ascontiguousarray(pc), "cw": cw,
                     "ident": np.eye(128, dtype=np.float32),
                     "w_glu": np.ascontiguousarray(inp["s5_w_glu"][0]), "w_out": np.ascontiguousarray(inp["w_out"][0])})
    res = run(b, maps)
    return [r["oT"] for r in res]


def build_LM(final):
    b = B()
    hT = b.din("hT", [D, TPC]); pvd = b.din("pv", [128, 4, 8]); fgd = b.din("fg", [128, 8])
    w1 = b.din("w1", [D, 4 * D]); w2 = b.din("w2", [4 * D, D])
    oT = b.dout("oT", [D, TPC])
    onesD = make_consts(b, D)

    def T(name, shape, dt=F32):
        return b.sb("s_" + name, shape, dt)
    h = T("h", [128, 8, TPC]); n = T("n", [128, 8, TPC], BF16)
    pv = T("pv", [128, 4, 8]); fg = T("fg", [128, 8]); gm = T("gm", [128, 8])
    b.dma(pv[:], pvd, w=["pv"]); b.dma(fg[:], fgd, w=["fg"])
    b.ts("dve", gm[:], pv[:, 2, :], 1.0, ALU.add, ["pv"], ["gma"])
    b.tt("dve", gm[:], gm[:], pv[:, 0, :], ALU.mult, ["gma", "pv"], ["gm"])
    sq = T("sq", [128, 8, CW], BF16); rstd = T("rstd", [128, CW]); rt = T("rt", [128, CW]); hn = T("hn", [128, CW])
    pst = b.psum("pst")

    def load_h(c):
        sl = slice(c * CW, (c + 1) * CW)
        b.dma(h[:, :, sl], hT[:, sl].rearrange("(kb p) n -> p kb n", p=128), w=["h%d_%d" % (kb, c) for kb in range(8)])

    def norm_chunk(c):
        sl = slice(c * CW, (c + 1) * CW)
        rms_rstd(b, lambda kb: h[:, kb, sl], 8, CW, onesD, sq, pst, rstd, rt, ["h%d_%d" % (k, c) for k in range(8)], "M")
        for kb in range(8):
            b.tt("dve", hn[:], h[:, kb, sl], rstd[:], ALU.mult, ["h%d_%d" % (kb, c), "Mrstd"], ["hn"])
            b.act(n[:, kb, sl], hn[:], AF.Identity, ["hn", "gm", "pv"], ["n%d_%d" % (kb, c)],
                  bias=pv[:, 1, kb:kb + 1], scale=gm[:, kb:kb + 1])
    w1g = [T("w1g%d" % i, [128, 8, 512], BF16) for i in range(2)]
    w2g = [T("w2g%d" % i, [128, 4, 1024], BF16) for i in range(2)]
    stage = [T("stg%d" % i, [128, 1024]) for i in range(3)]
    rl = [T("rl%d" % i, [128, CW]) for i in range(2)]
    h1 = [T("h1_%d" % i, [128, 4, CW], BF16) for i in range(2)]
    PH = [b.psum("ph%d" % i) for i in range(3)]
    PO = [b.psum("po%d" % i) for i in range(4)]
    cnt = {"sn": 0, "hn": 0, "on": 0}
    evt = [T("evt%d" % i, [128, CW]) for i in range(2)]

    def load_group(gi):
        par = gi % 2
        for kb in range(8):
            st = stage[cnt["sn"] % 3]; sk = "stg%d" % (cnt["sn"] % 3); cnt["sn"] += 1
            b.dma(st[:, :512], w1[kb * 128:(kb + 1) * 128, gi * 512:(gi + 1) * 512], w=[sk])
            b.copy("act" if kb % 2 == 0 else "pool", w1g[par][:, kb, :], st[:, :512], [sk], ["w1g%d_%d" % (par, kb)])
        for hb in range(4):
            st = stage[cnt["sn"] % 3]; sk = "stg%d" % (cnt["sn"] % 3); cnt["sn"] += 1
            b.dma(st[:], w2[gi * 512 + hb * 128:gi * 512 + (hb + 1) * 128, :], w=[sk])
            b.copy("act" if hb % 2 == 0 else "pool", w2g[par][:, hb, :], st[:], [sk], ["w2g%d_%d" % (par, hb)])

    def hidden(i):
        gi, c = divmod(i, NCHUNK)
        par = gi % 2; hp = i % 2
        sl = slice(c * CW, (c + 1) * CW)
        w1k = ["w1g%d_%d" % (par, k) for k in range(8)]
        for hb in range(4):
            ps = PH[cnt["hn"] % 3]; pk = "ph%d" % (cnt["hn"] % 3); cnt["hn"] += 1
            for kb in range(8):
                b.mm(ps[:], w1g[par][:, kb, hb * 128:(hb + 1) * 128], n[:, kb, sl], kb == 0, kb == 7,
                     w1k + ["n%d_%d" % (k, c) for k in range(8)], [pk])
            b.act(rl[hb % 2][:], ps[:], AF.Relu, [pk], ["rl%d" % (hb % 2)])
            b.tt("pool", h1[hp][:, hb, :], rl[hb % 2][:], rl[hb % 2][:], ALU.mult, ["rl%d" % (hb % 2)], ["h1_%d_%d" % (hp, hb)])

    def outproj(i):
        gi, c = divmod(i, NCHUNK)
        par = gi % 2; hp = i % 2
        sl = slice(c * CW, (c + 1) * CW)
        w2k = ["w2g%d_%d" % (par, k) for k in range(4)]
        h1k = ["h1_%d_%d" % (hp, k) for k in range(4)]
        for ob in range(8):
            ps = PO[cnt["on"] % 4]; pk = "po%d" % (cnt["on"] % 4); cnt["on"] += 1
            for hb in range(4):
                b.mm(ps[:], w2g[par][:, hb, ob * 128:(ob + 1) * 128], h1[hp][:, hb, :], hb == 0, hb == 3, w2k + h1k, [pk])
            b.stt("dve", h[:, ob, sl], ps[:], pv[:, 3, ob:ob + 1], h[:, ob, sl], ALU.mult, ALU.add,
                  [pk, "pv", "h%d_%d" % (ob, c)], ["h%d_%d" % (ob, c)])

    och = [T("och%d" % i, [128, 8, CW]) for i in range(2)] if final else None

    def finalize_chunk(c):
        sl = slice(c * CW, (c + 1) * CW)
        hk = ["h%d_%d" % (k, c) for k in range(8)]
        if final:
            rms_rstd(b, lambda kb: h[:, kb, sl], 8, CW, onesD, sq, pst, rstd, rt, hk, "F")
            oc = och[c % 2]
            for kb in range(8):
                b.stt("dve", oc[:, kb, :], h[:, kb, sl], fg[:, kb:kb + 1], rstd[:], ALU.mult, ALU.mult,
                      ["h%d_%d" % (kb, c), "fg", "Frstd"], ["och%d" % (c % 2)])
            b.dma(oT[:, sl].rearrange("(kb p) n -> p kb n", p=128), oc[:], r=["och%d" % (c % 2)])
        else:
            b.dma(oT[:, sl].rearrange("(kb p) n -> p kb n", p=128), h[:, :, sl], r=hk)

    NIT = 8 * NCHUNK
    load_h(0)
    load_group(0)
    for c in range(1, NCHUNK):
        load_h(c)
    norm_chunk(0)
    hidden(0)
    load_group(1)
    for c in range(1, NCHUNK):
        norm_chunk(c)
    for i in range(NIT):
        gi, c = divmod(i, NCHUNK)
        if i + 1 < NIT:
            hidden(i + 1)
        outproj(i)
        if c == NCHUNK - 1 and gi + 2 < 8:
            load_group(gi + 2)
        if gi == 7:
            finalize_chunk(c)
    return b


def host_LM(inp, mods, hT, layer):
    b = build_LM(final=(layer == 1))
    pv = np.stack([vec_pb(inp["norm_mlp_g"][layer], 8), mod_vec(mods, layer, 3, 0), mod_vec(mods, layer, 4, 0),
                   mod_vec(mods, layer, 5, 0)], axis=1)
    fg = vec_pb(inp["final_g"], 8)
    w1 = np.ascontiguousarray(inp["mlp_w1"][layer]); w2 = np.ascontiguousarray(inp["mlp_w2"][layer])
    maps = [{"hT": hT[k], "pv": np.ascontiguousarray(pv), "fg": fg, "w1": w1, "w2": w2} for k in range(NC)]
    res = run(b, maps)
    return [r["oT"] for r in res]


HALO = 8
WH = TPC + 2 * HALO
LD_CH = [(i * 512, 512) for i in range(4)] + [(2048, 16)]


def build_LD():
    b = B()
    hpT = b.din("hpT", [D, WH]); vald = b.din("valid", [128, WH]); pvd = b.din("pv", [128, 4, 8])
    ppd = b.din("pp", [128, 2, 8]); pwd = b.din("pw", [4, 256, 256])
    oT = b.dout("oT", [D, TPC])
    onesD = make_consts(b, D)

    def T(name, shape, dt=F32):
        return b.sb("s_" + name, shape, dt)
    h = T("h", [128, 8, WH]); val = T("val", [128, WH]); rstdF = T("rstdF", [128, WH])
    pv = T("pv", [128, 4, 8]); pp = T("pp", [128, 2, 8]); gm = T("gm", [128, 8]); Av = T("Av", [128, 8]); Bv = T("Bv", [128, 8])
    for kb in range(8):
        b.dma(h[:, kb, :], hpT[kb * 128:(kb + 1) * 128, :], w=["h%d" % kb])
    b.dma(val[:], vald, w=["val"]); b.dma(pv[:], pvd, w=["pv"]); b.dma(pp[:], ppd, w=["pp"])
    pwb = T("pwb", [128, 4, 2, 256], BF16)
    stage = [T("stg%d" % i, [128, 256]) for i in range(2)]
    sn = 0
    for gi in range(4):
        for kbl in range(2):
            st = stage[sn % 2]; sk = "stg%d" % (sn % 2); sn += 1
            b.dma(st[:], pwd[gi][kbl * 128:(kbl + 1) * 128, :], w=[sk])
            b.copy("act", pwb[:, gi, kbl, :], st[:], [sk], ["pwb"])
    b.ts("dve", gm[:], pv[:, 2, :], 1.0, ALU.add, ["pv"], ["gma"])
    b.tt("dve", gm[:], gm[:], pv[:, 0, :], ALU.mult, ["gma", "pv"], ["gm"])
    b.tt("dve", Av[:], pp[:, 1, :], pv[:, 3, :], ALU.mult, ["pp", "pv"], ["Av"])
    b.tt("dve", Bv[:], pp[:, 0, :], Av[:], ALU.mult, ["pp", "Av"], ["Bv"])
    sq = T("sq", [128, 8, CW], BF16); rstd = T("rstd", [128, CW]); rt = T("rt", [128, CW])
    pst = b.psum("pst")
    for (c0, n_) in LD_CH:
        sl = slice(c0, c0 + n_)
        rms_rstd(b, lambda kb: h[:, kb, sl], 8, n_, onesD, sq, pst, rstd, rt, ["h%d" % k for k in range(8)], "D")
        b.copy("pool", rstdF[:, sl], rstd[:, :n_], ["Drstd"], ["rstdF"])
    sA = T("sA", [128, WH]); sB = T("sB", [128, WH])
    nmb = [T("nmb%d" % i, [128, WH]) for i in range(2)]
    icnt = T("icnt", [128, TPC]); tW = T("tW", [128, TPC])
    pgb = T("pgb", [128, 8, TPC], BF16)

    def chain(eng, src, srck, levels):
        cur, curk, ln = src, srck, WH
        bufs = [(sA, "sA"), (sB, "sB")]
        for lv in range(levels):
            sh = 1 << lv
            dst, dstk = bufs[lv % 2]
            b.tt(eng, dst[:, 0:ln - sh], cur[:, 0:ln - sh], cur[:, sh:ln], ALU.add, [curk], [dstk])
            cur, curk, ln = dst[:], dstk, ln - sh
        return cur, curk

    for blk in range(8):
        gi = blk // 2
        w = 2 << gi
        off = HALO - w // 2
        if blk % 2 == 0:
            ct, ck = chain("pool", val[:], "val", gi + 1)
            b.recip(icnt[:], ct[:, off:off + TPC], [ck], ["icnt"])
        nb = nmb[blk % 2]; nk = "nmb%d" % (blk % 2)
        b.tt("dve", nb[:], h[:, blk, :], rstdF[:], ALU.mult, ["h%d" % blk, "rstdF"], [nk])
        b.act(nb[:], nb[:], AF.Identity, [nk, "gm", "pv"], [nk], bias=pv[:, 1, blk:blk + 1], scale=gm[:, blk:blk + 1])
        b.tt("pool", nb[:], nb[:], val[:], ALU.mult, [nk, "val"], [nk])
        st, sk = chain("dve", nb[:], nk, gi + 1)
        b.tt("dve", tW[:], st[:, off:off + TPC], icnt[:], ALU.mult, [sk, "icnt"], ["tW"])
        b.tt("dve", pgb[:, blk, :], tW[:], nb[:, HALO:HALO + TPC], ALU.subtract, ["tW", nk], ["pgb%d" % blk])
    PS = [b.psum("ps%d" % i) for i in range(4)]
    yt = [T("yt%d" % i, [128, CW]) for i in range(2)]
    pn = 0
    for c in range(NCHUNK):
        sl = slice(c * CW, (c + 1) * CW)
        slh = slice(HALO + c * CW, HALO + (c + 1) * CW)
        for gi in range(4):
            for obl in range(2):
                ob = 2 * gi + obl
                ps = PS[pn % 4]; pk = "ps%d" % (pn % 4); y_ = yt[pn % 2]; ykk = "yt%d" % (pn % 2); pn += 1
                for kbl in range(2):
                    b.mm(ps[:], pwb[:, gi, kbl, obl * 128:(obl + 1) * 128], pgb[:, 2 * gi + kbl, sl], kbl == 0, kbl == 1,
                         ["pwb", "pgb%d" % (2 * gi), "pgb%d" % (2 * gi + 1)], [pk])
                b.act(y_[:], ps[:], AF.Identity, [pk, "Av", "Bv"], [ykk], bias=Bv[:, ob:ob + 1], scale=Av[:, ob:ob + 1])
                b.tt("dve", h[:, ob, slh], h[:, ob, slh], y_[:], ALU.add, ["h%d" % ob, ykk], ["h%d" % ob, "hf%d_%d" % (ob, c)])
        b.dma(oT[:, sl].rearrange("(kb p) n -> p kb n", p=128), h[:, :, slh], r=["hf%d_%d" % (k, c) for k in range(8)])
    return b


def host_LD(inp, mods, hT):
    b = build_LD()
    hall = np.concatenate(hT, axis=1)
    hpad = np.zeros((D, L + 2 * HALO), np.float32)
    hpad[:, HALO:HALO + L] = hall
    vfull = np.zeros((L + 2 * HALO,), np.float32)
    vfull[HALO:HALO + L] = 1.0
    pv = np.stack([vec_pb(inp["norm_mix_g"][1], 8), mod_vec(mods, 1, 0, 0), mod_vec(mods, 1, 1, 0), mod_vec(mods, 1, 2, 0)], axis=1)
    pp = np.stack([vec_pb(inp["pool_b"][0].reshape(-1), 8), vec_pb(inp["pool_scale"][0], 8)], axis=1)
    pw = np.ascontiguousarray(inp["pool_w"][0])
    maps = []
    for k in range(NC):
        maps.append({"hpT": np.ascontiguousarray(hpad[:, k * TPC:k * TPC + WH]),
                     "valid": np.ascontiguousarray(np.broadcast_to(vfull[k * TPC:k * TPC + WH], (128, WH))),
                     "pv": np.ascontiguousarray(pv), "pp": np.ascontiguousarray(pp), "pw": pw})
    res = run(b, maps)
    return [r["oT"] for r in res]


def kernel(**inp):
    inp = {k: np.asarray(v) for k, v in inp.items()}
    mods = host_L0(inp)
    hT, u, vb, uc = host_LA(inp, mods)
    yA, yB = host_LB(inp, u, uc)
    h1T = host_LC(inp, mods, hT, u, vb, yA, yB)
    h2T = host_LM(inp, mods, h1T, 0)
    h3T = host_LD(inp, mods, h2T)
    oT = host_LM(inp, mods, h3T, 1)
    out = np.concatenate([o.T for o in oT], axis=0)
    return np.ascontiguousarray(out[None]).astype(np.float32, copy=False)
```
